# Optimizing a Trainium2 kernel written in Bass

```python
import math
import jax
import jax.numpy as jnp
from jax import lax
import numpy as np

D_MODEL = 1024
BATCH = 16
SEQ = 256
DEPTH = 4
DEC_BATCH = 4
DEC_SEQ = 2048
PAST_LEN = 512

GRID_W = 64
SSD_INNER = 2 * D_MODEL
SSD_HEAD_DIM = 64
SSD_HEADS = SSD_INNER // SSD_HEAD_DIM
SSD_GROUPS = 4
SSD_STATE = 128
SSD_CONV = 5
SSD_CHUNK = 128
SSD_BC = SSD_GROUPS * SSD_STATE
SSD_CONV_CH = SSD_INNER + 2 * SSD_BC
POOL_WIDTH = D_MODEL
POOL_WINDOWS = (2, 4, 8, 16)
POOL_GROUP = POOL_WIDTH // len(POOL_WINDOWS)
GMLP_WIDTH = D_MODEL
GMLP_GROUPS = 4
GMLP_GROUP = GMLP_WIDTH // GMLP_GROUPS
GMLP_CHUNK = 128
N_BRANCH = 3
D_FF = -(-8 * D_MODEL // (3 * 256)) * 256
IN_SIZES = (SSD_INNER, SSD_CONV_CH, 2 * SSD_HEADS, POOL_WIDTH, GMLP_WIDTH, GMLP_WIDTH, N_BRANCH * D_MODEL)
IN_COLS = sum(IN_SIZES)
RMS_EPS = 1e-6
POS_BASE = 10000.0

kernel_name = 'bidir_ssd_pool_gmlp_flow_step'


def rms_norm(x, g):
    xf = x.astype(jnp.float32)
    y = xf * lax.rsqrt(jnp.mean(xf * xf, axis=-1, keepdims=True) + RMS_EPS)
    return (y * g.astype(jnp.float32)).astype(x.dtype)


def grouped_rms_norm(x, g, n_groups):
    shp = x.shape
    xf = x.astype(jnp.float32).reshape(shp[:-1] + (n_groups, shp[-1] // n_groups))
    y = xf * lax.rsqrt(jnp.mean(xf * xf, axis=-1, keepdims=True) + RMS_EPS)
    return (y.reshape(shp) * g.astype(jnp.float32)).astype(x.dtype)


def grid_pos_embed(n_tokens, dim):
    rows = n_tokens // GRID_W
    quarter = dim // 4
    omega = jnp.exp(-math.log(POS_BASE) * jnp.arange(quarter, dtype=jnp.float32) / quarter)
    row = jnp.repeat(jnp.arange(rows, dtype=jnp.float32), GRID_W)
    col = jnp.tile(jnp.arange(GRID_W, dtype=jnp.float32), rows)

    def axis_emb(pos):
        ang = pos[:, None] * omega[None, :]
        return jnp.concatenate([jnp.sin(ang), jnp.cos(ang)], axis=-1)

    return jnp.concatenate([axis_emb(row), axis_emb(col)], axis=-1)


def centred_depthwise_conv(x, w, b):
    k, ch = w.shape
    y = lax.conv_general_dilated(x, w.astype(x.dtype)[:, None, :], window_strides=(1,),
                                 padding=[(k // 2, k // 2)], dimension_numbers=('NWC', 'WIO', 'NWC'),
                                 feature_group_count=ch)
    return y + b.astype(x.dtype)


def ssd_chunked_scan(x, dt, a, b_in, c_in, s0):
    bsz, seq, n_heads, hd = x.shape
    n_groups, d_state = b_in.shape[2], b_in.shape[3]
    hpg = n_heads // n_groups
    q = SSD_CHUNK
    nc = seq // q
    f32 = jnp.float32
    xdt = (x.astype(f32) * dt[..., None]).reshape(bsz, nc, q, n_groups, hpg, hd)
    log_a = (dt * a).reshape(bsz, nc, q, n_groups, hpg)
    acum = jnp.cumsum(log_a, axis=2)
    br = b_in.astype(f32).reshape(bsz, nc, q, n_groups, d_state)
    cr = c_in.astype(f32).reshape(bsz, nc, q, n_groups, d_state)
    acum_t = jnp.moveaxis(acum, 2, -1)
    seg = acum_t[..., :, None] - acum_t[..., None, :]
    pos = jnp.arange(q)
    lower = pos[:, None] >= pos[None, :]
    decay = jnp.exp(jnp.where(lower, seg, -jnp.inf))
    cb = jnp.einsum('bcign,bcjgn->bcgij', cr, br)
    y_diag = jnp.einsum('bcgij,bcgkij,bcjgkp->bcigkp', cb, decay, xdt)
    decay_end = jnp.exp(acum[:, :, -1:] - acum)
    chunk_states = jnp.einsum('bcjgn,bcjgk,bcjgkp->bcgkpn', br, decay_end, xdt)
    chunk_decay = jnp.exp(acum[:, :, -1])

    def step(s, inp):
        st, dec = inp
        return s * dec[..., None, None] + st, s

    s_init = s0.astype(f32).reshape(bsz, n_groups, hpg, hd, d_state)
    s_last, s_enter = lax.scan(step, s_init, (jnp.moveaxis(chunk_states, 1, 0), jnp.moveaxis(chunk_decay, 1, 0)))
    s_enter = jnp.moveaxis(s_enter, 0, 1)
    y_off = jnp.einsum('bcign,bcigk,bcgkpn->bcigkp', cr, jnp.exp(acum), s_enter)
    y = (y_diag + y_off).reshape(bsz, seq, n_heads, hd)
    return y, s_last.reshape(bsz, n_heads, hd, d_state).astype(s0.dtype)


def ssd_mixer(z, xbc, dt_raw, s0f, s0b, conv_w, conv_b, dt_bias, a_log, d_skip, g_norm):
    bsz, seq, _ = xbc.shape
    xbc = jax.nn.silu(centred_depthwise_conv(xbc, conv_w, conv_b))
    xs = xbc[..., :SSD_INNER].reshape(bsz, seq, SSD_HEADS, SSD_HEAD_DIM)
    bm = xbc[..., SSD_INNER:SSD_INNER + SSD_BC].reshape(bsz, seq, SSD_GROUPS, SSD_STATE)
    cm = xbc[..., SSD_INNER + SSD_BC:].reshape(bsz, seq, SSD_GROUPS, SSD_STATE)
    dt = jax.nn.softplus(dt_raw.astype(jnp.float32).reshape(bsz, seq, 2, SSD_HEADS) + dt_bias.astype(jnp.float32))
    a = -jnp.exp(a_log.astype(jnp.float32))
    y_f, s_f = ssd_chunked_scan(xs, dt[:, :, 0], a[0], bm, cm, s0f)
    flip = lambda t: jnp.flip(t, axis=1)
    y_b, s_b = ssd_chunked_scan(flip(xs), flip(dt[:, :, 1]), a[1], flip(bm), flip(cm), s0b)
    y = y_f + flip(y_b) + xs.astype(jnp.float32) * d_skip.astype(jnp.float32)[:, None]
    y = y.astype(z.dtype).reshape(bsz, seq, SSD_INNER) * jax.nn.silu(z)
    return grouped_rms_norm(y, g_norm, SSD_GROUPS), s_f, s_b


def pool_mixer(p, w_pool, pool_scale):
    bsz, seq, width = p.shape
    pf = p.astype(jnp.float32)
    cs = jnp.concatenate([jnp.zeros((bsz, 1, width), jnp.float32), jnp.cumsum(pf, axis=1)], axis=1)
    t = jnp.arange(seq)
    outs = []
    for gi, w in enumerate(POOL_WINDOWS):
        lo = jnp.clip(t - w // 2, 0, seq)
        hi = jnp.clip(t + w // 2, 0, seq)
        csg = cs[:, :, gi * POOL_GROUP:(gi + 1) * POOL_GROUP]
        mean = (csg[:, hi] - csg[:, lo]) / (hi - lo).astype(jnp.float32)[None, :, None]
        outs.append(mean - pf[:, :, gi * POOL_GROUP:(gi + 1) * POOL_GROUP])
    pooled = jnp.stack(outs, axis=2)
    mixed = jnp.einsum('blgc,gcd->blgd', pooled, w_pool.astype(jnp.float32)).reshape(bsz, seq, width)
    return (mixed * pool_scale.astype(jnp.float32)).astype(p.dtype)


def chunk_gmlp(u, v, g_sgu, w_spatial, b_spatial):
    bsz, seq, _ = v.shape
    v = rms_norm(v, g_sgu)
    vr = v.reshape(bsz, seq // GMLP_CHUNK, GMLP_CHUNK, GMLP_GROUPS, GMLP_GROUP)
    sv = jnp.einsum('gij,bcjgd->bcigd', w_spatial, vr) + b_spatial.T[None, None, :, :, None]
    return u * sv.reshape(bsz, seq, GMLP_WIDTH)


def trunk_layer(x, mod, s0f, s0b, lp):
    shift1, scale1, gate1, shift2, scale2, gate2 = jnp.split(mod, 6, axis=-1)
    h = rms_norm(x, lp['g_norm1']) * (1 + scale1) + shift1
    offs = np.cumsum(IN_SIZES)[:-1].tolist()
    z, xbc, dt_raw, p, u, v, gate_logits = jnp.split(h @ lp['w_in'], offs, axis=-1)
    y_ssd, s_f, s_b = ssd_mixer(z, xbc, dt_raw, s0f, s0b, lp['conv_w'], lp['conv_b'], lp['dt_bias'],
                                lp['a_log'], lp['d_skip'], lp['g_ssd'])
    y_pool = pool_mixer(p, lp['w_pool'], lp['pool_scale'])
    y_gmlp = chunk_gmlp(u, v, lp['g_sgu'], lp['w_spatial'], lp['b_spatial'])
    g_a, g_b, g_c = jnp.split(jax.nn.sigmoid(gate_logits), N_BRANCH, axis=-1)
    merged = (g_a * (y_ssd @ lp['w_br_ssd']) + g_b * (y_pool @ lp['w_br_pool'])
              + g_c * (y_gmlp @ lp['w_br_gmlp']))
    x = x + gate1 * (merged @ lp['w_out'])
    h2 = rms_norm(x, lp['g_norm2']) * (1 + scale2) + shift2
    f_gate, f_up = jnp.split(h2 @ lp['w_ffn_in'], 2, axis=-1)
    x = x + gate2 * ((jax.nn.silu(f_gate) * f_up) @ lp['w_ffn_out'])
    return x, s_f, s_b


def setup_inputs(seed: int = 0) -> dict:
    key = jax.random.key(seed)
    ks = jax.random.split(key, 32)
    f32 = jnp.float32
    nrm = lambda k, shape, s: jax.random.normal(k, shape, f32) * s
    state_shape = (DEC_BATCH, DEPTH, SSD_HEADS, SSD_HEAD_DIM, SSD_STATE)
    dt0 = jnp.exp(jax.random.uniform(ks[12], (DEPTH, 2, SSD_HEADS), f32, math.log(1e-3), math.log(1e-1)))
    return {
        'x_prompt': nrm(ks[0], (BATCH, SEQ, D_MODEL), 1.0),
        'x_sample': nrm(ks[1], (DEC_BATCH, DEC_SEQ, D_MODEL), 1.0),
        'state_ssd_fwd': nrm(ks[2], state_shape, 0.5),
        'state_ssd_bwd': nrm(ks[3], state_shape, 0.5),
        'c': nrm(ks[4], (DEC_BATCH, D_MODEL), 1.0),
        'c_ctx': nrm(ks[5], (D_MODEL,), 1.0),
        'w_ada': nrm(ks[6], (DEPTH, D_MODEL, 6 * D_MODEL), 0.5 * D_MODEL ** -0.5),
        'b_ada': nrm(ks[7], (DEPTH, 6 * D_MODEL), 0.02),
        'g_norm1': 1.0 + nrm(ks[8], (DEPTH, D_MODEL), 0.02),
        'w_in': nrm(ks[9], (DEPTH, D_MODEL, IN_COLS), D_MODEL ** -0.5),
        'conv_w': nrm(ks[10], (DEPTH, SSD_CONV, SSD_CONV_CH), SSD_CONV ** -0.5),
        'conv_b': nrm(ks[11], (DEPTH, SSD_CONV_CH), 0.02),
        'dt_bias': dt0 + jnp.log(-jnp.expm1(-dt0)),
        'a_log': jnp.log(jax.random.uniform(ks[13], (DEPTH, 2, SSD_HEADS), f32, 1.0, 16.0)),
        'd_skip': 1.0 + nrm(ks[14], (DEPTH, SSD_HEADS), 0.02),
        'g_ssd': 1.0 + nrm(ks[15], (DEPTH, SSD_INNER), 0.02),
        'w_br_ssd': nrm(ks[16], (DEPTH, SSD_INNER, D_MODEL), SSD_INNER ** -0.5),
        'w_pool': nrm(ks[17], (DEPTH, len(POOL_WINDOWS), POOL_GROUP, POOL_GROUP), POOL_GROUP ** -0.5),
        'pool_scale': 1.0 + nrm(ks[18], (DEPTH, POOL_WIDTH), 0.02),
        'w_br_pool': nrm(ks[19], (DEPTH, POOL_WIDTH, D_MODEL), POOL_WIDTH ** -0.5),
        'g_sgu': 1.0 + nrm(ks[20], (DEPTH, GMLP_WIDTH), 0.02),
        'w_spatial': nrm(ks[21], (DEPTH, GMLP_GROUPS, GMLP_CHUNK, GMLP_CHUNK), GMLP_CHUNK ** -0.5),
        'b_spatial': 1.0 + nrm(ks[22], (DEPTH, GMLP_GROUPS, GMLP_CHUNK), 0.02),
        'w_br_gmlp': nrm(ks[23], (DEPTH, GMLP_WIDTH, D_MODEL), GMLP_WIDTH ** -0.5),
        'w_out': nrm(ks[24], (DEPTH, D_MODEL, D_MODEL), D_MODEL ** -0.5),
        'g_norm2': 1.0 + nrm(ks[25], (DEPTH, D_MODEL), 0.02),
        'w_ffn_in': nrm(ks[26], (DEPTH, D_MODEL, 2 * D_FF), D_MODEL ** -0.5),
        'w_ffn_out': nrm(ks[27], (DEPTH, D_FF, D_MODEL), D_FF ** -0.5),
        'g_final': 1.0 + nrm(ks[28], (D_MODEL,), 0.02),
    }


def reference(x_prompt, x_sample, state_ssd_fwd, state_ssd_bwd, c, c_ctx, w_ada, b_ada, g_norm1, w_in,
              conv_w, conv_b, dt_bias, a_log, d_skip, g_ssd, w_br_ssd, w_pool, pool_scale, w_br_pool,
              g_sgu, w_spatial, b_spatial, w_br_gmlp, w_out, g_norm2, w_ffn_in, w_ffn_out, g_final):
    pos = grid_pos_embed(x_sample.shape[1], x_sample.shape[2])
    xp = x_prompt
    xs = x_sample + pos.astype(x_sample.dtype)[None]
    zero_state = jnp.zeros((x_prompt.shape[0], SSD_HEADS, SSD_HEAD_DIM, SSD_STATE), x_prompt.dtype)
    fwd_states, bwd_states = [], []
    for l in range(DEPTH):
        lp = {'g_norm1': g_norm1[l], 'w_in': w_in[l], 'conv_w': conv_w[l], 'conv_b': conv_b[l],
              'dt_bias': dt_bias[l], 'a_log': a_log[l], 'd_skip': d_skip[l], 'g_ssd': g_ssd[l],
              'w_br_ssd': w_br_ssd[l], 'w_pool': w_pool[l], 'pool_scale': pool_scale[l],
              'w_br_pool': w_br_pool[l], 'g_sgu': g_sgu[l], 'w_spatial': w_spatial[l],
              'b_spatial': b_spatial[l], 'w_br_gmlp': w_br_gmlp[l], 'w_out': w_out[l],
              'g_norm2': g_norm2[l], 'w_ffn_in': w_ffn_in[l], 'w_ffn_out': w_ffn_out[l]}
        mod_ctx = (jax.nn.silu(c_ctx) @ w_ada[l] + b_ada[l]).reshape(1, 1, -1)
        mod_lat = (jax.nn.silu(c) @ w_ada[l] + b_ada[l])[:, None, :]
        xp, s_f, s_b = trunk_layer(xp, mod_ctx, zero_state, zero_state, lp)
        fwd_states.append(s_f)
        bwd_states.append(s_b)
        xs, _, _ = trunk_layer(xs, mod_lat, state_ssd_fwd[:, l], state_ssd_bwd[:, l], lp)
    y_prompt = rms_norm(xp, g_final)
    y_sample = rms_norm(xs, g_final)
    new_state_ssd_fwd = jnp.stack(fwd_states, axis=1)
    new_state_ssd_bwd = jnp.stack(bwd_states, axis=1)
    return (y_prompt, y_sample, new_state_ssd_fwd, new_state_ssd_bwd)
```

```python
import math
from contextlib import ExitStack
import numpy as np
import concourse.bass as bass
import concourse.mybir as mybir
from concourse.bass_utils import run_bass_kernel_spmd

F32 = mybir.dt.float32
BF16 = mybir.dt.bfloat16
I32 = mybir.dt.int32
AF = mybir.ActivationFunctionType
ALU = mybir.AluOpType

D = 1024
DEPTH = 4
T = 2048
NTILE = 4
TT = 512
SEG = 256
HALO = 8
SEGP = SEG + 2 * HALO
XRW = T + 2 * HALO
NH = 32
DFF = 2816
INC = 11328
EPS = 1e-6
C_Z, C_X, C_B, C_C, C_DT, C_P, C_U, C_V, C_G = 0, 2048, 4096, 4608, 5120, 5184, 6208, 7232, 8256
NPP = 376 + 1024 + 512
NPPS = 376
PP_G1, PP_G2, PP_BADA, PP_CW, PP_CB, PP_DSK, PP_GSSD, PP_PSC, PP_DTB, PP_ALOG, PP_GSGU, PP_BSP = (
    0, 8, 16, 64, 184, 208, 224, 240, 248, 312, 376, 1400)
ARENA_KB = 206
CELL = 256
NSLOT = 4
SLOT_B = 4096
AHEAD = 2
NDSEM = 12
NFE = 24 * 512 + 8 * 2 * SEGP


def _esize(dt):
    return 2 if dt == BF16 else 4


class Op:
    __slots__ = ("stream", "dom", "idx", "fn", "waits", "signaled", "is_dma", "count")


class Sched:
    def __init__(self):
        self.streams = {k: [] for k in ("pe", "act", "dve", "pool", "sp")}
        self.domcnt = {}
        self.cells = {}
        self.waited = {k: {} for k in self.streams}
        self.dma_rr = {"sp": 0, "pool": 0}
        self.dma_last = {}
        self.nops = 0
        self.dry = False

    @staticmethod
    def region(ap):
        t = ap.tensor
        name = t.name
        es = _esize(ap.dtype)
        dims = [list(d) for d in ap.ap]
        cls = type(t).__name__
        if cls.startswith("DRam"):
            ext = sum((n - 1) * abs(s) for s, n in dims) + 1
            b0 = ap.offset * es
            return ("d:" + name, b0 // 65536, (b0 + ext * es - 1) // 65536 + 1, 0, 2)
        pstride, pn = dims[0]
        if pstride == 0:
            pstride = 1 << 40
        fd = dims[1:]
        ext = sum((n - 1) * abs(s) for s, n in fd) + 1
        p0 = ap.offset // pstride if pstride < (1 << 40) else 0
        c0 = ap.offset - p0 * pstride if pstride < (1 << 40) else ap.offset
        b0 = c0 * es
        b1 = (c0 + ext) * es
        p1 = p0 + pn
        h0 = 0 if p0 < 64 else 1
        h1 = 1 if p1 <= 64 else 2
        if cls.startswith("PS") or "psum" in name:
            return ("ps:" + name, b0 // 2048, (b1 - 1) // 2048 + 1, p0 // 32, (p1 - 1) // 32 + 1)
        return ("sb:" + name, b0 // CELL, (b1 - 1) // CELL + 1, h0, h1)

    def op(self, stream, fn, reads=(), writes=(), dma=False):
        self.nops += 1
        if self.dry:
            return None
        o = Op()
        o.stream = stream
        o.is_dma = dma
        o.fn = fn
        o.signaled = dma
        o.count = None
        if dma:
            k = self.dma_rr[stream] % NDSEM
            self.dma_rr[stream] += 1
            o.dom = (stream, k)
        else:
            o.dom = stream
        o.idx = self.domcnt.get(o.dom, 0)
        self.domcnt[o.dom] = o.idx + 1
        need = {}

        def want(w):
            if w is None:
                return
            if w.dom == "pe" and stream == "pe" and not dma:
                return
            cur = need.get(w.dom)
            if cur is None or cur.idx < w.idx:
                need[w.dom] = w
        if dma:
            want(self.dma_last.get(o.dom))
            self.dma_last[o.dom] = o
        cells = self.cells
        for ap in reads:
            if ap is None:
                continue
            sp, c0, c1, h0, h1 = self.region(ap)
            if sp.startswith("d:in_"):
                continue
            for c in range(c0, c1):
                for h in range(h0, h1):
                    rec = cells.get((sp, c, h))
                    if rec is None:
                        rec = [None, {}]
                        cells[(sp, c, h)] = rec
                    want(rec[0])
        for ap in writes:
            sp, c0, c1, h0, h1 = self.region(ap)
            for c in range(c0, c1):
                for h in range(h0, h1):
                    rec = cells.get((sp, c, h))
                    if rec is None:
                        rec = [None, {}]
                        cells[(sp, c, h)] = rec
                    want(rec[0])
                    for r in rec[1].values():
                        want(r)
        wl = []
        wd = self.waited[stream]
        for dom, w in need.items():
            if wd.get(dom, -1) >= w.idx:
                continue
            wd[dom] = w.idx
            w.signaled = True
            wl.append(w)
        o.waits = wl
        for ap in reads:
            if ap is None:
                continue
            sp, c0, c1, h0, h1 = self.region(ap)
            if sp.startswith("d:in_"):
                continue
            for c in range(c0, c1):
                for h in range(h0, h1):
                    rec = cells[(sp, c, h)]
                    cur = rec[1].get(o.dom)
                    if cur is None or cur.idx < o.idx:
                        rec[1][o.dom] = o
        for ap in writes:
            sp, c0, c1, h0, h1 = self.region(ap)
            for c in range(c0, c1):
                for h in range(h0, h1):
                    rec = cells[(sp, c, h)]
                    rec[0] = o
                    rec[1] = {}
        self.streams[stream].append(o)
        return o

    def emit(self, nc, final_waits):
        doms = set(self.domcnt.keys())
        for dom in doms:
            cnt = 0
            for st in self.streams.values():
                pass
        per_dom = {}
        for sname, ops in self.streams.items():
            for o in ops:
                per_dom.setdefault(o.dom, []).append(o)
        for dom, ops in per_dom.items():
            ops.sort(key=lambda x: x.idx)
            c = 0
            for o in ops:
                if o.signaled:
                    c += 1
                    o.count = c
        with ExitStack() as es:
            sems = {}
            for dom in per_dom:
                nm = dom if isinstance(dom, str) else "%s_d%d" % dom
                sems[dom] = es.enter_context(nc.semaphore("s_" + nm))
            block = es.enter_context(nc.Block())

            def run(sname, eng):
                for o in self.streams[sname]:
                    for w in o.waits:
                        inc = 16 if w.is_dma else 1
                        eng.wait_ge(sems[w.dom], w.count * inc)
                    ins = o.fn(eng)
                    if o.signaled:
                        ins.then_inc(sems[o.dom], 16 if o.is_dma else 1)
                if sname == "sp":
                    for o in final_waits:
                        eng.wait_ge(sems[o.dom], o.count * 16)

            @block.tensor
            def _(e):
                run("pe", e)

            @block.scalar
            def _(e):
                run("act", e)

            @block.vector
            def _(e):
                run("dve", e)

            @block.gpsimd
            def _(e):
                run("pool", e)

            @block.sync
            def _(e):
                run("sp", e)


class Builder:
    def __init__(self, nc, nl):
        self.nc = nc
        self.nl = nl
        self.S = Sched()
        self.top = 0
        self.arena = None
        self.psum = None
        self.wreq = []
        self.wi = 0
        self.wissued = 0
        self.out_dmas = []

    def alloc(self, nbytes):
        off = (self.top + CELL - 1) // CELL * CELL
        self.top = off + nbytes
        assert self.top <= ARENA_KB * 1024, ("SBUF arena overflow", self.top)
        return off

    def view(self, off, dt, shape):
        n = int(np.prod(shape))
        es = _esize(dt)
        a = self.arena[:, off // 4:(off + n * es + 3) // 4]
        if dt != F32:
            a = a.bitcast(dt)
        if len(shape) == 2:
            a = a.rearrange("p (a b) -> p a b", b=shape[1])
        elif len(shape) == 3:
            a = a.rearrange("p (a b c) -> p a b c", b=shape[1], c=shape[2])
        return a

    def new(self, dt, shape):
        return self.view(self.alloc(int(np.prod(shape)) * _esize(dt)), dt, shape)

    def bank(self, b, dt=F32):
        a = self.psum[:, b * 512:(b + 1) * 512]
        if dt != F32:
            a = a.bitcast(dt)
        return a

    def mm(self, out, lhsT, rhs, start=True, stop=True):
        rd = [lhsT, rhs] + ([] if start else [out])
        return self.S.op("pe", lambda e: e.matmul(out, lhsT=lhsT, rhs=rhs, start=start, stop=stop),
                         reads=rd, writes=[out])

    def tr(self, out, in_, ident):
        return self.S.op("pe", lambda e: e.transpose(out, in_, ident), reads=[in_, ident], writes=[out])

    def act(self, out, in_, func, bias=None, scale=1.0, accum_out=None):
        rd = [in_]
        kw = {}
        if bias is not None:
            kw["bias"] = bias
            if not isinstance(bias, (int, float)):
                rd.append(bias)
        if not isinstance(scale, (int, float)):
            rd.append(scale)
        wr = [out]
        if accum_out is not None:
            kw["accum_out"] = accum_out
            wr.append(accum_out)
        return self.S.op("act", lambda e: e.activation(out=out, in_=in_, func=func, scale=scale, **kw),
                         reads=rd, writes=wr)

    def tt(self, eng, out, in0, in1, op):
        return self.S.op(eng, lambda e: e.tensor_tensor(out=out, in0=in0, in1=in1, op=op),
                         reads=[in0, in1], writes=[out])

    def ts(self, eng, out, in0, s1, op0, s2=None, op1=None):
        rd = [in0] + [s for s in (s1, s2) if s is not None and not isinstance(s, (int, float))]
        if op1 is None:
            return self.S.op(eng, lambda e: e.tensor_scalar(out=out, in0=in0, scalar1=s1, scalar2=None, op0=op0),
                             reads=rd, writes=[out])
        return self.S.op(eng, lambda e: e.tensor_scalar(out=out, in0=in0, scalar1=s1, scalar2=s2, op0=op0, op1=op1),
                         reads=rd, writes=[out])

    def stt(self, out, in0, scalar, in1, op0, op1):
        rd = [in0, in1] + ([] if isinstance(scalar, (int, float)) else [scalar])
        return self.S.op("dve", lambda e: e.scalar_tensor_tensor(out=out, in0=in0, scalar=scalar, in1=in1,
                                                                   op0=op0, op1=op1), reads=rd, writes=[out])

    def copy(self, eng, out, in_):
        if eng == "act":
            return self.S.op("act", lambda e: e.copy(out=out, in_=in_), reads=[in_], writes=[out])
        return self.S.op(eng, lambda e: e.tensor_copy(out=out, in_=in_), reads=[in_], writes=[out])

    def memset(self, eng, ap, val):
        return self.S.op(eng, lambda e: e.memset(ap, val), reads=[], writes=[ap])

    def dma(self, stream, out, in_, slow=False):
        if slow:
            return self.S.op(stream, lambda e: e.dma_start(out=out, in_=in_, allow_slow_non_contiguous=True),
                             reads=[in_], writes=[out], dma=True)
        return self.S.op(stream, lambda e: e.dma_start(out=out, in_=in_), reads=[in_], writes=[out], dma=True)

    def wnext(self, src, K, ncols):
        assert K * ncols * 2 <= SLOT_B
        if self.S.dry:
            self.wreq.append((src, K, ncols))
            return self.view(self.wslots[0], BF16, [K, ncols])
        i = self.wi
        self.wi += 1
        while self.wissued < min(len(self.wreq), i + AHEAD + 1):
            s2, K2, n2 = self.wreq[self.wissued]
            dst = self.view(self.wslots[self.wissued % NSLOT], BF16, [K2, n2])
            self.dma("pool", dst, s2.rearrange("(k p) c -> p k c", p=128))
            self.wissued += 1
        return self.view(self.wslots[i % NSLOT], BF16, [K, ncols])


def build_program(nl=DEPTH):
    nc = bass.Bass("TRN2", target_bir_lowering=False)
    dt_in = {}

    def din(name, shape):
        dt_in[name] = nc.dram_tensor("in_" + name, list(shape), F32, kind="ExternalInput").ap()
        return dt_in[name]

    NLW = max(nl, 1)
    xin = din("xin", [T, D])
    s0f = din("s0f", [NLW, NH * 64, 128])
    s0b = din("s0b", [NLW, NH * 64, 128])
    cvec = din("cvec", [128, 8])
    flags = din("flags", [128, 16])
    consts = din("consts", [4, 128, 128])
    pp = din("pp", [NLW, 128, NPP])
    gfin = din("gfin", [128, D])
    w_ada = din("w_ada", [NLW, D, 6 * D])
    w_in = din("w_in", [NLW, D, INC])
    w_br_ssd = din("w_br_ssd", [NLW, 2048, D])
    w_pool = din("w_pool", [NLW, 4, 256, 256])
    w_br_pool = din("w_br_pool", [NLW, D, D])
    w_spatial = din("w_spatial", [NLW, 4, 128, 128])
    w_br_gmlp = din("w_br_gmlp", [NLW, D, D])
    w_out = din("w_out", [NLW, D, D])
    w_ffn_in = din("w_ffn_in", [NLW, D, 2 * DFF])
    w_ffn_out = din("w_ffn_out", [NLW, DFF, D])
    yout = nc.dram_tensor("yout", [T, D], F32, kind="ExternalOutput").ap()
    sfo = nc.dram_tensor("sfo", [4, NLW, NH * 64, 128], F32, kind="ExternalOutput").ap()
    sbo = nc.dram_tensor("sbo", [4, NLW, NH * 64, 128], F32, kind="ExternalOutput").ap()
    ybs = nc.dram_tensor("ybs", [16, 128, T], F32, kind="Internal").ap()
    fes = nc.dram_tensor("fes", [NTILE, 128, NFE], BF16, kind="Internal").ap()

    with ExitStack() as es:
        arena_t = es.enter_context(nc.sbuf_tensor("arena", [128, ARENA_KB * 256], F32))
        psum_t = es.enter_context(nc.psum_tensor("psum", [128, 8 * 512], F32))
        B = Builder(nc, nl)
        B.arena = arena_t[:]
        B.psum = psum_t[:]
        for dry in (True, False):
            B.S.dry = dry
            B.top = 0
            B.wi = 0
            B.wissued = 0
            _program(B, dt_in, yout, sfo, sbo, ybs, fes)
        B.S.emit(nc, B.out_dmas)
    return nc


def _program(B, I, yout, sfo, sbo, ybs, fes):
    nl = B.nl
    S = B.S
    xin, s0f, s0b, cvec, flags, consts, pp, gfin = (I[k] for k in
                                                      ("xin", "s0f", "s0b", "cvec", "flags", "consts", "pp", "gfin"))
    w_in = I["w_in"]
    B.wslots = [B.alloc(SLOT_B) for _ in range(NSLOT)]
    XR = B.new(F32, [8, XRW])
    CON = B.new(F32, [4, 128])
    IDF, TRI_F, MSK_F, MSK_B = CON[:, 0, :], CON[:, 1, :], CON[:, 1, :], CON[:, 3, :]
    CONB = B.new(BF16, [4, 128])
    IDB, TRIB_I, TRIB_E = CONB[:, 0, :], CONB[:, 1, :], CONB[:, 2, :]
    ONEB = B.new(BF16, [128])
    ONEF = B.new(F32, [128])
    ZERO = B.new(F32, [128])
    FLG = B.new(F32, [16])
    EPSC = B.new(F32, [2])
    MODV = B.new(F32, [DEPTH, 48])
    GS = B.new(F32, [DEPTH, 2, 8])
    PP = B.new(F32, [NPPS])
    NEGA = B.new(F32, [64])
    WST = B.new(BF16, [4, 128])
    SF = B.new(F32, [2048])
    SBF = B.new(BF16, [2048])
    HT = B.new(BF16, [8, 2, SEGP])
    HSAVE = B.new(BF16, [8, HALO])
    CV = B.new(F32, [8])
    CVB = B.new(BF16, [8])
    PPB = B.new(F32, [48])
    base_top = B.top

    def mcol(i):
        return FLG[:, 1 + i:2 + i]

    bankrr = [0]

    def mmbank():
        b = bankrr[0] % 4
        bankrr[0] += 1
        return b

    B.dma("sp", CON, consts.rearrange("c p f -> p c f"))
    B.dma("sp", FLG, flags)
    B.copy("dve", CONB, CON)
    B.memset("dve", ONEB, 1.0)
    B.memset("dve", ONEF, 1.0)
    B.memset("dve", ZERO, 0.0)
    B.memset("dve", EPSC[:, 0:1], EPS)
    B.memset("dve", EPSC[:, 1:2], 1.0)
    B.memset("dve", XR[:, :, 0:HALO], 0.0)
    B.memset("dve", XR[:, :, XRW - HALO:XRW], 0.0)
    EPS_AP = EPSC[:, 0:1]
    ONE_AP = EPSC[:, 1:2]

    tmp0 = B.top
    IOI = B.new(I32, [96])
    OMI = B.new(I32, [2])
    OM = B.new(F32, [2])
    POSV = B.new(F32, [96])
    ANG = B.new(F32, [4, 96])
    T1 = B.new(F32, [4, 96])
    KI = B.new(I32, [4 * 96])
    T2 = B.new(F32, [4, 96])
    S.op("pool", lambda e: e.iota(IOI[:, 0:32], pattern=[[1, 32]], base=0, channel_multiplier=0), writes=[IOI[:, 0:32]])
    S.op("pool", lambda e: e.iota(IOI[:, 32:96], pattern=[[1, 64]], base=0, channel_multiplier=0), writes=[IOI[:, 32:96]])
    S.op("pool", lambda e: e.iota(OMI, pattern=[[128, 2]], base=0, channel_multiplier=1), writes=[OMI])
    B.copy("dve", POSV, IOI)
    B.copy("dve", OM, OMI)
    B.act(OM, OM, AF.Exp, scale=-math.log(10000.0) / 256.0)
    TWO_PI = 2.0 * math.pi
    for b2 in range(2):
        B.ts("dve", ANG[:, b2, :], POSV, OM[:, b2:b2 + 1], ALU.mult)
        B.ts("dve", ANG[:, 2 + b2, :], POSV, OM[:, b2:b2 + 1], ALU.mult, math.pi / 2, ALU.add)
    A2 = ANG.rearrange("p a b -> p (a b)")
    T12 = T1.rearrange("p a b -> p (a b)")
    T22 = T2.rearrange("p a b -> p (a b)")
    B.ts("dve", T12, A2, 1.0 / TWO_PI, ALU.mult)
    B.copy("dve", KI, T12)
    B.copy("dve", T12, KI)
    B.stt(T22, T12, -TWO_PI, A2, ALU.mult, ALU.add)
    B.ts("dve", T12, T22, math.pi, ALU.is_gt)
    B.stt(T22, T12, -TWO_PI, T22, ALU.mult, ALU.add)
    B.ts("dve", T12, T22, -math.pi, ALU.is_lt)
    B.stt(T22, T12, TWO_PI, T22, ALU.mult, ALU.add)
    B.ts("dve", T22, T22, 3.1415925, ALU.min, -3.1415925, ALU.max)
    PTAB = B.new(F32, [4, 96])
    B.act(PTAB.rearrange("p a b -> p (a b)"), T22, AF.Sin)
    B.ts("dve", PTAB.rearrange("p a b -> p (a b)"), PTAB.rearrange("p a b -> p (a b)"), FLG[:, 0:1], ALU.mult)
    XS = [B.new(F32, [D]) for _ in range(2)]
    for tc in range(16):
        xs = XS[tc % 2]
        B.dma("sp", xs, xin[tc * 128:(tc + 1) * 128, :])
        for half in range(2):
            bk = 4 + (2 * tc + half) % 4
            pb = B.bank(bk)
            for j in range(4):
                blk = half * 4 + j
                B.tr(pb[:, j * 128:(j + 1) * 128], xs[:, blk * 128:(blk + 1) * 128], IDF)
            for j in range(4):
                blk = half * 4 + j
                src = pb[:, j * 128:(j + 1) * 128].rearrange("p (r c) -> p r c", c=64)
                dst = XR[:, blk, HALO + tc * 128:HALO + (tc + 1) * 128].rearrange("p (r c) -> p r c", c=64)
                if blk < 4:
                    pv = PTAB[:, blk, 2 * tc:2 * tc + 2].unsqueeze(2).broadcast_to([128, 2, 64])
                else:
                    pv = PTAB[:, blk - 4, 32:96].unsqueeze(1).broadcast_to([128, 2, 64])
                B.tt("dve", dst, src, pv, ALU.add)
    B.dma("sp", CV, cvec)
    B.act(CVB, CV, AF.Silu)

    def mod_piece(l, cb2):
        if cb2 == 0:
            B.dma("sp", PPB, pp[l, :, PP_BADA:PP_BADA + 48])
        pb = B.bank(6)[:, 510:512]
        wv = B.wnext(I["w_ada"][l, :, cb2 * 256:(cb2 + 1) * 256], 8, 256)
        for j in range(2):
            for k in range(8):
                B.mm(pb[:, j:j + 1], wv[:, k, j * 128:(j + 1) * 128], CVB[:, k:k + 1], start=(k == 0), stop=(k == 7))
        B.tt("dve", MODV[:, l, 2 * cb2:2 * cb2 + 2], pb, PPB[:, 2 * cb2:2 * cb2 + 2], ALU.add)

    for cb2 in range(24):
        mod_piece(0, cb2)
    B.top = base_top

    def modnorm(l, which, tau, mask_halo, fix_left=False):
        m0 = B.top
        SQ = B.new(BF16, [8, SEGP])
        LNV = B.new(F32, [SEGP])
        RSTD = B.new(F32, [SEGP])
        TMP = [B.new(F32, [SEGP]) for _ in range(2)]
        sh0 = 0 if which == 0 else 24
        for s in range(2):
            t0 = tau * TT + s * SEG
            xw = XR[:, :, t0:t0 + SEGP]
            B.act(SQ, xw, AF.Square)
            pb = B.bank(4 + s)
            for k in range(8):
                B.mm(pb[:, 0:SEGP], ONEB, SQ[:, k, :], start=(k == 0), stop=(k == 7))
            B.act(LNV, pb[:, 0:SEGP], AF.Ln, bias=EPS_AP, scale=1.0 / D)
            B.act(RSTD, LNV, AF.Exp, scale=-0.5)
            for k in range(8):
                tmp = TMP[k % 2]
                B.stt(tmp, XR[:, k, t0:t0 + SEGP], GS[:, l, which, k:k + 1], RSTD, ALU.mult, ALU.mult)
                B.act(HT[:, k, s, :], tmp, AF.Identity, bias=MODV[:, l, sh0 + k:sh0 + k + 1])
            if mask_halo:
                sg = 2 * tau + s
                B.ts("dve", HT[:, :, s, 0:HALO], HT[:, :, s, 0:HALO], mcol(sg), ALU.mult)
                B.ts("dve", HT[:, :, s, SEGP - HALO:SEGP], HT[:, :, s, SEGP - HALO:SEGP], mcol(sg + 1), ALU.mult)
        if fix_left:
            if tau >= 1:
                B.ts("dve", HT[:, :, 0, 0:HALO], HSAVE, mcol(2 * tau), ALU.mult)
            B.copy("dve", HSAVE, HT[:, :, 1, SEG:SEG + HALO])
        B.top = m0

    def layer_params(l):
        B.dma("sp", PP, pp[l, :, 0:NPPS])
        for which in range(2):
            sc0 = 8 if which == 0 else 32
            g0 = PP_G1 if which == 0 else PP_G2
            B.stt(GS[:, l, which, :], MODV[:, l, sc0:sc0 + 8], 1.0, PP[:, g0:g0 + 8], ALU.add, ALU.mult)
        B.act(NEGA, PP[:, PP_ALOG:PP_ALOG + 64], AF.Exp)
        B.ts("dve", NEGA, NEGA, -1.0, ALU.mult)
        m0 = B.top
        WSN = B.new(BF16, [4, 128])
        B.dma("pool", WSN, I["w_spatial"][l].rearrange("g i j -> i g j"))
        pb = B.bank(6, BF16)
        for g in range(4):
            B.tr(pb[:, g * 128:(g + 1) * 128], WSN[:, g, :], IDB)
        B.copy("dve", WST.rearrange("p a b -> p (a b)"), pb[:, 0:512])
        B.top = m0

    def load_state(l, src):
        m0 = B.top
        ST = [B.new(F32, [128]) for _ in range(2)]
        for blk in range(16):
            st = ST[blk % 2]
            B.dma("sp", st, src[l, blk * 128:(blk + 1) * 128, :])
            pb = B.bank(6 + blk % 2)
            B.tr(pb[:, 0:128], st, IDF)
            B.copy("act", SF[:, blk * 128:(blk + 1) * 128], pb[:, 0:128])
        B.copy("act", SBF, SF)
        B.top = m0

    v3 = lambda a: a.rearrange("p (c h) -> p c h", h=32)
    h64 = lambda a: a.rearrange("p (h x) -> p h x", x=64)

    def sweep(l, d):
        tiles = range(NTILE) if d == 0 else range(NTILE - 1, -1, -1)
        modq = [0]
        load_state(l, s0f if d == 0 else s0b)
        for tau in tiles:
            if d == 1:
                modnorm(l, 0, tau, True)
            tile_top = B.top
            if d == 0:
                YN_off = B.alloc(16 * TT * 2)
                YN = B.view(YN_off, BF16, [16, TT])
            pos_m = B.top
            XT = B.new(BF16, [16, TT])
            BT = B.new(BF16, [4, TT])
            CT = B.new(BF16, [4, TT])
            ACC = [B.new(F32, [SEG]) for _ in range(8)] if d == 1 else None
            CW0 = PP_CW
            ai = 0
            if d == 0:
                B.dma("sp", HT.rearrange("p a b c -> p (a b c)"), fes[tau, :, 24 * 512:NFE])
                B.dma("sp", XT.rearrange("p a b -> p (a b)"), fes[tau, :, 0:16 * 512])
                B.dma("sp", BT.rearrange("p a b -> p (a b)"), fes[tau, :, 16 * 512:20 * 512])
                B.dma("sp", CT.rearrange("p a b -> p (a b)"), fes[tau, :, 20 * 512:24 * 512])
            for cb2 in (range(12) if d == 1 else ()):
                wv = B.wnext(w_in[l, :, C_X + cb2 * 256:C_X + (cb2 + 1) * 256], 8, 256)
                o0 = HALO - 2
                ch = []
                for j in range(2):
                    cb = cb2 * 2 + j
                    for s in range(2):
                        pb = B.bank((cb2 % 2) * 4 + j * 2 + s)
                        for k in range(8):
                            B.mm(pb[:, 0:SEGP], wv[:, k, j * 128:(j + 1) * 128], HT[:, k, s, :],
                                 start=(k == 0), stop=(k == 7))
                        ch.append((cb, s, pb, ACC[(cb2 % 2) * 4 + j * 2 + s]))
                for (cb, s, pb, acc) in ch:
                    B.act(acc, pb[:, o0:o0 + SEG], AF.Identity, bias=PP[:, PP_CB + cb:PP_CB + cb + 1],
                          scale=PP[:, CW0 + cb:CW0 + cb + 1])
                for tap in range(1, 5):
                    for (cb, s, pb, acc) in ch:
                        B.stt(acc, pb[:, o0 + tap:o0 + tap + SEG],
                              PP[:, CW0 + tap * 24 + cb:CW0 + tap * 24 + cb + 1], acc, ALU.mult, ALU.add)
                for (cb, s, pb, acc) in ch:
                    if cb < 16:
                        dst = XT[:, cb, s * SEG:(s + 1) * SEG]
                    elif cb < 20:
                        dst = BT[:, cb - 16, s * SEG:(s + 1) * SEG]
                    else:
                        dst = CT[:, cb - 20, s * SEG:(s + 1) * SEG]
                    B.act(dst, acc, AF.Silu)
            if d == 1:
                B.dma("sp", fes[tau, :, 24 * 512:NFE], HT.rearrange("p a b c -> p (a b c)"))
                B.dma("sp", fes[tau, :, 0:16 * 512], XT.rearrange("p a b -> p (a b)"))
                B.dma("sp", fes[tau, :, 16 * 512:20 * 512], BT.rearrange("p a b -> p (a b)"))
                B.dma("sp", fes[tau, :, 20 * 512:24 * 512], CT.rearrange("p a b -> p (a b)"))
            DTP = B.new(F32, [128])
            DT_ = B.new(F32, [128])
            DLA = B.new(F32, [128])
            ACU = B.new(F32, [128])
            TOT = B.new(F32, [128])
            WV = B.new(F32, [128])
            DEC = B.new(F32, [128])
            HI = B.new(BF16, [128])
            LO = B.new(BF16, [128])
            wdt = B.wnext(w_in[l, :, C_DT + 32 * d:C_DT + 32 * d + 32], 8, 32)
            pb = B.bank(6)
            for c in range(4):
                s, c0 = c // 2, HALO + 128 * (c % 2)
                for k in range(8):
                    B.mm(pb[:, c * 32:(c + 1) * 32], HT[:, k, s, c0:c0 + 128], wdt[:, k, :],
                         start=(k == 0), stop=(k == 7))
            bb = PP[:, PP_DTB + 32 * d:PP_DTB + 32 * d + 32].unsqueeze(1).broadcast_to([128, 4, 32])
            na = NEGA[:, 32 * d:32 * d + 32].unsqueeze(1).broadcast_to([128, 4, 32])
            B.tt("dve", v3(DTP), v3(pb[:, 0:128]), bb, ALU.add)
            B.act(DTP, DTP, AF.Exp)
            B.act(DT_, DTP, AF.Ln, bias=ONE_AP, scale=1.0)
            B.tt("dve", v3(DLA), v3(DT_), na, ALU.mult)
            pb = B.bank(7)
            B.mm(pb[:, 0:128], TRI_F, DLA)
            B.mm(pb[:, 128:256], ONEF, DLA)
            B.copy("act", ACU, pb[:, 0:128])
            B.copy("act", TOT, pb[:, 128:256])
            B.act(DEC, TOT, AF.Exp)
            if d == 0:
                tri_b, msk = TRIB_I, MSK_F
            else:
                B.tt("dve", ACU, DLA, ACU, ALU.subtract)
                B.tt("dve", ACU, ACU, TOT, ALU.add)
                tri_b, msk = CONB[:, 3, :], MSK_B
            B.tt("dve", WV, TOT, ACU, ALU.subtract)
            B.act(WV, WV, AF.Exp)
            B.tt("dve", WV, WV, DT_, ALU.mult)
            B.copy("dve", HI, DLA)
            B.tt("dve", LO, DLA, HI, ALU.subtract)
            CLB = ACU
            BK = B.new(BF16, [4, 512])
            for cp in range(2):
                pbb = B.bank(4 + cp, BF16)
                for c in (2 * cp, 2 * cp + 1):
                    for g in range(4):
                        o_ = ((c % 2) * 4 + g) * 128
                        B.tr(pbb[:, o_:o_ + 128], BT[:, g, c * 128:(c + 1) * 128], IDB)
                B.copy("act", BK[:, 2 * cp:2 * cp + 2, :].rearrange("p a b -> p (a b)"), pbb[:, 0:1024])
            xkc_off = B.alloc(2 * 4096)
            XKC = [B.view(xkc_off + i * 4096, BF16, [2048]) for i in range(2)]
            YG = B.view(xkc_off, F32, [4, TT]) if d == 0 else B.new(F32, [4, TT])
            CBM = [B.new(BF16, [128]) for _ in range(2)]
            SD4 = [B.new(F32, [512]) for _ in range(2)]
            SDB = [B.new(BF16, [512]) for _ in range(2)]
            ER4 = [B.new(BF16, [512]) for _ in range(2)]
            MR = [B.new(BF16, [4, 128]) for _ in range(4)]
            XDT = [B.new(BF16, [512]) for _ in range(2)]
            r3_off = B.alloc(6144)
            MR += [B.view(r3_off + i * 1024, BF16, [4, 128]) for i in range(2)]
            XDT.append(B.view(r3_off + 2048, BF16, [512]))
            CSR = [B.new(BF16, [4, 128]) for _ in range(4)]
            XW = [B.new(BF16, [512]) for _ in range(2)]
            CSR += [B.view(r3_off + 3072 + i * 1024, BF16, [4, 128]) for i in range(2)]
            XW.append(B.view(r3_off + 5120, BF16, [512]))
            STG4 = B.new(F32, [512])
            if d == 0:
                YBT = [B.new(F32, [TT]) for _ in range(2)]
                ZS2 = [B.view(r3_off + i * 2048, F32, [TT]) for i in range(2)]
                SQG2 = [B.view(r3_off + 4096 + i * 1024, BF16, [TT]) for i in range(2)]
                LNG = B.new(F32, [TT])
            corder = list(range(4)) if d == 0 else [3, 2, 1, 0]
            qd = [0]

            def stageA(g, c, it):
                cols = slice(c * 128, (c + 1) * 128)
                pcb = B.bank(6)
                cbo = (it % 2) * 128
                B.mm(pcb[:, cbo:cbo + 128], BT[:, g, cols], CT[:, g, cols])
                cbm = CBM[it % 2]
                B.tt("dve", cbm, pcb[:, cbo:cbo + 128], msk, ALU.mult)
                B.tt("pool", h64(XW[it % 3]), h64(XKC[c % 2][:, g * 512:(g + 1) * 512]),
                     WV[:, c * 32 + 8 * g:c * 32 + 8 * g + 8].unsqueeze(2).broadcast_to([128, 8, 64]), ALU.mult)
                B.tt("pool", h64(XDT[it % 3]), h64(XKC[c % 2][:, g * 512:(g + 1) * 512]),
                     DT_[:, c * 32 + 8 * g:c * 32 + 8 * g + 8].unsqueeze(2).broadcast_to([128, 8, 64]), ALU.mult)
                for quad in range(2):
                    pab = B.bank((it % 2) * 2 + quad)
                    sd = SD4[qd[0] % 2]
                    er = ER4[qd[0] % 2]
                    qd[0] += 1
                    cs = CSR[(it % 3) * 2 + quad]
                    for e4 in range(4):
                        q = c * 32 + 8 * g + quad * 4 + e4
                        ab = pab[:, e4 * 128:(e4 + 1) * 128]
                        B.mm(ab, HI[:, q:q + 1].broadcast_to([128, 128]), tri_b, start=True, stop=False)
                        B.mm(ab, LO[:, q:q + 1].broadcast_to([128, 128]), tri_b, start=False, stop=True)
                    for e4 in range(4):
                        q = c * 32 + 8 * g + quad * 4 + e4
                        ab = pab[:, e4 * 128:(e4 + 1) * 128]
                        B.act(sd[:, e4 * 128:(e4 + 1) * 128], ab, AF.Relu, bias=CLB[:, q:q + 1], scale=-1.0)
                    sdb = SDB[(qd[0] - 1) % 2]
                    B.act(sdb, sd, AF.Exp, scale=-1.0)
                    B.act(er, pab, AF.Exp)
                    B.tt("dve", MR[(it % 3) * 2 + quad], sdb.rearrange("p (a b) -> p a b", b=128),
                         cbm.unsqueeze(1).broadcast_to([128, 4, 128]), ALU.mult)
                    B.tt("dve", cs, CT[:, g, cols].unsqueeze(1).broadcast_to([128, 4, 128]),
                         er.rearrange("p (a b) -> p a b", b=128), ALU.mult)

            def stageB(g, c, it):
                cols = slice(c * 128, (c + 1) * 128)
                py4 = B.bank(4 + it % 2)
                for hp in range(4):
                    py = py4[:, hp * 128:(hp + 1) * 128]
                    for e in range(2):
                        e8 = 2 * hp + e
                        h = 8 * g + e8
                        B.mm(py[e * 64:(e + 1) * 64, :], XDT[it % 3][:, e8 * 64:(e8 + 1) * 64],
                             MR[(it % 3) * 2 + e8 // 4][:, e8 % 4, :], start=True, stop=False)
                        B.mm(py[e * 64:(e + 1) * 64, :], SBF[:, h * 64:(h + 1) * 64],
                             CSR[(it % 3) * 2 + e8 // 4][:, e8 % 4, :], start=False, stop=True)
                if d == 0:
                    for hp in range(4):
                        blk = 4 * g + hp
                        B.stt(YN[:, blk, cols], XT[:, blk, cols], PP[:, PP_DSK + blk:PP_DSK + blk + 1],
                              py4[:, hp * 128:(hp + 1) * 128], ALU.mult, ALU.add)
                if d == 1:
                    ygr = YG[:, :, (it % 4) * 128:(it % 4 + 1) * 128]
                    B.copy("dve", ygr, py4.rearrange("p (a b) -> p a b", b=128))
                    t_a = tau * TT + c * 128
                    B.dma("sp", ybs[4 * g:4 * g + 4, :, t_a:t_a + 128].rearrange("b p t -> p b t"), ygr)
                pl = B.bank(7)
                B.mm(pl, BK[:, c, g * 128:(g + 1) * 128], XW[it % 3])
                sg_ = SF[:, g * 512:(g + 1) * 512]
                B.tt("pool", h64(sg_), h64(sg_),
                     DEC[:, c * 32 + 8 * g:c * 32 + 8 * g + 8].unsqueeze(2).broadcast_to([128, 8, 64]), ALU.mult)
                B.tt("dve", sg_, sg_, pl, ALU.add)
                seg_end = (c % 2 == 1) if d == 0 else (c % 2 == 0)
                if seg_end:
                    sgi = 2 * tau + c // 2
                    if sgi < 4:
                        dst_t = sfo if d == 0 else sbo
                        ptb = B.bank(7)
                        for b4 in range(4):
                            B.tr(ptb[:, b4 * 128:(b4 + 1) * 128], sg_[:, b4 * 128:(b4 + 1) * 128], IDF)
                        B.copy("act", STG4, ptb)
                        r0 = g * 512
                        o = B.dma("sp", dst_t[sgi, l, r0:r0 + 512, :].rearrange("(b p) n -> p b n", p=128),
                                  STG4.rearrange("p (b n) -> p b n", n=128))
                        if o is not None:
                            B.out_dmas.append(o)
                    bnd = (sgi + 1) if d == 0 else sgi
                    B.ts("dve", sg_, sg_, mcol(bnd), ALU.mult)
                B.copy("act", SBF[:, g * 512:(g + 1) * 512], sg_)

            def xk_transposes(c):
                for half in range(2):
                    pbb = B.bank(4 + half, BF16)
                    for j in range(8):
                        B.tr(pbb[:, j * 128:(j + 1) * 128], XT[:, half * 8 + j, c * 128:(c + 1) * 128], IDB)
                    B.copy("act", XKC[c % 2][:, half * 1024:(half + 1) * 1024], pbb[:, 0:1024])

            its = [(g, c) for c in corder for g in range(4)]
            pend = []
            for n, (g, c) in enumerate(its):
                if g == 0:
                    xk_transposes(c)
                stageA(g, c, n)
                pend.append((g, c, n))
                if len(pend) > 2:
                    stageB(*pend.pop(0))
                if d == 1 and l + 1 < nl and modq[0] < 24:
                    mod_piece(l + 1, modq[0])
                    modq[0] += 1
            while pend:
                stageB(*pend.pop(0))
            if d == 0:
                for g in range(4):
                    pss = B.bank(6)
                    for hp2 in range(2):
                        wv = B.wnext(w_in[l, :, C_Z + g * 512 + hp2 * 256:C_Z + g * 512 + (hp2 + 1) * 256], 8, 256)
                        for j in range(2):
                            hp = hp2 * 2 + j
                            blk = 4 * g + hp
                            ybt = YBT[hp % 2]
                            B.dma("sp", ybt, ybs[blk, :, tau * TT:(tau + 1) * TT])
                            B.tt("dve", YG[:, hp, :], YN[:, blk, :], ybt, ALU.add)
                            pz = B.bank(mmbank())
                            for s in range(2):
                                for k in range(8):
                                    B.mm(pz[:, s * SEG:(s + 1) * SEG], wv[:, k, j * 128:(j + 1) * 128],
                                         HT[:, k, s, HALO:HALO + SEG], start=(k == 0), stop=(k == 7))
                            ZS, SQG = ZS2[hp % 2], SQG2[hp % 2]
                            if hp >= 1:
                                B.mm(pss, ONEB, SQG2[(hp - 1) % 2], start=(hp == 1), stop=False)
                            B.act(ZS, pz, AF.Silu)
                            B.tt("dve", YG[:, hp, :], YG[:, hp, :], ZS, ALU.mult)
                            B.act(SQG, YG[:, hp, :], AF.Square)
                    B.mm(pss, ONEB, SQG2[1], start=False, stop=True)
                    B.act(LNG, pss, AF.Ln, bias=EPS_AP, scale=1.0 / 512)
                    B.act(LNG, LNG, AF.Exp, scale=-0.5)
                    for hp in range(4):
                        blk = 4 * g + hp
                        B.stt(YN[:, blk, :], YG[:, hp, :], PP[:, PP_GSSD + blk:PP_GSSD + blk + 1], LNG,
                              ALU.mult, ALU.mult)
            if d == 1:
                B.top = tile_top
                continue
            B.top = pos_m
            MRG = B.new(F32, [8, TT])
            MB = B.new(BF16, [8, TT])
            GA = [B.new(BF16, [TT]) for _ in range(2)]
            TMPF = [B.new(F32, [TT]) for _ in range(2)]
            br_top = B.top
            gate_w = {}

            def branch_out(wsrc, K, rhs_of, goff, mode):
                for jo in range(8):
                    if jo % 2 == 0:
                        gate_w[0] = B.wnext(w_in[l, :, goff + jo * 128:goff + (jo + 2) * 128], 8, 256)
                    wvg = gate_w[0]
                    pg = B.bank(mmbank())
                    for s in range(2):
                        for k in range(8):
                            B.mm(pg[:, s * SEG:(s + 1) * SEG], wvg[:, k, (jo % 2) * 128:(jo % 2) * 128 + 128],
                                 HT[:, k, s, HALO:HALO + SEG], start=(k == 0), stop=(k == 7))
                    ga = GA[jo % 2]
                    B.act(ga, pg, AF.Sigmoid)
                    pbr = B.bank(mmbank())
                    for kh in range((K + 15) // 16):
                        k0 = kh * 16
                        kk = min(16, K - k0)
                        wv = B.wnext(wsrc[k0 * 128:(k0 + kk) * 128, jo * 128:(jo + 1) * 128], kk, 128)
                        for k in range(kk):
                            B.mm(pbr, wv[:, k, :], rhs_of(k0 + k), start=(k0 + k == 0), stop=(k0 + k == K - 1))
                    if mode == 0:
                        B.tt("dve", MRG[:, jo, :], pbr, ga, ALU.mult)
                    else:
                        tf = TMPF[jo % 2]
                        B.tt("dve", tf, pbr, ga, ALU.mult)
                        B.tt("dve", MRG[:, jo, :] if mode == 1 else MB[:, jo, :], MRG[:, jo, :], tf, ALU.add)

            branch_out(I["w_br_ssd"][l], 16, lambda k: YN[:, k, :], C_G, 0)
            VW = B.view(YN_off, BF16, [8, 1024])
            for q4 in range(4):
                B.dma("pool", VW[:, :, q4 * 256:(q4 + 1) * 256],
                      w_in[l, :, C_V + q4 * 256:C_V + (q4 + 1) * 256].rearrange("(k p) c -> p k c", p=128))
            VI = B.new(F32, [2, SEGP])
            CN = [B.new(F32, [SEGP]) for _ in range(4)]
            RC = B.new(F32, [4, 2, SEG])
            for s in range(2):
                sg = 2 * tau + s
                B.memset("dve", VI[:, s, :], 1.0)
                B.ts("dve", VI[:, s, 0:HALO], VI[:, s, 0:HALO], mcol(sg), ALU.mult)
                B.ts("dve", VI[:, s, SEGP - HALO:SEGP], VI[:, s, SEGP - HALO:SEGP], mcol(sg + 1), ALU.mult)
                B.tt("dve", CN[0][:, 1:SEGP], VI[:, s, 0:SEGP - 1], VI[:, s, 1:SEGP], ALU.add)
                B.tt("dve", CN[1][:, 2:SEGP - 1], CN[0][:, 1:SEGP - 2], CN[0][:, 3:SEGP], ALU.add)
                B.tt("dve", CN[2][:, 4:SEGP - 3], CN[1][:, 2:SEGP - 5], CN[1][:, 6:SEGP - 1], ALU.add)
                B.tt("dve", CN[3][:, 8:SEGP - 7], CN[2][:, 4:SEGP - 11], CN[2][:, 12:SEGP - 3], ALU.add)
                for wi in range(4):
                    S.op("dve", (lambda o_, i_: (lambda e: e.reciprocal(out=o_, in_=i_)))(
                        RC[:, wi, s, :], CN[wi][:, HALO:HALO + SEG]),
                        reads=[CN[wi][:, HALO:HALO + SEG]], writes=[RC[:, wi, s, :]])
            PL = B.new(BF16, [8, TT])
            P0 = [B.new(F32, [SEGP]) for _ in range(2)]
            SM = [B.new(F32, [SEGP]) for _ in range(4)]
            pi_ = 0
            for cb2 in range(4):
                wv = B.wnext(w_in[l, :, C_P + cb2 * 256:C_P + (cb2 + 1) * 256], 8, 256)
                for j in range(2):
                    cb = cb2 * 2 + j
                    wi = cb // 2
                    for s in range(2):
                        pb = B.bank(mmbank())
                        for k in range(8):
                            B.mm(pb[:, 0:SEGP], wv[:, k, j * 128:(j + 1) * 128], HT[:, k, s, :],
                                 start=(k == 0), stop=(k == 7))
                        p0 = P0[pi_ % 2]
                        pi_ += 1
                        B.copy("act", p0, pb[:, 0:SEGP])
                        B.tt("dve", SM[0][:, 1:SEGP], p0[:, 0:SEGP - 1], p0[:, 1:SEGP], ALU.add)
                        if wi >= 1:
                            B.tt("dve", SM[1][:, 2:SEGP - 1], SM[0][:, 1:SEGP - 2], SM[0][:, 3:SEGP], ALU.add)
                        if wi >= 2:
                            B.tt("dve", SM[2][:, 4:SEGP - 3], SM[1][:, 2:SEGP - 5], SM[1][:, 6:SEGP - 1], ALU.add)
                        if wi >= 3:
                            B.tt("dve", SM[3][:, 8:SEGP - 7], SM[2][:, 4:SEGP - 11], SM[2][:, 12:SEGP - 3], ALU.add)
                        sm = SM[wi]
                        B.tt("dve", sm[:, HALO:HALO + SEG], sm[:, HALO:HALO + SEG], RC[:, wi, s, :], ALU.mult)
                        B.tt("dve", PL[:, cb, s * SEG:(s + 1) * SEG], sm[:, HALO:HALO + SEG],
                             p0[:, HALO:HALO + SEG], ALU.subtract)
            YP = B.new(BF16, [8, TT])
            wpv = B.wnext(I["w_pool"][l].rearrange("g r c -> (g r) c"), 8, 256)
            for g4 in range(4):
                for ob in range(2):
                    pb = B.bank(mmbank())
                    for kb in range(2):
                        B.mm(pb, wpv[:, g4 * 2 + kb, ob * 128:(ob + 1) * 128], PL[:, 2 * g4 + kb, :],
                             start=(kb == 0), stop=(kb == 1))
                    blk = 2 * g4 + ob
                    B.act(YP[:, blk, :], pb, AF.Identity, scale=PP[:, PP_PSC + blk:PP_PSC + blk + 1])
            branch_out(I["w_br_pool"][l], 8, lambda k: YP[:, k, :], C_G + 1024, 1)
            B.top = br_top
            GSGU = B.new(F32, [1024])
            BSP = B.new(F32, [512])
            B.dma("sp", GSGU, pp[l, :, PP_GSGU:PP_GSGU + 1024])
            B.dma("sp", BSP, pp[l, :, PP_BSP:PP_BSP + 512])
            VN = B.new(BF16, [4, 1024])
            UT = B.new(BF16, [8, TT])
            YM = B.new(BF16, [8, TT])
            SSV = B.new(F32, [8])
            JNK = B.new(BF16, [1024])
            for c in range(4):
                s, c0 = c // 2, HALO + 128 * (c % 2)
                b0 = (c % 2) * 2
                for q4 in range(4):
                    pb = B.bank(b0 + q4 // 2)
                    for k in range(8):
                        B.mm(pb[:, (q4 % 2) * 256:(q4 % 2) * 256 + 256], HT[:, k, s, c0:c0 + 128],
                             VW[:, k, q4 * 256:(q4 + 1) * 256], start=(k == 0), stop=(k == 7))
                pv = B.psum[:, b0 * 512:(b0 + 2) * 512]
                B.act(JNK, pv, AF.Square, accum_out=SSV[:, c:c + 1])
                B.act(SSV[:, 4 + c:5 + c], SSV[:, c:c + 1], AF.Ln, bias=EPS_AP, scale=1.0 / D)
                B.act(SSV[:, 4 + c:5 + c], SSV[:, 4 + c:5 + c], AF.Exp, scale=-0.5)
                B.stt(VN[:, c, :], pv, SSV[:, 4 + c:5 + c], GSGU, ALU.mult, ALU.mult)
            for cb2 in range(4):
                wv = B.wnext(w_in[l, :, C_U + cb2 * 256:C_U + (cb2 + 1) * 256], 8, 256)
                for j in range(2):
                    cb = cb2 * 2 + j
                    pu = B.bank(mmbank())
                    for s in range(2):
                        for k in range(8):
                            B.mm(pu[:, s * SEG:(s + 1) * SEG], wv[:, k, j * 128:(j + 1) * 128],
                                 HT[:, k, s, HALO:HALO + SEG], start=(k == 0), stop=(k == 7))
                    B.copy("act", UT[:, cb, :], pu)
                    g4 = cb // 2
                    psv = B.bank(mmbank())
                    for c in range(4):
                        B.mm(psv[:, c * 128:(c + 1) * 128], VN[:, c, cb * 128:(cb + 1) * 128], WST[:, g4, :])
                    tf = TMPF[cb % 2]
                    B.tt("dve", tf.rearrange("p (c i) -> p c i", i=128), psv.rearrange("p (c i) -> p c i", i=128),
                         BSP[:, g4 * 128:(g4 + 1) * 128].unsqueeze(1).broadcast_to([128, 4, 128]), ALU.add)
                    B.tt("dve", YM[:, cb, :], tf, UT[:, cb, :], ALU.mult)
            branch_out(I["w_br_gmlp"][l], 8, lambda k: YM[:, k, :], C_G + 2048, 2)
            tcol = slice(HALO + tau * TT, HALO + (tau + 1) * TT)
            for jo in range(8):
                wv = B.wnext(I["w_out"][l][:, jo * 128:(jo + 1) * 128], 8, 128)
                po = B.bank(mmbank())
                for k in range(8):
                    B.mm(po, wv[:, k, :], MB[:, k, :], start=(k == 0), stop=(k == 7))
                B.stt(XR[:, jo, tcol], po, MODV[:, l, 16 + jo:17 + jo], XR[:, jo, tcol], ALU.mult, ALU.add)
            B.top = tile_top
            modnorm(l, 1, tau, False)
            HID = B.new(BF16, [22, TT])
            SGT = [B.new(F32, [TT]) for _ in range(2)]
            for j2 in range(11):
                wg = B.wnext(I["w_ffn_in"][l][:, j2 * 256:(j2 + 1) * 256], 8, 256)
                wu = B.wnext(I["w_ffn_in"][l][:, DFF + j2 * 256:DFF + (j2 + 1) * 256], 8, 256)
                for j in range(2):
                    jj = j2 * 2 + j
                    pg = B.bank(mmbank())
                    pu = B.bank(mmbank())
                    for (pp_, wv) in ((pg, wg), (pu, wu)):
                        for s in range(2):
                            for k in range(8):
                                B.mm(pp_[:, s * SEG:(s + 1) * SEG], wv[:, k, j * 128:(j + 1) * 128],
                                     HT[:, k, s, HALO:HALO + SEG], start=(k == 0), stop=(k == 7))
                    sg_ = SGT[jj % 2]
                    B.act(sg_, pg, AF.Silu)
                    B.tt("dve", HID[:, jj, :], pu, sg_, ALU.mult)
            for jo in range(8):
                po = B.bank(mmbank())
                for kh in range(2):
                    wv = B.wnext(I["w_ffn_out"][l][kh * 11 * 128:(kh + 1) * 11 * 128, jo * 128:(jo + 1) * 128], 11, 128)
                    for k in range(11):
                        B.mm(po, wv[:, k, :], HID[:, kh * 11 + k, :], start=(kh == 0 and k == 0),
                             stop=(kh == 1 and k == 10))
                B.stt(XR[:, jo, tcol], po, MODV[:, l, 40 + jo:41 + jo], XR[:, jo, tcol], ALU.mult, ALU.add)
            B.top = tile_top

    for l in range(nl):
        layer_params(l)
        sweep(l, 1)
        sweep(l, 0)

    GF = B.new(F32, [D])
    B.dma("sp", GF, gfin)
    OS = [B.new(F32, [D]) for _ in range(2)]
    SSO = B.new(F32, [32])
    JN2 = B.new(BF16, [D])
    for tc in range(16):
        b0 = (tc % 2) * 2
        for blk in range(8):
            pb = B.bank(b0 + blk // 4)
            B.tr(pb[:, (blk % 4) * 128:(blk % 4) * 128 + 128], XR[:, blk, HALO + tc * 128:HALO + (tc + 1) * 128], IDF)
        pv = B.psum[:, b0 * 512:(b0 + 2) * 512]
        B.act(JN2, pv, AF.Square, accum_out=SSO[:, tc:tc + 1])
        B.act(SSO[:, 16 + tc:17 + tc], SSO[:, tc:tc + 1], AF.Ln, bias=EPS_AP, scale=1.0 / D)
        B.act(SSO[:, 16 + tc:17 + tc], SSO[:, 16 + tc:17 + tc], AF.Exp, scale=-0.5)
        osb = OS[tc % 2]
        B.stt(osb, pv, SSO[:, 16 + tc:17 + tc], GF, ALU.mult, ALU.mult)
        o = B.dma("sp", yout[tc * 128:(tc + 1) * 128, :], osb)
        if o is not None:
            B.out_dmas.append(o)


_CACHE = {}


def _consts():
    j = np.arange(128)[:, None]
    i = np.arange(128)[None, :]
    return np.stack([np.eye(128), (j <= i), (j < i), (j >= i)]).astype(np.float32)


def _pack_pp(inp, l):
    f = np.float32
    pm = lambda v: np.ascontiguousarray(np.asarray(v, f).reshape(-1, 128).T)
    cols = [pm(inp["g_norm1"][l]), pm(inp["g_norm2"][l]), pm(inp["b_ada"][l])]
    cw = np.asarray(inp["conv_w"][l], f)[:, 2048 - 2048:]
    cols.append(np.concatenate([pm(cw[t]) for t in range(5)], axis=1))
    cols.append(pm(inp["conv_b"][l]))
    cols.append(pm(np.repeat(np.asarray(inp["d_skip"][l], f), 64)))
    cols.append(pm(inp["g_ssd"][l]))
    cols.append(pm(inp["pool_scale"][l]))
    cols.append(np.broadcast_to(np.asarray(inp["dt_bias"][l], f).reshape(1, 64), (128, 64)))
    cols.append(np.broadcast_to(np.asarray(inp["a_log"][l], f).reshape(1, 64), (128, 64)))
    cols.append(np.broadcast_to(np.asarray(inp["g_sgu"][l], f).reshape(1, 1024), (128, 1024)))
    cols.append(np.broadcast_to(np.asarray(inp["b_spatial"][l], f).reshape(1, 512), (128, 512)))
    out = np.concatenate(cols, axis=1).astype(f)
    assert out.shape == (128, NPP), out.shape
    return out


def kernel(nl=DEPTH, **inp):
    f = np.float32
    if nl not in _CACHE:
        _CACHE[nl] = build_program(nl)
    nc = _CACHE[nl]
    xp = np.asarray(inp["x_prompt"], f)
    xs = np.asarray(inp["x_sample"], f)
    NLW = max(nl, 1)
    pp = np.stack([_pack_pp(inp, l) for l in range(NLW)])
    shared = {
        "in_consts": _consts(),
        "in_pp": pp,
        "in_gfin": np.ascontiguousarray(np.broadcast_to(np.asarray(inp["g_final"], f)[None, :], (128, D))),
    }
    for k in ("w_ada", "w_in", "w_br_ssd", "w_pool", "w_br_pool", "w_spatial", "w_br_gmlp", "w_out",
              "w_ffn_in", "w_ffn_out"):
        shared["in_" + k] = np.ascontiguousarray(np.asarray(inp[k], f)[:NLW])
    in_maps = []
    zst = np.zeros((NLW, NH * 64, 128), f)
    for core in range(8):
        m = dict(shared)
        fl = np.zeros((128, 16), f)
        if core < 4:
            m["in_xin"] = np.ascontiguousarray(xs[core])
            m["in_s0f"] = np.ascontiguousarray(np.asarray(inp["state_ssd_fwd"], f)[core, :NLW].reshape(NLW, NH * 64, 128))
            m["in_s0b"] = np.ascontiguousarray(np.asarray(inp["state_ssd_bwd"], f)[core, :NLW].reshape(NLW, NH * 64, 128))
            cv = np.asarray(inp["c"], f)[core]
            fl[:, 0] = 1.0
            fl[:, 2:9] = 1.0
        else:
            q = core - 4
            xx = np.zeros((T, D), f)
            xx[:1024] = xp[4 * q:4 * q + 4].reshape(1024, D)
            m["in_xin"] = xx
            m["in_s0f"] = zst
            m["in_s0b"] = zst
            cv = np.asarray(inp["c_ctx"], f)
        m["in_cvec"] = np.ascontiguousarray(cv.reshape(8, 128).T)
        m["in_flags"] = fl
        in_maps.append(m)
    res = run_bass_kernel_spmd(nc, in_maps, core_ids=list(range(8)))
    R = res.results
    y_sample = np.stack([R[c]["yout"] for c in range(4)]).astype(f)
    y_prompt = np.concatenate([R[4 + q]["yout"][:1024].reshape(4, 256, D) for q in range(4)]).astype(f)
    nsf = np.concatenate([R[4 + q]["sfo"].reshape(4, NLW, NH, 64, 128) for q in range(4)]).astype(f)
    nsb = np.concatenate([R[4 + q]["sbo"].reshape(4, NLW, NH, 64, 128) for q in range(4)]).astype(f)
    return (y_prompt, y_sample, nsf, nsb)
```

```python
import math
from contextlib import ExitStack
import numpy as np
import concourse.bass as bass
import concourse.mybir as mybir
from concourse.bass_utils import run_bass_kernel_spmd

F32 = mybir.dt.float32
BF16 = mybir.dt.bfloat16
I32 = mybir.dt.int32
AF = mybir.ActivationFunctionType
ALU = mybir.AluOpType

D = 1024
DEPTH = 4
T = 2048
NTILE = 4
TT = 512
SEG = 256
HALO = 8
SEGP = SEG + 2 * HALO
XRW = T + 2 * HALO
NH = 32
DFF = 2816
INC = 11328
EPS = 1e-6
C_Z, C_X, C_B, C_C, C_DT, C_P, C_U, C_V, C_G = 0, 2048, 4096, 4608, 5120, 5184, 6208, 7232, 8256
NPP = 376 + 1024 + 512
NPPS = 376
PP_G1, PP_G2, PP_BADA, PP_CW, PP_CB, PP_DSK, PP_GSSD, PP_PSC, PP_DTB, PP_ALOG, PP_GSGU, PP_BSP = (
    0, 8, 16, 64, 184, 208, 224, 240, 248, 312, 376, 1400)
ARENA_KB = 206
CELL = 256
NSLOT = 4
SLOT_B = 4096
AHEAD = 2
NDSEM = 12
NFE = 24 * 512 + 8 * 2 * SEGP


def _esize(dt):
    return 2 if dt == BF16 else 4


class Op:
    __slots__ = ("stream", "dom", "idx", "fn", "waits", "signaled", "is_dma", "count")


class Sched:
    def __init__(self):
        self.streams = {k: [] for k in ("pe", "act", "dve", "pool", "sp")}
        self.domcnt = {}
        self.cells = {}
        self.waited = {k: {} for k in self.streams}
        self.dma_rr = {"sp": 0, "pool": 0}
        self.dma_last = {}
        self.nops = 0
        self.dry = False

    @staticmethod
    def region(ap):
        t = ap.tensor
        name = t.name
        es = _esize(ap.dtype)
        dims = [list(d) for d in ap.ap]
        cls = type(t).__name__
        if cls.startswith("DRam"):
            ext = sum((n - 1) * abs(s) for s, n in dims) + 1
            b0 = ap.offset * es
            return ("d:" + name, b0 // 65536, (b0 + ext * es - 1) // 65536 + 1, 0, 2)
        pstride, pn = dims[0]
        if pstride == 0:
            pstride = 1 << 40
        fd = dims[1:]
        ext = sum((n - 1) * abs(s) for s, n in fd) + 1
        p0 = ap.offset // pstride if pstride < (1 << 40) else 0
        c0 = ap.offset - p0 * pstride if pstride < (1 << 40) else ap.offset
        b0 = c0 * es
        b1 = (c0 + ext) * es
        p1 = p0 + pn
        h0 = 0 if p0 < 64 else 1
        h1 = 1 if p1 <= 64 else 2
        if cls.startswith("PS") or "psum" in name:
            return ("ps:" + name, b0 // 2048, (b1 - 1) // 2048 + 1, p0 // 32, (p1 - 1) // 32 + 1)
        return ("sb:" + name, b0 // CELL, (b1 - 1) // CELL + 1, h0, h1)

    def op(self, stream, fn, reads=(), writes=(), dma=False):
        self.nops += 1
        if self.dry:
            return None
        o = Op()
        o.stream = stream
        o.is_dma = dma
        o.fn = fn
        o.signaled = dma
        o.count = None
        if dma:
            k = self.dma_rr[stream] % NDSEM
            self.dma_rr[stream] += 1
            o.dom = (stream, k)
        else:
            o.dom = stream
        o.idx = self.domcnt.get(o.dom, 0)
        self.domcnt[o.dom] = o.idx + 1
        need = {}

        def want(w):
            if w is None:
                return
            if w.dom == "pe" and stream == "pe" and not dma:
                return
            cur = need.get(w.dom)
            if cur is None or cur.idx < w.idx:
                need[w.dom] = w
        if dma:
            want(self.dma_last.get(o.dom))
            self.dma_last[o.dom] = o
        cells = self.cells
        for ap in reads:
            if ap is None:
                continue
            sp, c0, c1, h0, h1 = self.region(ap)
            if sp.startswith("d:in_"):
                continue
            for c in range(c0, c1):
                for h in range(h0, h1):
                    rec = cells.get((sp, c, h))
                    if rec is None:
                        rec = [None, {}]
                        cells[(sp, c, h)] = rec
                    want(rec[0])
        for ap in writes:
            sp, c0, c1, h0, h1 = self.region(ap)
            for c in range(c0, c1):
                for h in range(h0, h1):
                    rec = cells.get((sp, c, h))
                    if rec is None:
                        rec = [None, {}]
                        cells[(sp, c, h)] = rec
                    want(rec[0])
                    for r in rec[1].values():
                        want(r)
        wl = []
        wd = self.waited[stream]
        for dom, w in need.items():
            if wd.get(dom, -1) >= w.idx:
                continue
            wd[dom] = w.idx
            w.signaled = True
            wl.append(w)
        o.waits = wl
        for ap in reads:
            if ap is None:
                continue
            sp, c0, c1, h0, h1 = self.region(ap)
            if sp.startswith("d:in_"):
                continue
            for c in range(c0, c1):
                for h in range(h0, h1):
                    rec = cells[(sp, c, h)]
                    cur = rec[1].get(o.dom)
                    if cur is None or cur.idx < o.idx:
                        rec[1][o.dom] = o
        for ap in writes:
            sp, c0, c1, h0, h1 = self.region(ap)
            for c in range(c0, c1):
                for h in range(h0, h1):
                    rec = cells[(sp, c, h)]
                    rec[0] = o
                    rec[1] = {}
        self.streams[stream].append(o)
        return o

    def emit(self, nc, final_waits):
        doms = set(self.domcnt.keys())
        for dom in doms:
            cnt = 0
            for st in self.streams.values():
                pass
        per_dom = {}
        for sname, ops in self.streams.items():
            for o in ops:
                per_dom.setdefault(o.dom, []).append(o)
        for dom, ops in per_dom.items():
            ops.sort(key=lambda x: x.idx)
            c = 0
            for o in ops:
                if o.signaled:
                    c += 1
                    o.count = c
        with ExitStack() as es:
            sems = {}
            for dom in per_dom:
                nm = dom if isinstance(dom, str) else "%s_d%d" % dom
                sems[dom] = es.enter_context(nc.semaphore("s_" + nm))
            block = es.enter_context(nc.Block())

            def run(sname, eng):
                for o in self.streams[sname]:
                    for w in o.waits:
                        inc = 16 if w.is_dma else 1
                        eng.wait_ge(sems[w.dom], w.count * inc)
                    ins = o.fn(eng)
                    if o.signaled:
                        ins.then_inc(sems[o.dom], 16 if o.is_dma else 1)
                if sname == "sp":
                    for o in final_waits:
                        eng.wait_ge(sems[o.dom], o.count * 16)

            @block.tensor
            def _(e):
                run("pe", e)

            @block.scalar
            def _(e):
                run("act", e)

            @block.vector
            def _(e):
                run("dve", e)

            @block.gpsimd
            def _(e):
                run("pool", e)

            @block.sync
            def _(e):
                run("sp", e)


class Builder:
    def __init__(self, nc, nl):
        self.nc = nc
        self.nl = nl
        self.S = Sched()
        self.top = 0
        self.arena = None
        self.psum = None
        self.wreq = []
        self.wi = 0
        self.wissued = 0
        self.out_dmas = []

    def alloc(self, nbytes):
        off = (self.top + CELL - 1) // CELL * CELL
        self.top = off + nbytes
        assert self.top <= ARENA_KB * 1024, ("SBUF arena overflow", self.top)
        return off

    def view(self, off, dt, shape):
        n = int(np.prod(shape))
        es = _esize(dt)
        a = self.arena[:, off // 4:(off + n * es + 3) // 4]
        if dt != F32:
            a = a.bitcast(dt)
        if len(shape) == 2:
            a = a.rearrange("p (a b) -> p a b", b=shape[1])
        elif len(shape) == 3:
            a = a.rearrange("p (a b c) -> p a b c", b=shape[1], c=shape[2])
        return a

    def new(self, dt, shape):
        return self.view(self.alloc(int(np.prod(shape)) * _esize(dt)), dt, shape)

    def bank(self, b, dt=F32):
        a = self.psum[:, b * 512:(b + 1) * 512]
        if dt != F32:
            a = a.bitcast(dt)
        return a

    def mm(self, out, lhsT, rhs, start=True, stop=True):
        rd = [lhsT, rhs] + ([] if start else [out])
        return self.S.op("pe", lambda e: e.matmul(out, lhsT=lhsT, rhs=rhs, start=start, stop=stop),
                         reads=rd, writes=[out])

    def tr(self, out, in_, ident):
        return self.S.op("pe", lambda e: e.transpose(out, in_, ident), reads=[in_, ident], writes=[out])

    def act(self, out, in_, func, bias=None, scale=1.0, accum_out=None):
        rd = [in_]
        kw = {}
        if bias is not None:
            kw["bias"] = bias
            if not isinstance(bias, (int, float)):
                rd.append(bias)
        if not isinstance(scale, (int, float)):
            rd.append(scale)
        wr = [out]
        if accum_out is not None:
            kw["accum_out"] = accum_out
            wr.append(accum_out)
        return self.S.op("act", lambda e: e.activation(out=out, in_=in_, func=func, scale=scale, **kw),
                         reads=rd, writes=wr)

    def tt(self, eng, out, in0, in1, op):
        return self.S.op(eng, lambda e: e.tensor_tensor(out=out, in0=in0, in1=in1, op=op),
                         reads=[in0, in1], writes=[out])

    def ts(self, eng, out, in0, s1, op0, s2=None, op1=None):
        rd = [in0] + [s for s in (s1, s2) if s is not None and not isinstance(s, (int, float))]
        if op1 is None:
            return self.S.op(eng, lambda e: e.tensor_scalar(out=out, in0=in0, scalar1=s1, scalar2=None, op0=op0),
                             reads=rd, writes=[out])
        return self.S.op(eng, lambda e: e.tensor_scalar(out=out, in0=in0, scalar1=s1, scalar2=s2, op0=op0, op1=op1),
                         reads=rd, writes=[out])

    def stt(self, out, in0, scalar, in1, op0, op1):
        rd = [in0, in1] + ([] if isinstance(scalar, (int, float)) else [scalar])
        return self.S.op("dve", lambda e: e.scalar_tensor_tensor(out=out, in0=in0, scalar=scalar, in1=in1,
                                                                   op0=op0, op1=op1), reads=rd, writes=[out])

    def copy(self, eng, out, in_):
        if eng == "act":
            return self.S.op("act", lambda e: e.copy(out=out, in_=in_), reads=[in_], writes=[out])
        return self.S.op(eng, lambda e: e.tensor_copy(out=out, in_=in_), reads=[in_], writes=[out])

    def memset(self, eng, ap, val):
        return self.S.op(eng, lambda e: e.memset(ap, val), reads=[], writes=[ap])

    def dma(self, stream, out, in_, slow=False):
        if slow:
            return self.S.op(stream, lambda e: e.dma_start(out=out, in_=in_, allow_slow_non_contiguous=True),
                             reads=[in_], writes=[out], dma=True)
        return self.S.op(stream, lambda e: e.dma_start(out=out, in_=in_), reads=[in_], writes=[out], dma=True)

    def wnext(self, src, K, ncols):
        assert K * ncols * 2 <= SLOT_B
        if self.S.dry:
            self.wreq.append((src, K, ncols))
            return self.view(self.wslots[0], BF16, [K, ncols])
        i = self.wi
        self.wi += 1
        while self.wissued < min(len(self.wreq), i + AHEAD + 1):
            s2, K2, n2 = self.wreq[self.wissued]
            dst = self.view(self.wslots[self.wissued % NSLOT], BF16, [K2, n2])
            self.dma("pool", dst, s2.rearrange("(k p) c -> p k c", p=128))
            self.wissued += 1
        return self.view(self.wslots[i % NSLOT], BF16, [K, ncols])


def build_program(nl=DEPTH):
    nc = bass.Bass("TRN2", target_bir_lowering=False)
    dt_in = {}

    def din(name, shape):
        dt_in[name] = nc.dram_tensor("in_" + name, list(shape), F32, kind="ExternalInput").ap()
        return dt_in[name]

    NLW = max(nl, 1)
    xin = din("xin", [T, D])
    s0f = din("s0f", [NLW, NH * 64, 128])
    s0b = din("s0b", [NLW, NH * 64, 128])
    cvec = din("cvec", [128, 8])
    flags = din("flags", [128, 16])
    consts = din("consts", [4, 128, 128])
    pp = din("pp", [NLW, 128, NPP])
    gfin = din("gfin", [128, D])
    w_ada = din("w_ada", [NLW, D, 6 * D])
    w_in = din("w_in", [NLW, D, INC])
    w_br_ssd = din("w_br_ssd", [NLW, 2048, D])
    w_pool = din("w_pool", [NLW, 4, 256, 256])
    w_br_pool = din("w_br_pool", [NLW, D, D])
    w_spatial = din("w_spatial", [NLW, 4, 128, 128])
    w_br_gmlp = din("w_br_gmlp", [NLW, D, D])
    w_out = din("w_out", [NLW, D, D])
    w_ffn_in = din("w_ffn_in", [NLW, D, 2 * DFF])
    w_ffn_out = din("w_ffn_out", [NLW, DFF, D])
    yout = nc.dram_tensor("yout", [T, D], F32, kind="ExternalOutput").ap()
    sfo = nc.dram_tensor("sfo", [4, NLW, NH * 64, 128], F32, kind="ExternalOutput").ap()
    sbo = nc.dram_tensor("sbo", [4, NLW, NH * 64, 128], F32, kind="ExternalOutput").ap()
    ybs = nc.dram_tensor("ybs", [16, 128, T], F32, kind="Internal").ap()
    fes = nc.dram_tensor("fes", [NTILE, 128, NFE], BF16, kind="Internal").ap()

    with ExitStack() as es:
        arena_t = es.enter_context(nc.sbuf_tensor("arena", [128, ARENA_KB * 256], F32))
        psum_t = es.enter_context(nc.psum_tensor("psum", [128, 8 * 512], F32))
        B = Builder(nc, nl)
        B.arena = arena_t[:]
        B.psum = psum_t[:]
        for dry in (True, False):
            B.S.dry = dry
            B.top = 0
            B.wi = 0
            B.wissued = 0
            _program(B, dt_in, yout, sfo, sbo, ybs, fes)
        B.S.emit(nc, B.out_dmas)
    return nc


def _program(B, I, yout, sfo, sbo, ybs, fes):
    nl = B.nl
    S = B.S
    xin, s0f, s0b, cvec, flags, consts, pp, gfin = (I[k] for k in
                                                      ("xin", "s0f", "s0b", "cvec", "flags", "consts", "pp", "gfin"))
    w_in = I["w_in"]
    B.wslots = [B.alloc(SLOT_B) for _ in range(NSLOT)]
    XR = B.new(F32, [8, XRW])
    CON = B.new(F32, [4, 128])
    IDF, TRI_F, MSK_F, MSK_B = CON[:, 0, :], CON[:, 1, :], CON[:, 1, :], CON[:, 3, :]
    CONB = B.new(BF16, [4, 128])
    IDB, TRIB_I, TRIB_E = CONB[:, 0, :], CONB[:, 1, :], CONB[:, 2, :]
    ONEB = B.new(BF16, [128])
    ONEF = B.new(F32, [128])
    ZERO = B.new(F32, [128])
    FLG = B.new(F32, [16])
    EPSC = B.new(F32, [2])
    MODV = B.new(F32, [DEPTH, 48])
    GS = B.new(F32, [DEPTH, 2, 8])
    PP = B.new(F32, [NPPS])
    NEGA = B.new(F32, [64])
    WST = B.new(BF16, [4, 128])
    SF = B.new(F32, [2048])
    SBF = B.new(BF16, [2048])
    HT = B.new(BF16, [8, 2, SEGP])
    HSAVE = B.new(BF16, [8, HALO])
    CV = B.new(F32, [8])
    CVB = B.new(BF16, [8])
    PPB = B.new(F32, [48])
    base_top = B.top

    def mcol(i):
        return FLG[:, 1 + i:2 + i]

    bankrr = [0]

    def mmbank():
        b = bankrr[0] % 4
        bankrr[0] += 1
        return b

    B.dma("sp", CON, consts.rearrange("c p f -> p c f"))
    B.dma("sp", FLG, flags)
    B.copy("dve", CONB, CON)
    B.memset("dve", ONEB, 1.0)
    B.memset("dve", ONEF, 1.0)
    B.memset("dve", ZERO, 0.0)
    B.memset("dve", EPSC[:, 0:1], EPS)
    B.memset("dve", EPSC[:, 1:2], 1.0)
    B.memset("dve", XR[:, :, 0:HALO], 0.0)
    B.memset("dve", XR[:, :, XRW - HALO:XRW], 0.0)
    EPS_AP = EPSC[:, 0:1]
    ONE_AP = EPSC[:, 1:2]

    tmp0 = B.top
    IOI = B.new(I32, [96])
    OMI = B.new(I32, [2])
    OM = B.new(F32, [2])
    POSV = B.new(F32, [96])
    ANG = B.new(F32, [4, 96])
    T1 = B.new(F32, [4, 96])
    KI = B.new(I32, [4 * 96])
    T2 = B.new(F32, [4, 96])
    S.op("pool", lambda e: e.iota(IOI[:, 0:32], pattern=[[1, 32]], base=0, channel_multiplier=0), writes=[IOI[:, 0:32]])
    S.op("pool", lambda e: e.iota(IOI[:, 32:96], pattern=[[1, 64]], base=0, channel_multiplier=0), writes=[IOI[:, 32:96]])
    S.op("pool", lambda e: e.iota(OMI, pattern=[[128, 2]], base=0, channel_multiplier=1), writes=[OMI])
    B.copy("dve", POSV, IOI)
    B.copy("dve", OM, OMI)
    B.act(OM, OM, AF.Exp, scale=-math.log(10000.0) / 256.0)
    TWO_PI = 2.0 * math.pi
    for b2 in range(2):
        B.ts("dve", ANG[:, b2, :], POSV, OM[:, b2:b2 + 1], ALU.mult)
        B.ts("dve", ANG[:, 2 + b2, :], POSV, OM[:, b2:b2 + 1], ALU.mult, math.pi / 2, ALU.add)
    A2 = ANG.rearrange("p a b -> p (a b)")
    T12 = T1.rearrange("p a b -> p (a b)")
    T22 = T2.rearrange("p a b -> p (a b)")
    B.ts("dve", T12, A2, 1.0 / TWO_PI, ALU.mult)
    B.copy("dve", KI, T12)
    B.copy("dve", T12, KI)
    B.stt(T22, T12, -TWO_PI, A2, ALU.mult, ALU.add)
    B.ts("dve", T12, T22, math.pi, ALU.is_gt)
    B.stt(T22, T12, -TWO_PI, T22, ALU.mult, ALU.add)
    B.ts("dve", T12, T22, -math.pi, ALU.is_lt)
    B.stt(T22, T12, TWO_PI, T22, ALU.mult, ALU.add)
    B.ts("dve", T22, T22, 3.1415925, ALU.min, -3.1415925, ALU.max)
    PTAB = B.new(F32, [4, 96])
    B.act(PTAB.rearrange("p a b -> p (a b)"), T22, AF.Sin)
    B.ts("dve", PTAB.rearrange("p a b -> p (a b)"), PTAB.rearrange("p a b -> p (a b)"), FLG[:, 0:1], ALU.mult)
    XS = [B.new(F32, [D]) for _ in range(2)]
    for tc in range(16):
        xs = XS[tc % 2]
        B.dma("sp", xs, xin[tc * 128:(tc + 1) * 128, :])
        for half in range(2):
            bk = 4 + (2 * tc + half) % 4
            pb = B.bank(bk)
            for j in range(4):
                blk = half * 4 + j
                B.tr(pb[:, j * 128:(j + 1) * 128], xs[:, blk * 128:(blk + 1) * 128], IDF)
            for j in range(4):
                blk = half * 4 + j
                src = pb[:, j * 128:(j + 1) * 128].rearrange("p (r c) -> p r c", c=64)
                dst = XR[:, blk, HALO + tc * 128:HALO + (tc + 1) * 128].rearrange("p (r c) -> p r c", c=64)
                if blk < 4:
                    pv = PTAB[:, blk, 2 * tc:2 * tc + 2].unsqueeze(2).broadcast_to([128, 2, 64])
                else:
                    pv = PTAB[:, blk - 4, 32:96].unsqueeze(1).broadcast_to([128, 2, 64])
                B.tt("dve", dst, src, pv, ALU.add)
    B.dma("sp", CV, cvec)
    B.act(CVB, CV, AF.Silu)

    def mod_piece(l, cb2):
        if cb2 == 0:
            B.dma("sp", PPB, pp[l, :, PP_BADA:PP_BADA + 48])
        pb = B.bank(6)[:, 510:512]
        wv = B.wnext(I["w_ada"][l, :, cb2 * 256:(cb2 + 1) * 256], 8, 256)
        for j in range(2):
            for k in range(8):
                B.mm(pb[:, j:j + 1], wv[:, k, j * 128:(j + 1) * 128], CVB[:, k:k + 1], start=(k == 0), stop=(k == 7))
        B.tt("dve", MODV[:, l, 2 * cb2:2 * cb2 + 2], pb, PPB[:, 2 * cb2:2 * cb2 + 2], ALU.add)

    for cb2 in range(24):
        mod_piece(0, cb2)
    B.top = base_top

    def modnorm(l, which, tau, mask_halo, fix_left=False):
        m0 = B.top
        SQ = B.new(BF16, [8, SEGP])
        LNV = B.new(F32, [SEGP])
        RSTD = B.new(F32, [SEGP])
        TMP = [B.new(F32, [SEGP]) for _ in range(2)]
        sh0 = 0 if which == 0 else 24
        for s in range(2):
            t0 = tau * TT + s * SEG
            xw = XR[:, :, t0:t0 + SEGP]
            B.act(SQ, xw, AF.Square)
            pb = B.bank(4 + s)
            for k in range(8):
                B.mm(pb[:, 0:SEGP], ONEB, SQ[:, k, :], start=(k == 0), stop=(k == 7))
            B.act(LNV, pb[:, 0:SEGP], AF.Ln, bias=EPS_AP, scale=1.0 / D)
            B.act(RSTD, LNV, AF.Exp, scale=-0.5)
            for k in range(8):
                tmp = TMP[k % 2]
                B.stt(tmp, XR[:, k, t0:t0 + SEGP], GS[:, l, which, k:k + 1], RSTD, ALU.mult, ALU.mult)
                B.act(HT[:, k, s, :], tmp, AF.Identity, bias=MODV[:, l, sh0 + k:sh0 + k + 1])
            if mask_halo:
                sg = 2 * tau + s
                B.ts("dve", HT[:, :, s, 0:HALO], HT[:, :, s, 0:HALO], mcol(sg), ALU.mult)
                B.ts("dve", HT[:, :, s, SEGP - HALO:SEGP], HT[:, :, s, SEGP - HALO:SEGP], mcol(sg + 1), ALU.mult)
        if fix_left:
            if tau >= 1:
                B.ts("dve", HT[:, :, 0, 0:HALO], HSAVE, mcol(2 * tau), ALU.mult)
            B.copy("dve", HSAVE, HT[:, :, 1, SEG:SEG + HALO])
        B.top = m0

    def layer_params(l):
        B.dma("sp", PP, pp[l, :, 0:NPPS])
        for which in range(2):
            sc0 = 8 if which == 0 else 32
            g0 = PP_G1 if which == 0 else PP_G2
            B.stt(GS[:, l, which, :], MODV[:, l, sc0:sc0 + 8], 1.0, PP[:, g0:g0 + 8], ALU.add, ALU.mult)
        B.act(NEGA, PP[:, PP_ALOG:PP_ALOG + 64], AF.Exp)
        B.ts("dve", NEGA, NEGA, -1.0, ALU.mult)
        m0 = B.top
        WSN = B.new(BF16, [4, 128])
        B.dma("pool", WSN, I["w_spatial"][l].rearrange("g i j -> i g j"))
        pb = B.bank(6, BF16)
        for g in range(4):
            B.tr(pb[:, g * 128:(g + 1) * 128], WSN[:, g, :], IDB)
        B.copy("dve", WST.rearrange("p a b -> p (a b)"), pb[:, 0:512])
        B.top = m0

    def load_state(l, src):
        m0 = B.top
        ST = [B.new(F32, [128]) for _ in range(2)]
        for blk in range(16):
            st = ST[blk % 2]
            B.dma("sp", st, src[l, blk * 128:(blk + 1) * 128, :])
            pb = B.bank(6 + blk % 2)
            B.tr(pb[:, 0:128], st, IDF)
            B.copy("act", SF[:, blk * 128:(blk + 1) * 128], pb[:, 0:128])
        B.copy("act", SBF, SF)
        B.top = m0

    v3 = lambda a: a.rearrange("p (c h) -> p c h", h=32)
    h64 = lambda a: a.rearrange("p (h x) -> p h x", x=64)

    def sweep(l, d):
        tiles = range(NTILE) if d == 0 else range(NTILE - 1, -1, -1)
        modq = [0]
        load_state(l, s0f if d == 0 else s0b)
        for tau in tiles:
            if d == 1:
                modnorm(l, 0, tau, True)
            tile_top = B.top
            if d == 0:
                YN_off = B.alloc(16 * TT * 2)
                YN = B.view(YN_off, BF16, [16, TT])
            pos_m = B.top
            XT = B.new(BF16, [16, TT])
            BT = B.new(BF16, [4, TT])
            CT = B.new(BF16, [4, TT])
            ACC = [B.new(F32, [SEG]) for _ in range(8)] if d == 1 else None
            CW0 = PP_CW
            ai = 0
            if d == 0:
                B.dma("sp", HT.rearrange("p a b c -> p (a b c)"), fes[tau, :, 24 * 512:NFE])
                B.dma("sp", XT.rearrange("p a b -> p (a b)"), fes[tau, :, 0:16 * 512])
                B.dma("sp", BT.rearrange("p a b -> p (a b)"), fes[tau, :, 16 * 512:20 * 512])
                B.dma("sp", CT.rearrange("p a b -> p (a b)"), fes[tau, :, 20 * 512:24 * 512])
            pend_silu = []

            def flush_silu():
                while pend_silu:
                    for (cb, s, pb, acc) in pend_silu.pop(0):
                        if cb < 16:
                            dst = XT[:, cb, s * SEG:(s + 1) * SEG]
                        elif cb < 20:
                            dst = BT[:, cb - 16, s * SEG:(s + 1) * SEG]
                        else:
                            dst = CT[:, cb - 20, s * SEG:(s + 1) * SEG]
                        B.act(dst, acc, AF.Silu)

            for cb2 in (range(12) if d == 1 else ()):
                wv = B.wnext(w_in[l, :, C_X + cb2 * 256:C_X + (cb2 + 1) * 256], 8, 256)
                o0 = HALO - 2
                ch = []
                for j in range(2):
                    cb = cb2 * 2 + j
                    for s in range(2):
                        pb = B.bank((cb2 % 2) * 4 + j * 2 + s)
                        for k in range(8):
                            B.mm(pb[:, 0:SEGP], wv[:, k, j * 128:(j + 1) * 128], HT[:, k, s, :],
                                 start=(k == 0), stop=(k == 7))
                        ch.append((cb, s, pb, ACC[(cb2 % 2) * 4 + j * 2 + s]))
                for (cb, s, pb, acc) in ch:
                    B.act(acc, pb[:, o0:o0 + SEG], AF.Identity, bias=PP[:, PP_CB + cb:PP_CB + cb + 1],
                          scale=PP[:, CW0 + cb:CW0 + cb + 1])
                flush_silu()
                for tap in range(1, 5):
                    for (cb, s, pb, acc) in ch:
                        B.stt(acc, pb[:, o0 + tap:o0 + tap + SEG],
                              PP[:, CW0 + tap * 24 + cb:CW0 + tap * 24 + cb + 1], acc, ALU.mult, ALU.add)
                pend_silu.append(ch)
                if cb2 == 11:
                    flush_silu()
            if d == 1:
                B.dma("sp", fes[tau, :, 24 * 512:NFE], HT.rearrange("p a b c -> p (a b c)"))
                B.dma("sp", fes[tau, :, 0:16 * 512], XT.rearrange("p a b -> p (a b)"))
                B.dma("sp", fes[tau, :, 16 * 512:20 * 512], BT.rearrange("p a b -> p (a b)"))
                B.dma("sp", fes[tau, :, 20 * 512:24 * 512], CT.rearrange("p a b -> p (a b)"))
            DTP = B.new(F32, [128])
            DT_ = B.new(F32, [128])
            DLA = B.new(F32, [128])
            ACU = B.new(F32, [128])
            TOT = B.new(F32, [128])
            WV = B.new(F32, [128])
            DEC = B.new(F32, [128])
            HI = B.new(BF16, [128])
            LO = B.new(BF16, [128])
            wdt = B.wnext(w_in[l, :, C_DT + 32 * d:C_DT + 32 * d + 32], 8, 32)
            pb = B.bank(6)
            for c in range(4):
                s, c0 = c // 2, HALO + 128 * (c % 2)
                for k in range(8):
                    B.mm(pb[:, c * 32:(c + 1) * 32], HT[:, k, s, c0:c0 + 128], wdt[:, k, :],
                         start=(k == 0), stop=(k == 7))
            bb = PP[:, PP_DTB + 32 * d:PP_DTB + 32 * d + 32].unsqueeze(1).broadcast_to([128, 4, 32])
            na = NEGA[:, 32 * d:32 * d + 32].unsqueeze(1).broadcast_to([128, 4, 32])
            B.tt("dve", v3(DTP), v3(pb[:, 0:128]), bb, ALU.add)
            B.act(DTP, DTP, AF.Exp)
            B.act(DT_, DTP, AF.Ln, bias=ONE_AP, scale=1.0)
            B.tt("dve", v3(DLA), v3(DT_), na, ALU.mult)
            pb = B.bank(7)
            B.mm(pb[:, 0:128], TRI_F, DLA)
            B.mm(pb[:, 128:256], ONEF, DLA)
            B.copy("act", ACU, pb[:, 0:128])
            B.copy("act", TOT, pb[:, 128:256])
            B.act(DEC, TOT, AF.Exp)
            if d == 0:
                tri_b, msk = TRIB_I, MSK_F
            else:
                B.tt("dve", ACU, DLA, ACU, ALU.subtract)
                B.tt("dve", ACU, ACU, TOT, ALU.add)
                tri_b, msk = CONB[:, 3, :], MSK_B
            B.tt("dve", WV, TOT, ACU, ALU.subtract)
            B.act(WV, WV, AF.Exp)
            B.tt("dve", WV, WV, DT_, ALU.mult)
            B.copy("dve", HI, DLA)
            B.tt("dve", LO, DLA, HI, ALU.subtract)
            CLB = ACU
            BK = B.new(BF16, [4, 512])
            for cp in range(2):
                pbb = B.bank(4 + cp, BF16)
                for c in (2 * cp, 2 * cp + 1):
                    for g in range(4):
                        o_ = ((c % 2) * 4 + g) * 128
                        B.tr(pbb[:, o_:o_ + 128], BT[:, g, c * 128:(c + 1) * 128], IDB)
                B.copy("act", BK[:, 2 * cp:2 * cp + 2, :].rearrange("p a b -> p (a b)"), pbb[:, 0:1024])
            xkc_off = B.alloc(2 * 4096)
            XKC = [B.view(xkc_off + i * 4096, BF16, [2048]) for i in range(2)]
            YG = B.view(xkc_off, F32, [4, TT]) if d == 0 else B.new(F32, [4, TT])
            CBM = [B.new(BF16, [128]) for _ in range(2)]
            SD4 = [B.new(F32, [512]) for _ in range(2)]
            SDB = [B.new(BF16, [512]) for _ in range(2)]
            ER4 = [B.new(BF16, [512]) for _ in range(2)]
            MR = [B.new(BF16, [4, 128]) for _ in range(4)]
            XDT = [B.new(BF16, [512]) for _ in range(2)]
            r3_off = B.alloc(8192)
            MR += [B.view(r3_off + i * 1024, BF16, [4, 128]) for i in range(2)]
            XDT.append(B.view(r3_off + 2048, BF16, [512]))
            CSR = [B.new(BF16, [4, 128]) for _ in range(4)]
            XW = [B.new(BF16, [512]) for _ in range(2)]
            CSR += [B.view(r3_off + 3072 + i * 1024, BF16, [4, 128]) for i in range(2)]
            XW.append(B.view(r3_off + 5120, BF16, [512]))
            STG4 = B.new(F32, [512])
            if d == 0:
                YBT = [B.new(F32, [TT]) for _ in range(2)]
                ZS2 = [B.view(r3_off + i * 2048, F32, [TT]) for i in range(2)]
                SQG2 = [B.view(r3_off + 4096 + i * 1024, BF16, [TT]) for i in range(4)]
                LNG = B.new(F32, [TT])
            corder = list(range(4)) if d == 0 else [3, 2, 1, 0]
            qd = [0]

            def stageA(g, c, it):
                cols = slice(c * 128, (c + 1) * 128)
                pcb = B.bank(6)
                cbo = (it % 2) * 128
                B.mm(pcb[:, cbo:cbo + 128], BT[:, g, cols], CT[:, g, cols])
                cbm = CBM[it % 2]
                B.tt("dve", cbm, pcb[:, cbo:cbo + 128], msk, ALU.mult)
                B.tt("pool", h64(XW[it % 3]), h64(XKC[c % 2][:, g * 512:(g + 1) * 512]),
                     WV[:, c * 32 + 8 * g:c * 32 + 8 * g + 8].unsqueeze(2).broadcast_to([128, 8, 64]), ALU.mult)
                B.tt("pool", h64(XDT[it % 3]), h64(XKC[c % 2][:, g * 512:(g + 1) * 512]),
                     DT_[:, c * 32 + 8 * g:c * 32 + 8 * g + 8].unsqueeze(2).broadcast_to([128, 8, 64]), ALU.mult)
                for quad in range(2):
                    pab = B.bank((it % 2) * 2 + quad)
                    sd = SD4[qd[0] % 2]
                    er = ER4[qd[0] % 2]
                    qd[0] += 1
                    cs = CSR[(it % 3) * 2 + quad]
                    for e4 in range(4):
                        q = c * 32 + 8 * g + quad * 4 + e4
                        ab = pab[:, e4 * 128:(e4 + 1) * 128]
                        B.mm(ab, HI[:, q:q + 1].broadcast_to([128, 128]), tri_b, start=True, stop=False)
                        B.mm(ab, LO[:, q:q + 1].broadcast_to([128, 128]), tri_b, start=False, stop=True)
                    for e4 in range(4):
                        q = c * 32 + 8 * g + quad * 4 + e4
                        ab = pab[:, e4 * 128:(e4 + 1) * 128]
                        B.act(sd[:, e4 * 128:(e4 + 1) * 128], ab, AF.Relu, bias=CLB[:, q:q + 1], scale=-1.0)
                    sdb = SDB[(qd[0] - 1) % 2]
                    B.act(sdb, sd, AF.Exp, scale=-1.0)
                    B.act(er, pab, AF.Exp)
                    B.tt("dve", MR[(it % 3) * 2 + quad], sdb.rearrange("p (a b) -> p a b", b=128),
                         cbm.unsqueeze(1).broadcast_to([128, 4, 128]), ALU.mult)
                    B.tt("dve", cs, CT[:, g, cols].unsqueeze(1).broadcast_to([128, 4, 128]),
                         er.rearrange("p (a b) -> p a b", b=128), ALU.mult)

            def stageB(g, c, it):
                cols = slice(c * 128, (c + 1) * 128)
                py4 = B.bank(4 + it % 2)
                for hp in range(4):
                    py = py4[:, hp * 128:(hp + 1) * 128]
                    for e in range(2):
                        e8 = 2 * hp + e
                        h = 8 * g + e8
                        B.mm(py[e * 64:(e + 1) * 64, :], XDT[it % 3][:, e8 * 64:(e8 + 1) * 64],
                             MR[(it % 3) * 2 + e8 // 4][:, e8 % 4, :], start=True, stop=False)
                        B.mm(py[e * 64:(e + 1) * 64, :], SBF[:, h * 64:(h + 1) * 64],
                             CSR[(it % 3) * 2 + e8 // 4][:, e8 % 4, :], start=False, stop=True)
                if d == 0:
                    for hp in range(4):
                        blk = 4 * g + hp
                        B.stt(YN[:, blk, cols], XT[:, blk, cols], PP[:, PP_DSK + blk:PP_DSK + blk + 1],
                              py4[:, hp * 128:(hp + 1) * 128], ALU.mult, ALU.add)
                if d == 1:
                    ygr = YG[:, :, (it % 4) * 128:(it % 4 + 1) * 128]
                    B.copy("dve", ygr, py4.rearrange("p (a b) -> p a b", b=128))
                    t_a = tau * TT + c * 128
                    B.dma("sp", ybs[4 * g:4 * g + 4, :, t_a:t_a + 128].rearrange("b p t -> p b t"), ygr)
                pl = B.bank(7)
                B.mm(pl, BK[:, c, g * 128:(g + 1) * 128], XW[it % 3])
                sg_ = SF[:, g * 512:(g + 1) * 512]
                B.tt("pool", h64(sg_), h64(sg_),
                     DEC[:, c * 32 + 8 * g:c * 32 + 8 * g + 8].unsqueeze(2).broadcast_to([128, 8, 64]), ALU.mult)
                B.tt("dve", sg_, sg_, pl, ALU.add)
                seg_end = (c % 2 == 1) if d == 0 else (c % 2 == 0)
                if seg_end:
                    sgi = 2 * tau + c // 2
                    if sgi < 4:
                        dst_t = sfo if d == 0 else sbo
                        ptb = B.bank(7)
                        for b4 in range(4):
                            B.tr(ptb[:, b4 * 128:(b4 + 1) * 128], sg_[:, b4 * 128:(b4 + 1) * 128], IDF)
                        B.copy("act", STG4, ptb)
                        r0 = g * 512
                        o = B.dma("sp", dst_t[sgi, l, r0:r0 + 512, :].rearrange("(b p) n -> p b n", p=128),
                                  STG4.rearrange("p (b n) -> p b n", n=128))
                        if o is not None:
                            B.out_dmas.append(o)
                    bnd = (sgi + 1) if d == 0 else sgi
                    B.ts("dve", sg_, sg_, mcol(bnd), ALU.mult)
                B.copy("act", SBF[:, g * 512:(g + 1) * 512], sg_)

            def xk_transposes(c):
                for half in range(2):
                    pbb = B.bank(4 + half, BF16)
                    for j in range(8):
                        B.tr(pbb[:, j * 128:(j + 1) * 128], XT[:, half * 8 + j, c * 128:(c + 1) * 128], IDB)
                    B.copy("act", XKC[c % 2][:, half * 1024:(half + 1) * 1024], pbb[:, 0:1024])

            its = [(g, c) for c in corder for g in range(4)]
            pend = []
            for n, (g, c) in enumerate(its):
                if g == 0:
                    xk_transposes(c)
                stageA(g, c, n)
                pend.append((g, c, n))
                if len(pend) > 2:
                    stageB(*pend.pop(0))
                if d == 1 and l + 1 < nl and modq[0] < 24:
                    mod_piece(l + 1, modq[0])
                    modq[0] += 1
            while pend:
                stageB(*pend.pop(0))
            if d == 0:
                gate_tail = []
                for g in range(4):
                    pss = B.bank(6)
                    pzs = []
                    for hp2 in range(2):
                        wv = B.wnext(w_in[l, :, C_Z + g * 512 + hp2 * 256:C_Z + g * 512 + (hp2 + 1) * 256], 8, 256)
                        for j in range(2):
                            hp = hp2 * 2 + j
                            pz = B.bank(mmbank())
                            for s in range(2):
                                for k in range(8):
                                    B.mm(pz[:, s * SEG:(s + 1) * SEG], wv[:, k, j * 128:(j + 1) * 128],
                                         HT[:, k, s, HALO:HALO + SEG], start=(k == 0), stop=(k == 7))
                            pzs.append(pz)
                    if gate_tail:
                        gate_tail.pop(0)()
                    for hp in range(4):
                        blk = 4 * g + hp
                        ybt = YBT[hp % 2]
                        B.dma("sp", ybt, ybs[blk, :, tau * TT:(tau + 1) * TT])
                        B.tt("dve", YG[:, hp, :], YN[:, blk, :], ybt, ALU.add)
                    B.act(ZS2[0], pzs[0], AF.Silu)
                    B.act(ZS2[1], pzs[1], AF.Silu)
                    for hp in range(4):
                        B.tt("dve", YG[:, hp, :], YG[:, hp, :], ZS2[hp % 2], ALU.mult)
                        if hp + 2 < 4:
                            B.act(ZS2[hp % 2], pzs[hp + 2], AF.Silu)
                        B.act(SQG2[hp], YG[:, hp, :], AF.Square)

                    def tail(g=g, pss=pss):
                        for hp in range(4):
                            B.mm(pss, ONEB, SQG2[hp], start=(hp == 0), stop=(hp == 3))
                        B.act(LNG, pss, AF.Ln, bias=EPS_AP, scale=1.0 / 512)
                        B.act(LNG, LNG, AF.Exp, scale=-0.5)
                        for hp in range(4):
                            blk = 4 * g + hp
                            B.stt(YN[:, blk, :], YG[:, hp, :], PP[:, PP_GSSD + blk:PP_GSSD + blk + 1], LNG,
                                  ALU.mult, ALU.mult)
                    gate_tail.append(tail)
                while gate_tail:
                    gate_tail.pop(0)()
            if d == 1:
                B.top = tile_top
                continue
            B.top = pos_m
            MRG = B.new(F32, [8, TT])
            MB = B.new(BF16, [8, TT])
            GA = [B.new(BF16, [TT]) for _ in range(2)]
            TMPF = [B.new(F32, [TT]) for _ in range(2)]
            br_top = B.top
            gate_w = {}

            def branch_out(wsrc, K, rhs_of, goff, mode):
                for jo in range(8):
                    if jo % 2 == 0:
                        gate_w[0] = B.wnext(w_in[l, :, goff + jo * 128:goff + (jo + 2) * 128], 8, 256)
                    wvg = gate_w[0]
                    pg = B.bank(mmbank())
                    for s in range(2):
                        for k in range(8):
                            B.mm(pg[:, s * SEG:(s + 1) * SEG], wvg[:, k, (jo % 2) * 128:(jo % 2) * 128 + 128],
                                 HT[:, k, s, HALO:HALO + SEG], start=(k == 0), stop=(k == 7))
                    ga = GA[jo % 2]
                    B.act(ga, pg, AF.Sigmoid)
                    pbr = B.bank(mmbank())
                    for kh in range((K + 15) // 16):
                        k0 = kh * 16
                        kk = min(16, K - k0)
                        wv = B.wnext(wsrc[k0 * 128:(k0 + kk) * 128, jo * 128:(jo + 1) * 128], kk, 128)
                        for k in range(kk):
                            B.mm(pbr, wv[:, k, :], rhs_of(k0 + k), start=(k0 + k == 0), stop=(k0 + k == K - 1))
                    if mode == 0:
                        B.tt("dve", MRG[:, jo, :], pbr, ga, ALU.mult)
                    else:
                        tf = TMPF[jo % 2]
                        B.tt("dve", tf, pbr, ga, ALU.mult)
                        B.tt("dve", MRG[:, jo, :] if mode == 1 else MB[:, jo, :], MRG[:, jo, :], tf, ALU.add)

            branch_out(I["w_br_ssd"][l], 16, lambda k: YN[:, k, :], C_G, 0)
            VW = B.view(YN_off, BF16, [8, 1024])
            for q4 in range(4):
                B.dma("pool", VW[:, :, q4 * 256:(q4 + 1) * 256],
                      w_in[l, :, C_V + q4 * 256:C_V + (q4 + 1) * 256].rearrange("(k p) c -> p k c", p=128))
            VI = B.new(F32, [2, SEGP])
            CN = [B.new(F32, [SEGP]) for _ in range(4)]
            RC = B.new(F32, [4, 2, SEG])
            for s in range(2):
                sg = 2 * tau + s
                B.memset("dve", VI[:, s, :], 1.0)
                B.ts("dve", VI[:, s, 0:HALO], VI[:, s, 0:HALO], mcol(sg), ALU.mult)
                B.ts("dve", VI[:, s, SEGP - HALO:SEGP], VI[:, s, SEGP - HALO:SEGP], mcol(sg + 1), ALU.mult)
                B.tt("dve", CN[0][:, 1:SEGP], VI[:, s, 0:SEGP - 1], VI[:, s, 1:SEGP], ALU.add)
                B.tt("dve", CN[1][:, 2:SEGP - 1], CN[0][:, 1:SEGP - 2], CN[0][:, 3:SEGP], ALU.add)
                B.tt("dve", CN[2][:, 4:SEGP - 3], CN[1][:, 2:SEGP - 5], CN[1][:, 6:SEGP - 1], ALU.add)
                B.tt("dve", CN[3][:, 8:SEGP - 7], CN[2][:, 4:SEGP - 11], CN[2][:, 12:SEGP - 3], ALU.add)
                for wi in range(4):
                    S.op("dve", (lambda o_, i_: (lambda e: e.reciprocal(out=o_, in_=i_)))(
                        RC[:, wi, s, :], CN[wi][:, HALO:HALO + SEG]),
                        reads=[CN[wi][:, HALO:HALO + SEG]], writes=[RC[:, wi, s, :]])
            PL = B.new(BF16, [8, TT])
            P0 = [B.new(F32, [SEGP]) for _ in range(2)]
            SM = [B.new(F32, [SEGP]) for _ in range(4)]
            pi_ = 0
            for cb2 in range(4):
                wv = B.wnext(w_in[l, :, C_P + cb2 * 256:C_P + (cb2 + 1) * 256], 8, 256)
                for j in range(2):
                    cb = cb2 * 2 + j
                    wi = cb // 2
                    for s in range(2):
                        pb = B.bank(mmbank())
                        for k in range(8):
                            B.mm(pb[:, 0:SEGP], wv[:, k, j * 128:(j + 1) * 128], HT[:, k, s, :],
                                 start=(k == 0), stop=(k == 7))
                        p0 = P0[pi_ % 2]
                        pi_ += 1
                        B.copy("act", p0, pb[:, 0:SEGP])
                        B.tt("dve", SM[0][:, 1:SEGP], p0[:, 0:SEGP - 1], p0[:, 1:SEGP], ALU.add)
                        if wi >= 1:
                            B.tt("dve", SM[1][:, 2:SEGP - 1], SM[0][:, 1:SEGP - 2], SM[0][:, 3:SEGP], ALU.add)
                        if wi >= 2:
                            B.tt("dve", SM[2][:, 4:SEGP - 3], SM[1][:, 2:SEGP - 5], SM[1][:, 6:SEGP - 1], ALU.add)
                        if wi >= 3:
                            B.tt("dve", SM[3][:, 8:SEGP - 7], SM[2][:, 4:SEGP - 11], SM[2][:, 12:SEGP - 3], ALU.add)
                        sm = SM[wi]
                        B.tt("dve", sm[:, HALO:HALO + SEG], sm[:, HALO:HALO + SEG], RC[:, wi, s, :], ALU.mult)
                        B.tt("dve", PL[:, cb, s * SEG:(s + 1) * SEG], sm[:, HALO:HALO + SEG],
                             p0[:, HALO:HALO + SEG], ALU.subtract)
            YP = B.new(BF16, [8, TT])
            wpv = B.wnext(I["w_pool"][l].rearrange("g r c -> (g r) c"), 8, 256)
            for g4 in range(4):
                for ob in range(2):
                    pb = B.bank(mmbank())
                    for kb in range(2):
                        B.mm(pb, wpv[:, g4 * 2 + kb, ob * 128:(ob + 1) * 128], PL[:, 2 * g4 + kb, :],
                             start=(kb == 0), stop=(kb == 1))
                    blk = 2 * g4 + ob
                    B.act(YP[:, blk, :], pb, AF.Identity, scale=PP[:, PP_PSC + blk:PP_PSC + blk + 1])
            branch_out(I["w_br_pool"][l], 8, lambda k: YP[:, k, :], C_G + 1024, 1)
            B.top = br_top
            GSGU = B.new(F32, [1024])
            BSP = B.new(F32, [512])
            B.dma("sp", GSGU, pp[l, :, PP_GSGU:PP_GSGU + 1024])
            B.dma("sp", BSP, pp[l, :, PP_BSP:PP_BSP + 512])
            VN = B.new(BF16, [4, 1024])
            UT = B.new(BF16, [8, TT])
            YM = B.new(BF16, [8, TT])
            SSV = B.new(F32, [8])
            JNK = B.new(BF16, [1024])
            for c in range(4):
                s, c0 = c // 2, HALO + 128 * (c % 2)
                b0 = (c % 2) * 2
                for q4 in range(4):
                    pb = B.bank(b0 + q4 // 2)
                    for k in range(8):
                        B.mm(pb[:, (q4 % 2) * 256:(q4 % 2) * 256 + 256], HT[:, k, s, c0:c0 + 128],
                             VW[:, k, q4 * 256:(q4 + 1) * 256], start=(k == 0), stop=(k == 7))
                pv = B.psum[:, b0 * 512:(b0 + 2) * 512]
                B.act(JNK, pv, AF.Square, accum_out=SSV[:, c:c + 1])
                B.act(SSV[:, 4 + c:5 + c], SSV[:, c:c + 1], AF.Ln, bias=EPS_AP, scale=1.0 / D)
                B.act(SSV[:, 4 + c:5 + c], SSV[:, 4 + c:5 + c], AF.Exp, scale=-0.5)
                B.stt(VN[:, c, :], pv, SSV[:, 4 + c:5 + c], GSGU, ALU.mult, ALU.mult)
            for cb2 in range(4):
                wv = B.wnext(w_in[l, :, C_U + cb2 * 256:C_U + (cb2 + 1) * 256], 8, 256)
                for j in range(2):
                    cb = cb2 * 2 + j
                    pu = B.bank(mmbank())
                    for s in range(2):
                        for k in range(8):
                            B.mm(pu[:, s * SEG:(s + 1) * SEG], wv[:, k, j * 128:(j + 1) * 128],
                                 HT[:, k, s, HALO:HALO + SEG], start=(k == 0), stop=(k == 7))
                    B.copy("act", UT[:, cb, :], pu)
                    g4 = cb // 2
                    psv = B.bank(mmbank())
                    for c in range(4):
                        B.mm(psv[:, c * 128:(c + 1) * 128], VN[:, c, cb * 128:(cb + 1) * 128], WST[:, g4, :])
                    tf = TMPF[cb % 2]
                    B.tt("dve", tf.rearrange("p (c i) -> p c i", i=128), psv.rearrange("p (c i) -> p c i", i=128),
                         BSP[:, g4 * 128:(g4 + 1) * 128].unsqueeze(1).broadcast_to([128, 4, 128]), ALU.add)
                    B.tt("dve", YM[:, cb, :], tf, UT[:, cb, :], ALU.mult)
            branch_out(I["w_br_gmlp"][l], 8, lambda k: YM[:, k, :], C_G + 2048, 2)
            tcol = slice(HALO + tau * TT, HALO + (tau + 1) * TT)
            for jo in range(8):
                wv = B.wnext(I["w_out"][l][:, jo * 128:(jo + 1) * 128], 8, 128)
                po = B.bank(mmbank())
                for k in range(8):
                    B.mm(po, wv[:, k, :], MB[:, k, :], start=(k == 0), stop=(k == 7))
                B.stt(XR[:, jo, tcol], po, MODV[:, l, 16 + jo:17 + jo], XR[:, jo, tcol], ALU.mult, ALU.add)
            B.top = tile_top
            modnorm(l, 1, tau, False)
            HID = B.new(BF16, [22, TT])
            SGT = [B.new(F32, [TT]) for _ in range(2)]
            for j2 in range(11):
                wg = B.wnext(I["w_ffn_in"][l][:, j2 * 256:(j2 + 1) * 256], 8, 256)
                wu = B.wnext(I["w_ffn_in"][l][:, DFF + j2 * 256:DFF + (j2 + 1) * 256], 8, 256)
                for j in range(2):
                    jj = j2 * 2 + j
                    pg = B.bank(mmbank())
                    pu = B.bank(mmbank())
                    for (pp_, wv) in ((pg, wg), (pu, wu)):
                        for s in range(2):
                            for k in range(8):
                                B.mm(pp_[:, s * SEG:(s + 1) * SEG], wv[:, k, j * 128:(j + 1) * 128],
                                     HT[:, k, s, HALO:HALO + SEG], start=(k == 0), stop=(k == 7))
                    sg_ = SGT[jj % 2]
                    B.act(sg_, pg, AF.Silu)
                    B.tt("dve", HID[:, jj, :], pu, sg_, ALU.mult)
            for jo in range(8):
                po = B.bank(mmbank())
                for kh in range(2):
                    wv = B.wnext(I["w_ffn_out"][l][kh * 11 * 128:(kh + 1) * 11 * 128, jo * 128:(jo + 1) * 128], 11, 128)
                    for k in range(11):
                        B.mm(po, wv[:, k, :], HID[:, kh * 11 + k, :], start=(kh == 0 and k == 0),
                             stop=(kh == 1 and k == 10))
                B.stt(XR[:, jo, tcol], po, MODV[:, l, 40 + jo:41 + jo], XR[:, jo, tcol], ALU.mult, ALU.add)
            B.top = tile_top

    for l in range(nl):
        layer_params(l)
        sweep(l, 1)
        sweep(l, 0)

    GF = B.new(F32, [D])
    B.dma("sp", GF, gfin)
    OS = [B.new(F32, [D]) for _ in range(2)]
    SSO = B.new(F32, [32])
    JN2 = B.new(BF16, [D])
    for tc in range(16):
        b0 = (tc % 2) * 2
        for blk in range(8):
            pb = B.bank(b0 + blk // 4)
            B.tr(pb[:, (blk % 4) * 128:(blk % 4) * 128 + 128], XR[:, blk, HALO + tc * 128:HALO + (tc + 1) * 128], IDF)
        pv = B.psum[:, b0 * 512:(b0 + 2) * 512]
        B.act(JN2, pv, AF.Square, accum_out=SSO[:, tc:tc + 1])
        B.act(SSO[:, 16 + tc:17 + tc], SSO[:, tc:tc + 1], AF.Ln, bias=EPS_AP, scale=1.0 / D)
        B.act(SSO[:, 16 + tc:17 + tc], SSO[:, 16 + tc:17 + tc], AF.Exp, scale=-0.5)
        osb = OS[tc % 2]
        B.stt(osb, pv, SSO[:, 16 + tc:17 + tc], GF, ALU.mult, ALU.mult)
        o = B.dma("sp", yout[tc * 128:(tc + 1) * 128, :], osb)
        if o is not None:
            B.out_dmas.append(o)


_CACHE = {}


def _consts():
    j = np.arange(128)[:, None]
    i = np.arange(128)[None, :]
    return np.stack([np.eye(128), (j <= i), (j < i), (j >= i)]).astype(np.float32)


def _pack_pp(inp, l):
    f = np.float32
    pm = lambda v: np.ascontiguousarray(np.asarray(v, f).reshape(-1, 128).T)
    cols = [pm(inp["g_norm1"][l]), pm(inp["g_norm2"][l]), pm(inp["b_ada"][l])]
    cw = np.asarray(inp["conv_w"][l], f)[:, 2048 - 2048:]
    cols.append(np.concatenate([pm(cw[t]) for t in range(5)], axis=1))
    cols.append(pm(inp["conv_b"][l]))
    cols.append(pm(np.repeat(np.asarray(inp["d_skip"][l], f), 64)))
    cols.append(pm(inp["g_ssd"][l]))
    cols.append(pm(inp["pool_scale"][l]))
    cols.append(np.broadcast_to(np.asarray(inp["dt_bias"][l], f).reshape(1, 64), (128, 64)))
    cols.append(np.broadcast_to(np.asarray(inp["a_log"][l], f).reshape(1, 64), (128, 64)))
    cols.append(np.broadcast_to(np.asarray(inp["g_sgu"][l], f).reshape(1, 1024), (128, 1024)))
    cols.append(np.broadcast_to(np.asarray(inp["b_spatial"][l], f).reshape(1, 512), (128, 512)))
    out = np.concatenate(cols, axis=1).astype(f)
    assert out.shape == (128, NPP), out.shape
    return out


def kernel(nl=DEPTH, **inp):
    f = np.float32
    if nl not in _CACHE:
        _CACHE[nl] = build_program(nl)
    nc = _CACHE[nl]
    xp = np.asarray(inp["x_prompt"], f)
    xs = np.asarray(inp["x_sample"], f)
    NLW = max(nl, 1)
    pp = np.stack([_pack_pp(inp, l) for l in range(NLW)])
    shared = {
        "in_consts": _consts(),
        "in_pp": pp,
        "in_gfin": np.ascontiguousarray(np.broadcast_to(np.asarray(inp["g_final"], f)[None, :], (128, D))),
    }
    for k in ("w_ada", "w_in", "w_br_ssd", "w_pool", "w_br_pool", "w_spatial", "w_br_gmlp", "w_out",
              "w_ffn_in", "w_ffn_out"):
        shared["in_" + k] = np.ascontiguousarray(np.asarray(inp[k], f)[:NLW])
    in_maps = []
    zst = np.zeros((NLW, NH * 64, 128), f)
    for core in range(8):
        m = dict(shared)
        fl = np.zeros((128, 16), f)
        if core < 4:
            m["in_xin"] = np.ascontiguousarray(xs[core])
            m["in_s0f"] = np.ascontiguousarray(np.asarray(inp["state_ssd_fwd"], f)[core, :NLW].reshape(NLW, NH * 64, 128))
            m["in_s0b"] = np.ascontiguousarray(np.asarray(inp["state_ssd_bwd"], f)[core, :NLW].reshape(NLW, NH * 64, 128))
            cv = np.asarray(inp["c"], f)[core]
            fl[:, 0] = 1.0
            fl[:, 2:9] = 1.0
        else:
            q = core - 4
            xx = np.zeros((T, D), f)
            xx[:1024] = xp[4 * q:4 * q + 4].reshape(1024, D)
            m["in_xin"] = xx
            m["in_s0f"] = zst
            m["in_s0b"] = zst
            cv = np.asarray(inp["c_ctx"], f)
        m["in_cvec"] = np.ascontiguousarray(cv.reshape(8, 128).T)
        m["in_flags"] = fl
        in_maps.append(m)
    res = run_bass_kernel_spmd(nc, in_maps, core_ids=list(range(8)))
    R = res.results
    y_sample = np.stack([R[c]["yout"] for c in range(4)]).astype(f)
    y_prompt = np.concatenate([R[4 + q]["yout"][:1024].reshape(4, 256, D) for q in range(4)]).astype(f)
    nsf = np.concatenate([R[4 + q]["sfo"].reshape(4, NLW, NH, 64, 128) for q in range(4)]).astype(f)
    nsb = np.concatenate([R[4 + q]["sbo"].reshape(4, NLW, NH, 64, 128) for q in range(4)]).astype(f)
    return (y_prompt, y_sample, nsf, nsb)
```

```python
import math
from contextlib import ExitStack
import numpy as np
import concourse.bass as bass
import concourse.mybir as mybir
from concourse.bass_utils import run_bass_kernel_spmd

F32 = mybir.dt.float32
BF16 = mybir.dt.bfloat16
I32 = mybir.dt.int32
AF = mybir.ActivationFunctionType
ALU = mybir.AluOpType

D = 1024
DEPTH = 4
T = 2048
NTILE = 4
TT = 512
SEG = 256
HALO = 8
SEGP = SEG + 2 * HALO
XRW = T + 2 * HALO
NH = 32
DFF = 2816
INC = 11328
EPS = 1e-6
C_Z, C_X, C_B, C_C, C_DT, C_P, C_U, C_V, C_G = 0, 2048, 4096, 4608, 5120, 5184, 6208, 7232, 8256
NPP = 376 + 1024 + 512
NPPS = 376
PP_G1, PP_G2, PP_BADA, PP_CW, PP_CB, PP_DSK, PP_GSSD, PP_PSC, PP_DTB, PP_ALOG, PP_GSGU, PP_BSP = (
    0, 8, 16, 64, 184, 208, 224, 240, 248, 312, 376, 1400)
ARENA_KB = 206
CELL = 256
NSLOT = 4
SLOT_B = 4096
AHEAD = 2
NDSEM = 12
NFE = 24 * 512 + 8 * 2 * SEGP


def _esize(dt):
    return 2 if dt == BF16 else 4


class Op:
    __slots__ = ("stream", "dom", "idx", "fn", "waits", "signaled", "is_dma", "count")


class Sched:
    def __init__(self):
        self.streams = {k: [] for k in ("pe", "act", "dve", "pool", "sp")}
        self.domcnt = {}
        self.cells = {}
        self.waited = {k: {} for k in self.streams}
        self.dma_rr = {"sp": 0, "pool": 0}
        self.dma_last = {}
        self.nops = 0
        self.dry = False

    @staticmethod
    def region(ap):
        t = ap.tensor
        name = t.name
        es = _esize(ap.dtype)
        dims = [list(d) for d in ap.ap]
        cls = type(t).__name__
        if cls.startswith("DRam"):
            ext = sum((n - 1) * abs(s) for s, n in dims) + 1
            b0 = ap.offset * es
            return ("d:" + name, b0 // 65536, (b0 + ext * es - 1) // 65536 + 1, 0, 2)
        pstride, pn = dims[0]
        if pstride == 0:
            pstride = 1 << 40
        fd = dims[1:]
        ext = sum((n - 1) * abs(s) for s, n in fd) + 1
        p0 = ap.offset // pstride if pstride < (1 << 40) else 0
        c0 = ap.offset - p0 * pstride if pstride < (1 << 40) else ap.offset
        b0 = c0 * es
        b1 = (c0 + ext) * es
        p1 = p0 + pn
        h0 = 0 if p0 < 64 else 1
        h1 = 1 if p1 <= 64 else 2
        if cls.startswith("PS") or "psum" in name:
            return ("ps:" + name, b0 // 2048, (b1 - 1) // 2048 + 1, p0 // 32, (p1 - 1) // 32 + 1)
        return ("sb:" + name, b0 // CELL, (b1 - 1) // CELL + 1, h0, h1)

    def op(self, stream, fn, reads=(), writes=(), dma=False):
        self.nops += 1
        if self.dry:
            return None
        o = Op()
        o.stream = stream
        o.is_dma = dma
        o.fn = fn
        o.signaled = dma
        o.count = None
        if dma:
            k = self.dma_rr[stream] % NDSEM
            self.dma_rr[stream] += 1
            o.dom = (stream, k)
        else:
            o.dom = stream
        o.idx = self.domcnt.get(o.dom, 0)
        self.domcnt[o.dom] = o.idx + 1
        need = {}

        def want(w):
            if w is None:
                return
            if w.dom == "pe" and stream == "pe" and not dma:
                return
            cur = need.get(w.dom)
            if cur is None or cur.idx < w.idx:
                need[w.dom] = w
        if dma:
            want(self.dma_last.get(o.dom))
            self.dma_last[o.dom] = o
        cells = self.cells
        for ap in reads:
            if ap is None:
                continue
            sp, c0, c1, h0, h1 = self.region(ap)
            if sp.startswith("d:in_"):
                continue
            for c in range(c0, c1):
                for h in range(h0, h1):
                    rec = cells.get((sp, c, h))
                    if rec is None:
                        rec = [None, {}]
                        cells[(sp, c, h)] = rec
                    want(rec[0])
        for ap in writes:
            sp, c0, c1, h0, h1 = self.region(ap)
            for c in range(c0, c1):
                for h in range(h0, h1):
                    rec = cells.get((sp, c, h))
                    if rec is None:
                        rec = [None, {}]
                        cells[(sp, c, h)] = rec
                    want(rec[0])
                    for r in rec[1].values():
                        want(r)
        wl = []
        wd = self.waited[stream]
        for dom, w in need.items():
            if wd.get(dom, -1) >= w.idx:
                continue
            wd[dom] = w.idx
            w.signaled = True
            wl.append(w)
        o.waits = wl
        for ap in reads:
            if ap is None:
                continue
            sp, c0, c1, h0, h1 = self.region(ap)
            if sp.startswith("d:in_"):
                continue
            for c in range(c0, c1):
                for h in range(h0, h1):
                    rec = cells[(sp, c, h)]
                    cur = rec[1].get(o.dom)
                    if cur is None or cur.idx < o.idx:
                        rec[1][o.dom] = o
        for ap in writes:
            sp, c0, c1, h0, h1 = self.region(ap)
            for c in range(c0, c1):
                for h in range(h0, h1):
                    rec = cells[(sp, c, h)]
                    rec[0] = o
                    rec[1] = {}
        self.streams[stream].append(o)
        return o

    def emit(self, nc, final_waits):
        doms = set(self.domcnt.keys())
        for dom in doms:
            cnt = 0
            for st in self.streams.values():
                pass
        per_dom = {}
        for sname, ops in self.streams.items():
            for o in ops:
                per_dom.setdefault(o.dom, []).append(o)
        for dom, ops in per_dom.items():
            ops.sort(key=lambda x: x.idx)
            c = 0
            for o in ops:
                if o.signaled:
                    c += 1
                    o.count = c
        with ExitStack() as es:
            sems = {}
            for dom in per_dom:
                nm = dom if isinstance(dom, str) else "%s_d%d" % dom
                sems[dom] = es.enter_context(nc.semaphore("s_" + nm))
            block = es.enter_context(nc.Block())

            def run(sname, eng):
                for o in self.streams[sname]:
                    for w in o.waits:
                        inc = 16 if w.is_dma else 1
                        eng.wait_ge(sems[w.dom], w.count * inc)
                    ins = o.fn(eng)
                    if o.signaled:
                        ins.then_inc(sems[o.dom], 16 if o.is_dma else 1)
                if sname == "sp":
                    for o in final_waits:
                        eng.wait_ge(sems[o.dom], o.count * 16)

            @block.tensor
            def _(e):
                run("pe", e)

            @block.scalar
            def _(e):
                run("act", e)

            @block.vector
            def _(e):
                run("dve", e)

            @block.gpsimd
            def _(e):
                run("pool", e)

            @block.sync
            def _(e):
                run("sp", e)


class Builder:
    def __init__(self, nc, nl):
        self.nc = nc
        self.nl = nl
        self.S = Sched()
        self.top = 0
        self.arena = None
        self.psum = None
        self.wreq = []
        self.wi = 0
        self.wissued = 0
        self.out_dmas = []

    def alloc(self, nbytes):
        off = (self.top + CELL - 1) // CELL * CELL
        self.top = off + nbytes
        assert self.top <= ARENA_KB * 1024, ("SBUF arena overflow", self.top)
        return off

    def view(self, off, dt, shape):
        n = int(np.prod(shape))
        es = _esize(dt)
        a = self.arena[:, off // 4:(off + n * es + 3) // 4]
        if dt != F32:
            a = a.bitcast(dt)
        if len(shape) == 2:
            a = a.rearrange("p (a b) -> p a b", b=shape[1])
        elif len(shape) == 3:
            a = a.rearrange("p (a b c) -> p a b c", b=shape[1], c=shape[2])
        return a

    def new(self, dt, shape):
        return self.view(self.alloc(int(np.prod(shape)) * _esize(dt)), dt, shape)

    def bank(self, b, dt=F32):
        a = self.psum[:, b * 512:(b + 1) * 512]
        if dt != F32:
            a = a.bitcast(dt)
        return a

    def mm(self, out, lhsT, rhs, start=True, stop=True):
        rd = [lhsT, rhs] + ([] if start else [out])
        return self.S.op("pe", lambda e: e.matmul(out, lhsT=lhsT, rhs=rhs, start=start, stop=stop),
                         reads=rd, writes=[out])

    def tr(self, out, in_, ident):
        return self.S.op("pe", lambda e: e.transpose(out, in_, ident), reads=[in_, ident], writes=[out])

    def act(self, out, in_, func, bias=None, scale=1.0, accum_out=None):
        rd = [in_]
        kw = {}
        if bias is not None:
            kw["bias"] = bias
            if not isinstance(bias, (int, float)):
                rd.append(bias)
        if not isinstance(scale, (int, float)):
            rd.append(scale)
        wr = [out]
        if accum_out is not None:
            kw["accum_out"] = accum_out
            wr.append(accum_out)
        return self.S.op("act", lambda e: e.activation(out=out, in_=in_, func=func, scale=scale, **kw),
                         reads=rd, writes=wr)

    def tt(self, eng, out, in0, in1, op):
        return self.S.op(eng, lambda e: e.tensor_tensor(out=out, in0=in0, in1=in1, op=op),
                         reads=[in0, in1], writes=[out])

    def ts(self, eng, out, in0, s1, op0, s2=None, op1=None):
        rd = [in0] + [s for s in (s1, s2) if s is not None and not isinstance(s, (int, float))]
        if op1 is None:
            return self.S.op(eng, lambda e: e.tensor_scalar(out=out, in0=in0, scalar1=s1, scalar2=None, op0=op0),
                             reads=rd, writes=[out])
        return self.S.op(eng, lambda e: e.tensor_scalar(out=out, in0=in0, scalar1=s1, scalar2=s2, op0=op0, op1=op1),
                         reads=rd, writes=[out])

    def stt(self, out, in0, scalar, in1, op0, op1):
        rd = [in0, in1] + ([] if isinstance(scalar, (int, float)) else [scalar])
        return self.S.op("dve", lambda e: e.scalar_tensor_tensor(out=out, in0=in0, scalar=scalar, in1=in1,
                                                                   op0=op0, op1=op1), reads=rd, writes=[out])

    def copy(self, eng, out, in_):
        if eng == "act":
            return self.S.op("act", lambda e: e.copy(out=out, in_=in_), reads=[in_], writes=[out])
        return self.S.op(eng, lambda e: e.tensor_copy(out=out, in_=in_), reads=[in_], writes=[out])

    def memset(self, eng, ap, val):
        return self.S.op(eng, lambda e: e.memset(ap, val), reads=[], writes=[ap])

    def dma(self, stream, out, in_, slow=False):
        if slow:
            return self.S.op(stream, lambda e: e.dma_start(out=out, in_=in_, allow_slow_non_contiguous=True),
                             reads=[in_], writes=[out], dma=True)
        return self.S.op(stream, lambda e: e.dma_start(out=out, in_=in_), reads=[in_], writes=[out], dma=True)

    def wnext(self, src, K, ncols):
        assert K * ncols * 2 <= SLOT_B
        if self.S.dry:
            self.wreq.append((src, K, ncols))
            return self.view(self.wslots[0], BF16, [K, ncols])
        i = self.wi
        self.wi += 1
        while self.wissued < min(len(self.wreq), i + AHEAD + 1):
            s2, K2, n2 = self.wreq[self.wissued]
            dst = self.view(self.wslots[self.wissued % NSLOT], BF16, [K2, n2])
            self.dma("pool", dst, s2.rearrange("(k p) c -> p k c", p=128))
            self.wissued += 1
        return self.view(self.wslots[i % NSLOT], BF16, [K, ncols])


def build_program(nl=DEPTH):
    nc = bass.Bass("TRN2", target_bir_lowering=False)
    dt_in = {}

    def din(name, shape):
        dt_in[name] = nc.dram_tensor("in_" + name, list(shape), F32, kind="ExternalInput").ap()
        return dt_in[name]

    NLW = max(nl, 1)
    xin = din("xin", [T, D])
    s0f = din("s0f", [NLW, NH * 64, 128])
    s0b = din("s0b", [NLW, NH * 64, 128])
    cvec = din("cvec", [128, 8])
    flags = din("flags", [128, 16])
    consts = din("consts", [4, 128, 128])
    pp = din("pp", [NLW, 128, NPP])
    gfin = din("gfin", [128, D])
    w_ada = din("w_ada", [NLW, D, 6 * D])
    w_in = din("w_in", [NLW, D, INC])
    w_br_ssd = din("w_br_ssd", [NLW, 2048, D])
    w_pool = din("w_pool", [NLW, 4, 256, 256])
    w_br_pool = din("w_br_pool", [NLW, D, D])
    w_spatial = din("w_spatial", [NLW, 4, 128, 128])
    w_br_gmlp = din("w_br_gmlp", [NLW, D, D])
    w_out = din("w_out", [NLW, D, D])
    w_ffn_in = din("w_ffn_in", [NLW, D, 2 * DFF])
    w_ffn_out = din("w_ffn_out", [NLW, DFF, D])
    yout = nc.dram_tensor("yout", [T, D], F32, kind="ExternalOutput").ap()
    sfo = nc.dram_tensor("sfo", [4, NLW, NH * 64, 128], F32, kind="ExternalOutput").ap()
    sbo = nc.dram_tensor("sbo", [4, NLW, NH * 64, 128], F32, kind="ExternalOutput").ap()
    ybs = nc.dram_tensor("ybs", [16, 128, T], F32, kind="Internal").ap()
    fes = nc.dram_tensor("fes", [NTILE, 128, NFE], BF16, kind="Internal").ap()

    with ExitStack() as es:
        arena_t = es.enter_context(nc.sbuf_tensor("arena", [128, ARENA_KB * 256], F32))
        psum_t = es.enter_context(nc.psum_tensor("psum", [128, 8 * 512], F32))
        B = Builder(nc, nl)
        B.arena = arena_t[:]
        B.psum = psum_t[:]
        for dry in (True, False):
            B.S.dry = dry
            B.top = 0
            B.wi = 0
            B.wissued = 0
            _program(B, dt_in, yout, sfo, sbo, ybs, fes)
        B.S.emit(nc, B.out_dmas)
    return nc


def _program(B, I, yout, sfo, sbo, ybs, fes):
    nl = B.nl
    S = B.S
    xin, s0f, s0b, cvec, flags, consts, pp, gfin = (I[k] for k in
                                                      ("xin", "s0f", "s0b", "cvec", "flags", "consts", "pp", "gfin"))
    w_in = I["w_in"]
    B.wslots = [B.alloc(SLOT_B) for _ in range(NSLOT)]
    XR = B.new(F32, [8, XRW])
    CON = B.new(F32, [4, 128])
    IDF, TRI_F, MSK_F, MSK_B = CON[:, 0, :], CON[:, 1, :], CON[:, 1, :], CON[:, 3, :]
    CONB = B.new(BF16, [4, 128])
    IDB, TRIB_I, TRIB_E = CONB[:, 0, :], CONB[:, 1, :], CONB[:, 2, :]
    ONEB = B.new(BF16, [128])
    ONEF = B.new(F32, [128])
    ZERO = B.new(F32, [128])
    FLG = B.new(F32, [16])
    EPSC = B.new(F32, [2])
    MODV = B.new(F32, [DEPTH, 48])
    GS = B.new(F32, [DEPTH, 2, 8])
    PP = B.new(F32, [NPPS])
    NEGA = B.new(F32, [64])
    WST = B.new(BF16, [4, 128])
    SF = B.new(F32, [2048])
    SBF = B.new(BF16, [2048])
    HT = B.new(BF16, [8, 2, SEGP])
    HSAVE = B.new(BF16, [8, HALO])
    CV = B.new(F32, [8])
    CVB = B.new(BF16, [8])
    PPB = B.new(F32, [48])
    base_top = B.top

    def mcol(i):
        return FLG[:, 1 + i:2 + i]

    bankrr = [0]

    def mmbank():
        b = bankrr[0] % 4
        bankrr[0] += 1
        return b

    B.dma("sp", CON, consts.rearrange("c p f -> p c f"))
    B.dma("sp", FLG, flags)
    B.copy("dve", CONB, CON)
    B.memset("dve", ONEB, 1.0)
    B.memset("dve", ONEF, 1.0)
    B.memset("dve", ZERO, 0.0)
    B.memset("dve", EPSC[:, 0:1], EPS)
    B.memset("dve", EPSC[:, 1:2], 1.0)
    B.memset("dve", XR[:, :, 0:HALO], 0.0)
    B.memset("dve", XR[:, :, XRW - HALO:XRW], 0.0)
    EPS_AP = EPSC[:, 0:1]
    ONE_AP = EPSC[:, 1:2]

    tmp0 = B.top
    IOI = B.new(I32, [96])
    OMI = B.new(I32, [2])
    OM = B.new(F32, [2])
    POSV = B.new(F32, [96])
    ANG = B.new(F32, [4, 96])
    T1 = B.new(F32, [4, 96])
    KI = B.new(I32, [4 * 96])
    T2 = B.new(F32, [4, 96])
    S.op("pool", lambda e: e.iota(IOI[:, 0:32], pattern=[[1, 32]], base=0, channel_multiplier=0), writes=[IOI[:, 0:32]])
    S.op("pool", lambda e: e.iota(IOI[:, 32:96], pattern=[[1, 64]], base=0, channel_multiplier=0), writes=[IOI[:, 32:96]])
    S.op("pool", lambda e: e.iota(OMI, pattern=[[128, 2]], base=0, channel_multiplier=1), writes=[OMI])
    B.copy("dve", POSV, IOI)
    B.copy("dve", OM, OMI)
    B.act(OM, OM, AF.Exp, scale=-math.log(10000.0) / 256.0)
    TWO_PI = 2.0 * math.pi
    for b2 in range(2):
        B.ts("dve", ANG[:, b2, :], POSV, OM[:, b2:b2 + 1], ALU.mult)
        B.ts("dve", ANG[:, 2 + b2, :], POSV, OM[:, b2:b2 + 1], ALU.mult, math.pi / 2, ALU.add)
    A2 = ANG.rearrange("p a b -> p (a b)")
    T12 = T1.rearrange("p a b -> p (a b)")
    T22 = T2.rearrange("p a b -> p (a b)")
    B.ts("dve", T12, A2, 1.0 / TWO_PI, ALU.mult)
    B.copy("dve", KI, T12)
    B.copy("dve", T12, KI)
    B.stt(T22, T12, -TWO_PI, A2, ALU.mult, ALU.add)
    B.ts("dve", T12, T22, math.pi, ALU.is_gt)
    B.stt(T22, T12, -TWO_PI, T22, ALU.mult, ALU.add)
    B.ts("dve", T12, T22, -math.pi, ALU.is_lt)
    B.stt(T22, T12, TWO_PI, T22, ALU.mult, ALU.add)
    B.ts("dve", T22, T22, 3.1415925, ALU.min, -3.1415925, ALU.max)
    PTAB = B.new(F32, [4, 96])
    B.act(PTAB.rearrange("p a b -> p (a b)"), T22, AF.Sin)
    B.ts("dve", PTAB.rearrange("p a b -> p (a b)"), PTAB.rearrange("p a b -> p (a b)"), FLG[:, 0:1], ALU.mult)
    XS = [B.new(F32, [D]) for _ in range(2)]
    for tc in range(16):
        xs = XS[tc % 2]
        B.dma("sp", xs, xin[tc * 128:(tc + 1) * 128, :])
        for half in range(2):
            bk = 4 + (2 * tc + half) % 4
            pb = B.bank(bk)
            for j in range(4):
                blk = half * 4 + j
                B.tr(pb[:, j * 128:(j + 1) * 128], xs[:, blk * 128:(blk + 1) * 128], IDF)
            for j in range(4):
                blk = half * 4 + j
                src = pb[:, j * 128:(j + 1) * 128].rearrange("p (r c) -> p r c", c=64)
                dst = XR[:, blk, HALO + tc * 128:HALO + (tc + 1) * 128].rearrange("p (r c) -> p r c", c=64)
                if blk < 4:
                    pv = PTAB[:, blk, 2 * tc:2 * tc + 2].unsqueeze(2).broadcast_to([128, 2, 64])
                else:
                    pv = PTAB[:, blk - 4, 32:96].unsqueeze(1).broadcast_to([128, 2, 64])
                B.tt("dve", dst, src, pv, ALU.add)
    B.dma("sp", CV, cvec)
    B.act(CVB, CV, AF.Silu)

    def mod_piece(l, cb2):
        if cb2 == 0:
            B.dma("sp", PPB, pp[l, :, PP_BADA:PP_BADA + 48])
        pb = B.bank(6)[:, 510:512]
        wv = B.wnext(I["w_ada"][l, :, cb2 * 256:(cb2 + 1) * 256], 8, 256)
        for j in range(2):
            for k in range(8):
                B.mm(pb[:, j:j + 1], wv[:, k, j * 128:(j + 1) * 128], CVB[:, k:k + 1], start=(k == 0), stop=(k == 7))
        B.tt("dve", MODV[:, l, 2 * cb2:2 * cb2 + 2], pb, PPB[:, 2 * cb2:2 * cb2 + 2], ALU.add)

    for cb2 in range(24):
        mod_piece(0, cb2)
    B.top = base_top

    def modnorm(l, which, tau, mask_halo, fix_left=False):
        m0 = B.top
        SQ = B.new(BF16, [8, SEGP])
        LNV = B.new(F32, [SEGP])
        RSTD = B.new(F32, [SEGP])
        TMP = [B.new(F32, [SEGP]) for _ in range(2)]
        sh0 = 0 if which == 0 else 24
        for s in range(2):
            t0 = tau * TT + s * SEG
            xw = XR[:, :, t0:t0 + SEGP]
            B.act(SQ, xw, AF.Square)
            pb = B.bank(4 + s)
            for k in range(8):
                B.mm(pb[:, 0:SEGP], ONEB, SQ[:, k, :], start=(k == 0), stop=(k == 7))
            B.act(LNV, pb[:, 0:SEGP], AF.Ln, bias=EPS_AP, scale=1.0 / D)
            B.act(RSTD, LNV, AF.Exp, scale=-0.5)
            for k in range(8):
                tmp = TMP[k % 2]
                B.stt(tmp, XR[:, k, t0:t0 + SEGP], GS[:, l, which, k:k + 1], RSTD, ALU.mult, ALU.mult)
                B.act(HT[:, k, s, :], tmp, AF.Identity, bias=MODV[:, l, sh0 + k:sh0 + k + 1])
            if mask_halo:
                sg = 2 * tau + s
                B.ts("dve", HT[:, :, s, 0:HALO], HT[:, :, s, 0:HALO], mcol(sg), ALU.mult)
                B.ts("dve", HT[:, :, s, SEGP - HALO:SEGP], HT[:, :, s, SEGP - HALO:SEGP], mcol(sg + 1), ALU.mult)
        if fix_left:
            if tau >= 1:
                B.ts("dve", HT[:, :, 0, 0:HALO], HSAVE, mcol(2 * tau), ALU.mult)
            B.copy("dve", HSAVE, HT[:, :, 1, SEG:SEG + HALO])
        B.top = m0

    def layer_params(l):
        B.dma("sp", PP, pp[l, :, 0:NPPS])
        for which in range(2):
            sc0 = 8 if which == 0 else 32
            g0 = PP_G1 if which == 0 else PP_G2
            B.stt(GS[:, l, which, :], MODV[:, l, sc0:sc0 + 8], 1.0, PP[:, g0:g0 + 8], ALU.add, ALU.mult)
        B.act(NEGA, PP[:, PP_ALOG:PP_ALOG + 64], AF.Exp)
        B.ts("dve", NEGA, NEGA, -1.0, ALU.mult)
        m0 = B.top
        WSN = B.new(BF16, [4, 128])
        B.dma("pool", WSN, I["w_spatial"][l].rearrange("g i j -> i g j"))
        pb = B.bank(6, BF16)
        for g in range(4):
            B.tr(pb[:, g * 128:(g + 1) * 128], WSN[:, g, :], IDB)
        B.copy("dve", WST.rearrange("p a b -> p (a b)"), pb[:, 0:512])
        B.top = m0

    def load_state(l, src):
        m0 = B.top
        ST = [B.new(F32, [128]) for _ in range(2)]
        for blk in range(16):
            st = ST[blk % 2]
            B.dma("sp", st, src[l, blk * 128:(blk + 1) * 128, :])
            pb = B.bank(6 + blk % 2)
            B.tr(pb[:, 0:128], st, IDF)
            B.copy("act", SF[:, blk * 128:(blk + 1) * 128], pb[:, 0:128])
        B.copy("act", SBF, SF)
        B.top = m0

    v3 = lambda a: a.rearrange("p (c h) -> p c h", h=32)
    h64 = lambda a: a.rearrange("p (h x) -> p h x", x=64)

    def sweep(l, d):
        tiles = range(NTILE) if d == 0 else range(NTILE - 1, -1, -1)
        modq = [0]
        load_state(l, s0f if d == 0 else s0b)
        for tau in tiles:
            if d == 1:
                modnorm(l, 0, tau, True)
            tile_top = B.top
            if d == 0:
                YN_off = B.alloc(16 * TT * 2)
                YN = B.view(YN_off, BF16, [16, TT])
            pos_m = B.top
            XT = B.new(BF16, [16, TT])
            BT = B.new(BF16, [4, TT])
            CT = B.new(BF16, [4, TT])
            ACC = [B.new(F32, [SEG]) for _ in range(8)] if d == 1 else None
            CW0 = PP_CW
            ai = 0
            if d == 0:
                B.dma("sp", HT.rearrange("p a b c -> p (a b c)"), fes[tau, :, 24 * 512:NFE])
                B.dma("sp", XT.rearrange("p a b -> p (a b)"), fes[tau, :, 0:16 * 512])
                B.dma("sp", BT.rearrange("p a b -> p (a b)"), fes[tau, :, 16 * 512:20 * 512])
                B.dma("sp", CT.rearrange("p a b -> p (a b)"), fes[tau, :, 20 * 512:24 * 512])
            pend_silu = []

            def flush_silu():
                while pend_silu:
                    for (cb, s, pb, acc) in pend_silu.pop(0):
                        if cb < 16:
                            dst = XT[:, cb, s * SEG:(s + 1) * SEG]
                        elif cb < 20:
                            dst = BT[:, cb - 16, s * SEG:(s + 1) * SEG]
                        else:
                            dst = CT[:, cb - 20, s * SEG:(s + 1) * SEG]
                        B.act(dst, acc, AF.Silu)

            for cb2 in (range(12) if d == 1 else ()):
                wv = B.wnext(w_in[l, :, C_X + cb2 * 256:C_X + (cb2 + 1) * 256], 8, 256)
                o0 = HALO - 2
                ch = []
                for j in range(2):
                    cb = cb2 * 2 + j
                    for s in range(2):
                        pb = B.bank((cb2 % 2) * 4 + j * 2 + s)
                        for k in range(8):
                            B.mm(pb[:, 0:SEGP], wv[:, k, j * 128:(j + 1) * 128], HT[:, k, s, :],
                                 start=(k == 0), stop=(k == 7))
                        ch.append((cb, s, pb, ACC[(cb2 % 2) * 4 + j * 2 + s]))
                for (cb, s, pb, acc) in ch:
                    B.act(acc, pb[:, o0:o0 + SEG], AF.Identity, bias=PP[:, PP_CB + cb:PP_CB + cb + 1],
                          scale=PP[:, CW0 + cb:CW0 + cb + 1])
                flush_silu()
                for tap in range(1, 5):
                    for (cb, s, pb, acc) in ch:
                        B.stt(acc, pb[:, o0 + tap:o0 + tap + SEG],
                              PP[:, CW0 + tap * 24 + cb:CW0 + tap * 24 + cb + 1], acc, ALU.mult, ALU.add)
                pend_silu.append(ch)
                if cb2 == 11:
                    flush_silu()
            if d == 1:
                B.dma("sp", fes[tau, :, 24 * 512:NFE], HT.rearrange("p a b c -> p (a b c)"))
                B.dma("sp", fes[tau, :, 0:16 * 512], XT.rearrange("p a b -> p (a b)"))
                B.dma("sp", fes[tau, :, 16 * 512:20 * 512], BT.rearrange("p a b -> p (a b)"))
                B.dma("sp", fes[tau, :, 20 * 512:24 * 512], CT.rearrange("p a b -> p (a b)"))
            DTP = B.new(F32, [128])
            DT_ = B.new(F32, [128])
            DLA = B.new(F32, [128])
            ACU = B.new(F32, [128])
            TOT = B.new(F32, [128])
            WV = B.new(F32, [128])
            DEC = B.new(F32, [128])
            HI = B.new(BF16, [128])
            LO = B.new(BF16, [128])
            wdt = B.wnext(w_in[l, :, C_DT + 32 * d:C_DT + 32 * d + 32], 8, 32)
            pb = B.bank(6)
            for c in range(4):
                s, c0 = c // 2, HALO + 128 * (c % 2)
                for k in range(8):
                    B.mm(pb[:, c * 32:(c + 1) * 32], HT[:, k, s, c0:c0 + 128], wdt[:, k, :],
                         start=(k == 0), stop=(k == 7))
            bb = PP[:, PP_DTB + 32 * d:PP_DTB + 32 * d + 32].unsqueeze(1).broadcast_to([128, 4, 32])
            na = NEGA[:, 32 * d:32 * d + 32].unsqueeze(1).broadcast_to([128, 4, 32])
            B.tt("dve", v3(DTP), v3(pb[:, 0:128]), bb, ALU.add)
            B.act(DTP, DTP, AF.Exp)
            B.act(DT_, DTP, AF.Ln, bias=ONE_AP, scale=1.0)
            B.tt("dve", v3(DLA), v3(DT_), na, ALU.mult)
            pb = B.bank(7)
            B.mm(pb[:, 0:128], TRI_F, DLA)
            B.mm(pb[:, 128:256], ONEF, DLA)
            B.copy("act", ACU, pb[:, 0:128])
            B.copy("act", TOT, pb[:, 128:256])
            B.act(DEC, TOT, AF.Exp)
            if d == 0:
                tri_b, msk = TRIB_I, MSK_F
            else:
                B.tt("dve", ACU, DLA, ACU, ALU.subtract)
                B.tt("dve", ACU, ACU, TOT, ALU.add)
                tri_b, msk = CONB[:, 3, :], MSK_B
            B.tt("dve", WV, TOT, ACU, ALU.subtract)
            B.act(WV, WV, AF.Exp)
            B.tt("dve", WV, WV, DT_, ALU.mult)
            B.copy("dve", HI, DLA)
            B.tt("dve", LO, DLA, HI, ALU.subtract)
            CLB = ACU
            BK = B.new(BF16, [4, 512])
            for cp in range(2):
                pbb = B.bank(4 + cp, BF16)
                for c in (2 * cp, 2 * cp + 1):
                    for g in range(4):
                        o_ = ((c % 2) * 4 + g) * 128
                        B.tr(pbb[:, o_:o_ + 128], BT[:, g, c * 128:(c + 1) * 128], IDB)
                B.copy("act", BK[:, 2 * cp:2 * cp + 2, :].rearrange("p a b -> p (a b)"), pbb[:, 0:1024])
            xkc_off = B.alloc(2 * 4096)
            XKC = [B.view(xkc_off + i * 4096, BF16, [2048]) for i in range(2)]
            YG = B.view(xkc_off, F32, [4, TT]) if d == 0 else B.new(F32, [4, TT])
            CBM = [B.new(BF16, [128]) for _ in range(2)]
            SD4 = [B.new(F32, [512]) for _ in range(2)]
            SDB = [B.new(BF16, [512]) for _ in range(2)]
            ER4 = [B.new(BF16, [512]) for _ in range(2)]
            MR = [B.new(BF16, [4, 128]) for _ in range(4)]
            XDT = [B.new(BF16, [512]) for _ in range(2)]
            r3_off = B.alloc(8192)
            MR += [B.view(r3_off + i * 1024, BF16, [4, 128]) for i in range(2)]
            XDT.append(B.view(r3_off + 2048, BF16, [512]))
            CSR = [B.new(BF16, [4, 128]) for _ in range(4)]
            XW = [B.new(BF16, [512]) for _ in range(2)]
            CSR += [B.view(r3_off + 3072 + i * 1024, BF16, [4, 128]) for i in range(2)]
            XW.append(B.view(r3_off + 5120, BF16, [512]))
            STG4 = B.new(F32, [512])
            if d == 0:
                YBT = [B.new(F32, [TT]) for _ in range(2)]
                ZS2 = [B.view(r3_off + i * 2048, F32, [TT]) for i in range(2)]
                SQG2 = [B.view(r3_off + 4096 + i * 1024, BF16, [TT]) for i in range(4)]
                LNG = B.new(F32, [TT])
            corder = list(range(4)) if d == 0 else [3, 2, 1, 0]
            qd = [0]

            def stageA(g, c, it):
                cols = slice(c * 128, (c + 1) * 128)
                pcb = B.bank(6)
                cbo = (it % 2) * 128
                B.mm(pcb[:, cbo:cbo + 128], BT[:, g, cols], CT[:, g, cols])
                cbm = CBM[it % 2]
                B.tt("dve", cbm, pcb[:, cbo:cbo + 128], msk, ALU.mult)
                B.tt("pool", h64(XW[it % 3]), h64(XKC[c % 2][:, g * 512:(g + 1) * 512]),
                     WV[:, c * 32 + 8 * g:c * 32 + 8 * g + 8].unsqueeze(2).broadcast_to([128, 8, 64]), ALU.mult)
                B.tt("pool", h64(XDT[it % 3]), h64(XKC[c % 2][:, g * 512:(g + 1) * 512]),
                     DT_[:, c * 32 + 8 * g:c * 32 + 8 * g + 8].unsqueeze(2).broadcast_to([128, 8, 64]), ALU.mult)
                for quad in range(2):
                    pab = B.bank((it % 2) * 2 + quad)
                    sd = SD4[qd[0] % 2]
                    er = ER4[qd[0] % 2]
                    qd[0] += 1
                    cs = CSR[(it % 3) * 2 + quad]
                    for e4 in range(4):
                        q = c * 32 + 8 * g + quad * 4 + e4
                        ab = pab[:, e4 * 128:(e4 + 1) * 128]
                        B.mm(ab, HI[:, q:q + 1].broadcast_to([128, 128]), tri_b, start=True, stop=False)
                        B.mm(ab, LO[:, q:q + 1].broadcast_to([128, 128]), tri_b, start=False, stop=True)
                    for e4 in range(4):
                        q = c * 32 + 8 * g + quad * 4 + e4
                        ab = pab[:, e4 * 128:(e4 + 1) * 128]
                        B.act(sd[:, e4 * 128:(e4 + 1) * 128], ab, AF.Relu, bias=CLB[:, q:q + 1], scale=-1.0)
                    sdb = SDB[(qd[0] - 1) % 2]
                    B.act(sdb, sd, AF.Exp, scale=-1.0)
                    B.act(er, pab, AF.Exp)
                    B.tt("dve", MR[(it % 3) * 2 + quad], sdb.rearrange("p (a b) -> p a b", b=128),
                         cbm.unsqueeze(1).broadcast_to([128, 4, 128]), ALU.mult)
                    B.tt("dve", cs, CT[:, g, cols].unsqueeze(1).broadcast_to([128, 4, 128]),
                         er.rearrange("p (a b) -> p a b", b=128), ALU.mult)

            def stageB(g, c, it):
                cols = slice(c * 128, (c + 1) * 128)
                py4 = B.bank(4 + it % 2)
                for hp in range(4):
                    py = py4[:, hp * 128:(hp + 1) * 128]
                    for e in range(2):
                        e8 = 2 * hp + e
                        h = 8 * g + e8
                        B.mm(py[e * 64:(e + 1) * 64, :], XDT[it % 3][:, e8 * 64:(e8 + 1) * 64],
                             MR[(it % 3) * 2 + e8 // 4][:, e8 % 4, :], start=True, stop=False)
                        B.mm(py[e * 64:(e + 1) * 64, :], SBF[:, h * 64:(h + 1) * 64],
                             CSR[(it % 3) * 2 + e8 // 4][:, e8 % 4, :], start=False, stop=True)
                if d == 0:
                    for hp in range(4):
                        blk = 4 * g + hp
                        B.stt(YN[:, blk, cols], XT[:, blk, cols], PP[:, PP_DSK + blk:PP_DSK + blk + 1],
                              py4[:, hp * 128:(hp + 1) * 128], ALU.mult, ALU.add)
                if d == 1:
                    ygr = YG[:, :, (it % 4) * 128:(it % 4 + 1) * 128]
                    B.copy("dve", ygr, py4.rearrange("p (a b) -> p a b", b=128))
                    t_a = tau * TT + c * 128
                    B.dma("sp", ybs[4 * g:4 * g + 4, :, t_a:t_a + 128].rearrange("b p t -> p b t"), ygr)
                pl = B.bank(7)
                B.mm(pl, BK[:, c, g * 128:(g + 1) * 128], XW[it % 3])
                sg_ = SF[:, g * 512:(g + 1) * 512]
                B.tt("pool", h64(sg_), h64(sg_),
                     DEC[:, c * 32 + 8 * g:c * 32 + 8 * g + 8].unsqueeze(2).broadcast_to([128, 8, 64]), ALU.mult)
                B.tt("dve", sg_, sg_, pl, ALU.add)
                seg_end = (c % 2 == 1) if d == 0 else (c % 2 == 0)
                if seg_end:
                    sgi = 2 * tau + c // 2
                    if sgi < 4:
                        dst_t = sfo if d == 0 else sbo
                        ptb = B.bank(7)
                        for b4 in range(4):
                            B.tr(ptb[:, b4 * 128:(b4 + 1) * 128], sg_[:, b4 * 128:(b4 + 1) * 128], IDF)
                        B.copy("act", STG4, ptb)
                        r0 = g * 512
                        o = B.dma("sp", dst_t[sgi, l, r0:r0 + 512, :].rearrange("(b p) n -> p b n", p=128),
                                  STG4.rearrange("p (b n) -> p b n", n=128))
                        if o is not None:
                            B.out_dmas.append(o)
                    bnd = (sgi + 1) if d == 0 else sgi
                    B.ts("dve", sg_, sg_, mcol(bnd), ALU.mult)
                B.copy("act", SBF[:, g * 512:(g + 1) * 512], sg_)

            def xk_transposes(c):
                for half in range(2):
                    pbb = B.bank(4 + half, BF16)
                    for j in range(8):
                        B.tr(pbb[:, j * 128:(j + 1) * 128], XT[:, half * 8 + j, c * 128:(c + 1) * 128], IDB)
                    B.copy("act", XKC[c % 2][:, half * 1024:(half + 1) * 1024], pbb[:, 0:1024])

            its = [(g, c) for c in corder for g in range(4)]
            pend = []
            for n, (g, c) in enumerate(its):
                if g == 0:
                    xk_transposes(c)
                stageA(g, c, n)
                pend.append((g, c, n))
                if len(pend) > 2:
                    stageB(*pend.pop(0))
                if d == 1 and l + 1 < nl and modq[0] < 24:
                    mod_piece(l + 1, modq[0])
                    modq[0] += 1
            while pend:
                stageB(*pend.pop(0))
            if d == 0:
                gate_tail = []
                for g in range(4):
                    pss = B.bank(6)
                    pzs = []
                    for hp2 in range(2):
                        wv = B.wnext(w_in[l, :, C_Z + g * 512 + hp2 * 256:C_Z + g * 512 + (hp2 + 1) * 256], 8, 256)
                        for j in range(2):
                            hp = hp2 * 2 + j
                            pz = B.bank(mmbank())
                            for s in range(2):
                                for k in range(8):
                                    B.mm(pz[:, s * SEG:(s + 1) * SEG], wv[:, k, j * 128:(j + 1) * 128],
                                         HT[:, k, s, HALO:HALO + SEG], start=(k == 0), stop=(k == 7))
                            pzs.append(pz)
                    if gate_tail:
                        gate_tail.pop(0)()
                    for hp in range(4):
                        blk = 4 * g + hp
                        ybt = YBT[hp % 2]
                        B.dma("sp", ybt, ybs[blk, :, tau * TT:(tau + 1) * TT])
                        B.tt("dve", YG[:, hp, :], YN[:, blk, :], ybt, ALU.add)
                    B.act(ZS2[0], pzs[0], AF.Silu)
                    B.act(ZS2[1], pzs[1], AF.Silu)
                    for hp in range(4):
                        B.tt("dve", YG[:, hp, :], YG[:, hp, :], ZS2[hp % 2], ALU.mult)
                        if hp + 2 < 4:
                            B.act(ZS2[hp % 2], pzs[hp + 2], AF.Silu)
                        B.act(SQG2[hp], YG[:, hp, :], AF.Square)

                    def tail(g=g, pss=pss):
                        for hp in range(4):
                            B.mm(pss, ONEB, SQG2[hp], start=(hp == 0), stop=(hp == 3))
                        B.act(LNG, pss, AF.Ln, bias=EPS_AP, scale=1.0 / 512)
                        B.act(LNG, LNG, AF.Exp, scale=-0.5)
                        for hp in range(4):
                            blk = 4 * g + hp
                            B.stt(YN[:, blk, :], YG[:, hp, :], PP[:, PP_GSSD + blk:PP_GSSD + blk + 1], LNG,
                                  ALU.mult, ALU.mult)
                    gate_tail.append(tail)
                while gate_tail:
                    gate_tail.pop(0)()
            if d == 1:
                B.top = tile_top
                continue
            B.top = pos_m
            MRG = B.new(F32, [8, TT])
            MB = B.new(BF16, [8, TT])
            GA = [B.new(BF16, [TT]) for _ in range(2)]
            TMPF = [B.new(F32, [TT]) for _ in range(2)]
            br_top = B.top
            gate_w = {}

            def branch_out(wsrc, K, rhs_of, goff, mode, hook=None):
                for jo in range(8):
                    if hook is not None:
                        hook(jo)
                    wvg = B.wnext(w_in[l, :, goff + jo * 128:goff + (jo + 1) * 128], 8, 128)
                    pg = B.bank(mmbank())
                    for s in range(2):
                        for k in range(8):
                            B.mm(pg[:, s * SEG:(s + 1) * SEG], wvg[:, k, :],
                                 HT[:, k, s, HALO:HALO + SEG], start=(k == 0), stop=(k == 7))
                    ga = GA[jo % 2]
                    B.act(ga, pg, AF.Sigmoid)
                    pbr = B.bank(mmbank())
                    for kh in range((K + 15) // 16):
                        k0 = kh * 16
                        kk = min(16, K - k0)
                        wv = B.wnext(wsrc[k0 * 128:(k0 + kk) * 128, jo * 128:(jo + 1) * 128], kk, 128)
                        for k in range(kk):
                            B.mm(pbr, wv[:, k, :], rhs_of(k0 + k), start=(k0 + k == 0), stop=(k0 + k == K - 1))
                    if mode == 0:
                        B.tt("dve", MRG[:, jo, :], pbr, ga, ALU.mult)
                    else:
                        tf = TMPF[jo % 2]
                        B.tt("dve", tf, pbr, ga, ALU.mult)
                        B.tt("dve", MRG[:, jo, :] if mode == 1 else MB[:, jo, :], MRG[:, jo, :], tf, ALU.add)

            VI = B.new(F32, [2, SEGP])
            CN = [B.new(F32, [SEGP]) for _ in range(4)]
            RC = B.new(F32, [4, 2, SEG])
            for s in range(2):
                sg = 2 * tau + s
                B.memset("dve", VI[:, s, :], 1.0)
                B.ts("dve", VI[:, s, 0:HALO], VI[:, s, 0:HALO], mcol(sg), ALU.mult)
                B.ts("dve", VI[:, s, SEGP - HALO:SEGP], VI[:, s, SEGP - HALO:SEGP], mcol(sg + 1), ALU.mult)
                B.tt("dve", CN[0][:, 1:SEGP], VI[:, s, 0:SEGP - 1], VI[:, s, 1:SEGP], ALU.add)
                B.tt("dve", CN[1][:, 2:SEGP - 1], CN[0][:, 1:SEGP - 2], CN[0][:, 3:SEGP], ALU.add)
                B.tt("dve", CN[2][:, 4:SEGP - 3], CN[1][:, 2:SEGP - 5], CN[1][:, 6:SEGP - 1], ALU.add)
                B.tt("dve", CN[3][:, 8:SEGP - 7], CN[2][:, 4:SEGP - 11], CN[2][:, 12:SEGP - 3], ALU.add)
                for wi in range(4):
                    S.op("dve", (lambda o_, i_: (lambda e: e.reciprocal(out=o_, in_=i_)))(
                        RC[:, wi, s, :], CN[wi][:, HALO:HALO + SEG]),
                        reads=[CN[wi][:, HALO:HALO + SEG]], writes=[RC[:, wi, s, :]])
            PL = B.new(BF16, [8, TT])
            P0 = [B.new(F32, [SEGP]) for _ in range(2)]
            SM = [B.new(F32, [SEGP]) for _ in range(4)]
            pst = {"pi": 0, "wv": None}

            def pool_chain(cb):
                wv = B.wnext(w_in[l, :, C_P + cb * 128:C_P + (cb + 1) * 128], 8, 128)
                j = 0
                wi = cb // 2
                for s in range(2):
                    pb = B.bank(mmbank())
                    for k in range(8):
                        B.mm(pb[:, 0:SEGP], wv[:, k, j * 128:(j + 1) * 128], HT[:, k, s, :],
                             start=(k == 0), stop=(k == 7))
                    p0 = P0[pst["pi"] % 2]
                    pst["pi"] += 1
                    B.copy("act", p0, pb[:, 0:SEGP])
                    B.tt("dve", SM[0][:, 1:SEGP], p0[:, 0:SEGP - 1], p0[:, 1:SEGP], ALU.add)
                    if wi >= 1:
                        B.tt("dve", SM[1][:, 2:SEGP - 1], SM[0][:, 1:SEGP - 2], SM[0][:, 3:SEGP], ALU.add)
                    if wi >= 2:
                        B.tt("dve", SM[2][:, 4:SEGP - 3], SM[1][:, 2:SEGP - 5], SM[1][:, 6:SEGP - 1], ALU.add)
                    if wi >= 3:
                        B.tt("dve", SM[3][:, 8:SEGP - 7], SM[2][:, 4:SEGP - 11], SM[2][:, 12:SEGP - 3], ALU.add)
                    sm = SM[wi]
                    B.tt("dve", sm[:, HALO:HALO + SEG], sm[:, HALO:HALO + SEG], RC[:, wi, s, :], ALU.mult)
                    B.tt("dve", PL[:, cb, s * SEG:(s + 1) * SEG], sm[:, HALO:HALO + SEG],
                         p0[:, HALO:HALO + SEG], ALU.subtract)

            branch_out(I["w_br_ssd"][l], 16, lambda k: YN[:, k, :], C_G, 0, hook=pool_chain)
            VW = B.view(YN_off, BF16, [8, 1024])
            for q4 in range(4):
                B.dma("pool", VW[:, :, q4 * 256:(q4 + 1) * 256],
                      w_in[l, :, C_V + q4 * 256:C_V + (q4 + 1) * 256].rearrange("(k p) c -> p k c", p=128))
            YP = B.new(BF16, [8, TT])
            wpv = B.wnext(I["w_pool"][l].rearrange("g r c -> (g r) c"), 8, 256)
            for g4 in range(4):
                for ob in range(2):
                    pb = B.bank(mmbank())
                    for kb in range(2):
                        B.mm(pb, wpv[:, g4 * 2 + kb, ob * 128:(ob + 1) * 128], PL[:, 2 * g4 + kb, :],
                             start=(kb == 0), stop=(kb == 1))
                    blk = 2 * g4 + ob
                    B.act(YP[:, blk, :], pb, AF.Identity, scale=PP[:, PP_PSC + blk:PP_PSC + blk + 1])
            branch_out(I["w_br_pool"][l], 8, lambda k: YP[:, k, :], C_G + 1024, 1)
            B.top = br_top
            GSGU = B.new(F32, [1024])
            BSP = B.new(F32, [512])
            B.dma("sp", GSGU, pp[l, :, PP_GSGU:PP_GSGU + 1024])
            B.dma("sp", BSP, pp[l, :, PP_BSP:PP_BSP + 512])
            VN = B.new(BF16, [4, 1024])
            UT = B.new(BF16, [8, TT])
            YM = B.new(BF16, [8, TT])
            SSV = B.new(F32, [8])
            JNK = B.new(BF16, [1024])
            for c in range(4):
                s, c0 = c // 2, HALO + 128 * (c % 2)
                b0 = (c % 2) * 2
                for q4 in range(4):
                    pb = B.bank(b0 + q4 // 2)
                    for k in range(8):
                        B.mm(pb[:, (q4 % 2) * 256:(q4 % 2) * 256 + 256], HT[:, k, s, c0:c0 + 128],
                             VW[:, k, q4 * 256:(q4 + 1) * 256], start=(k == 0), stop=(k == 7))
                pv = B.psum[:, b0 * 512:(b0 + 2) * 512]
                B.act(JNK, pv, AF.Square, accum_out=SSV[:, c:c + 1])
                B.act(SSV[:, 4 + c:5 + c], SSV[:, c:c + 1], AF.Ln, bias=EPS_AP, scale=1.0 / D)
                B.act(SSV[:, 4 + c:5 + c], SSV[:, 4 + c:5 + c], AF.Exp, scale=-0.5)
                B.stt(VN[:, c, :], pv, SSV[:, 4 + c:5 + c], GSGU, ALU.mult, ALU.mult)
            for cb2 in range(4):
                wv = B.wnext(w_in[l, :, C_U + cb2 * 256:C_U + (cb2 + 1) * 256], 8, 256)
                for j in range(2):
                    cb = cb2 * 2 + j
                    pu = B.bank(mmbank())
                    for s in range(2):
                        for k in range(8):
                            B.mm(pu[:, s * SEG:(s + 1) * SEG], wv[:, k, j * 128:(j + 1) * 128],
                                 HT[:, k, s, HALO:HALO + SEG], start=(k == 0), stop=(k == 7))
                    B.copy("act", UT[:, cb, :], pu)
                    g4 = cb // 2
                    psv = B.bank(mmbank())
                    for c in range(4):
                        B.mm(psv[:, c * 128:(c + 1) * 128], VN[:, c, cb * 128:(cb + 1) * 128], WST[:, g4, :])
                    tf = TMPF[cb % 2]
                    B.tt("dve", tf.rearrange("p (c i) -> p c i", i=128), psv.rearrange("p (c i) -> p c i", i=128),
                         BSP[:, g4 * 128:(g4 + 1) * 128].unsqueeze(1).broadcast_to([128, 4, 128]), ALU.add)
                    B.tt("dve", YM[:, cb, :], tf, UT[:, cb, :], ALU.mult)
            branch_out(I["w_br_gmlp"][l], 8, lambda k: YM[:, k, :], C_G + 2048, 2)
            tcol = slice(HALO + tau * TT, HALO + (tau + 1) * TT)
            for jo in range(8):
                wv = B.wnext(I["w_out"][l][:, jo * 128:(jo + 1) * 128], 8, 128)
                po = B.bank(mmbank())
                for k in range(8):
                    B.mm(po, wv[:, k, :], MB[:, k, :], start=(k == 0), stop=(k == 7))
                B.stt(XR[:, jo, tcol], po, MODV[:, l, 16 + jo:17 + jo], XR[:, jo, tcol], ALU.mult, ALU.add)
            B.top = tile_top
            modnorm(l, 1, tau, False)
            HID = B.new(BF16, [22, TT])
            SGT = [B.new(F32, [TT]) for _ in range(2)]
            for j2 in range(11):
                wg = B.wnext(I["w_ffn_in"][l][:, j2 * 256:(j2 + 1) * 256], 8, 256)
                wu = B.wnext(I["w_ffn_in"][l][:, DFF + j2 * 256:DFF + (j2 + 1) * 256], 8, 256)
                for j in range(2):
                    jj = j2 * 2 + j
                    pg = B.bank(mmbank())
                    pu = B.bank(mmbank())
                    for (pp_, wv) in ((pg, wg), (pu, wu)):
                        for s in range(2):
                            for k in range(8):
                                B.mm(pp_[:, s * SEG:(s + 1) * SEG], wv[:, k, j * 128:(j + 1) * 128],
                                     HT[:, k, s, HALO:HALO + SEG], start=(k == 0), stop=(k == 7))
                    sg_ = SGT[jj % 2]
                    B.act(sg_, pg, AF.Silu)
                    B.tt("dve", HID[:, jj, :], pu, sg_, ALU.mult)
            for jo in range(8):
                po = B.bank(mmbank())
                for kh in range(2):
                    wv = B.wnext(I["w_ffn_out"][l][kh * 11 * 128:(kh + 1) * 11 * 128, jo * 128:(jo + 1) * 128], 11, 128)
                    for k in range(11):
                        B.mm(po, wv[:, k, :], HID[:, kh * 11 + k, :], start=(kh == 0 and k == 0),
                             stop=(kh == 1 and k == 10))
                B.stt(XR[:, jo, tcol], po, MODV[:, l, 40 + jo:41 + jo], XR[:, jo, tcol], ALU.mult, ALU.add)
            B.top = tile_top

    for l in range(nl):
        layer_params(l)
        sweep(l, 1)
        sweep(l, 0)

    GF = B.new(F32, [D])
    B.dma("sp", GF, gfin)
    OS = [B.new(F32, [D]) for _ in range(2)]
    SSO = B.new(F32, [32])
    JN2 = B.new(BF16, [D])
    for tc in range(16):
        b0 = (tc % 2) * 2
        for blk in range(8):
            pb = B.bank(b0 + blk // 4)
            B.tr(pb[:, (blk % 4) * 128:(blk % 4) * 128 + 128], XR[:, blk, HALO + tc * 128:HALO + (tc + 1) * 128], IDF)
        pv = B.psum[:, b0 * 512:(b0 + 2) * 512]
        B.act(JN2, pv, AF.Square, accum_out=SSO[:, tc:tc + 1])
        B.act(SSO[:, 16 + tc:17 + tc], SSO[:, tc:tc + 1], AF.Ln, bias=EPS_AP, scale=1.0 / D)
        B.act(SSO[:, 16 + tc:17 + tc], SSO[:, 16 + tc:17 + tc], AF.Exp, scale=-0.5)
        osb = OS[tc % 2]
        B.stt(osb, pv, SSO[:, 16 + tc:17 + tc], GF, ALU.mult, ALU.mult)
        o = B.dma("sp", yout[tc * 128:(tc + 1) * 128, :], osb)
        if o is not None:
            B.out_dmas.append(o)


_CACHE = {}


def _consts():
    j = np.arange(128)[:, None]
    i = np.arange(128)[None, :]
    return np.stack([np.eye(128), (j <= i), (j < i), (j >= i)]).astype(np.float32)


def _pack_pp(inp, l):
    f = np.float32
    pm = lambda v: np.ascontiguousarray(np.asarray(v, f).reshape(-1, 128).T)
    cols = [pm(inp["g_norm1"][l]), pm(inp["g_norm2"][l]), pm(inp["b_ada"][l])]
    cw = np.asarray(inp["conv_w"][l], f)[:, 2048 - 2048:]
    cols.append(np.concatenate([pm(cw[t]) for t in range(5)], axis=1))
    cols.append(pm(inp["conv_b"][l]))
    cols.append(pm(np.repeat(np.asarray(inp["d_skip"][l], f), 64)))
    cols.append(pm(inp["g_ssd"][l]))
    cols.append(pm(inp["pool_scale"][l]))
    cols.append(np.broadcast_to(np.asarray(inp["dt_bias"][l], f).reshape(1, 64), (128, 64)))
    cols.append(np.broadcast_to(np.asarray(inp["a_log"][l], f).reshape(1, 64), (128, 64)))
    cols.append(np.broadcast_to(np.asarray(inp["g_sgu"][l], f).reshape(1, 1024), (128, 1024)))
    cols.append(np.broadcast_to(np.asarray(inp["b_spatial"][l], f).reshape(1, 512), (128, 512)))
    out = np.concatenate(cols, axis=1).astype(f)
    assert out.shape == (128, NPP), out.shape
    return out


def kernel(nl=DEPTH, **inp):
    f = np.float32
    if nl not in _CACHE:
        _CACHE[nl] = build_program(nl)
    nc = _CACHE[nl]
    xp = np.asarray(inp["x_prompt"], f)
    xs = np.asarray(inp["x_sample"], f)
    NLW = max(nl, 1)
    pp = np.stack([_pack_pp(inp, l) for l in range(NLW)])
    shared = {
        "in_consts": _consts(),
        "in_pp": pp,
        "in_gfin": np.ascontiguousarray(np.broadcast_to(np.asarray(inp["g_final"], f)[None, :], (128, D))),
    }
    for k in ("w_ada", "w_in", "w_br_ssd", "w_pool", "w_br_pool", "w_spatial", "w_br_gmlp", "w_out",
              "w_ffn_in", "w_ffn_out"):
        shared["in_" + k] = np.ascontiguousarray(np.asarray(inp[k], f)[:NLW])
    in_maps = []
    zst = np.zeros((NLW, NH * 64, 128), f)
    for core in range(8):
        m = dict(shared)
        fl = np.zeros((128, 16), f)
        if core < 4:
            m["in_xin"] = np.ascontiguousarray(xs[core])
            m["in_s0f"] = np.ascontiguousarray(np.asarray(inp["state_ssd_fwd"], f)[core, :NLW].reshape(NLW, NH * 64, 128))
            m["in_s0b"] = np.ascontiguousarray(np.asarray(inp["state_ssd_bwd"], f)[core, :NLW].reshape(NLW, NH * 64, 128))
            cv = np.asarray(inp["c"], f)[core]
            fl[:, 0] = 1.0
            fl[:, 2:9] = 1.0
        else:
            q = core - 4
            xx = np.zeros((T, D), f)
            xx[:1024] = xp[4 * q:4 * q + 4].reshape(1024, D)
            m["in_xin"] = xx
            m["in_s0f"] = zst
            m["in_s0b"] = zst
            cv = np.asarray(inp["c_ctx"], f)
        m["in_cvec"] = np.ascontiguousarray(cv.reshape(8, 128).T)
        m["in_flags"] = fl
        in_maps.append(m)
    res = run_bass_kernel_spmd(nc, in_maps, core_ids=list(range(8)))
    R = res.results
    y_sample = np.stack([R[c]["yout"] for c in range(4)]).astype(f)
    y_prompt = np.concatenate([R[4 + q]["yout"][:1024].reshape(4, 256, D) for q in range(4)]).astype(f)
    nsf = np.concatenate([R[4 + q]["sfo"].reshape(4, NLW, NH, 64, 128) for q in range(4)]).astype(f)
    nsb = np.concatenate([R[4 + q]["sbo"].reshape(4, NLW, NH, 64, 128) for q in range(4)]).astype(f)
    return (y_prompt, y_sample, nsf, nsb)
```

```python
import math
from contextlib import ExitStack
import numpy as np
import concourse.bass as bass
import concourse.mybir as mybir
from concourse.bass_utils import run_bass_kernel_spmd

F32 = mybir.dt.float32
BF16 = mybir.dt.bfloat16
I32 = mybir.dt.int32
AF = mybir.ActivationFunctionType
ALU = mybir.AluOpType

D = 1024
DEPTH = 4
T = 2048
NTILE = 4
TT = 512
SEG = 256
HALO = 8
SEGP = SEG + 2 * HALO
XRW = T + 2 * HALO
NH = 32
DFF = 2816
INC = 11328
EPS = 1e-6
C_Z, C_X, C_B, C_C, C_DT, C_P, C_U, C_V, C_G = 0, 2048, 4096, 4608, 5120, 5184, 6208, 7232, 8256
NPP = 376 + 1024 + 512
NPPS = 376
PP_G1, PP_G2, PP_BADA, PP_CW, PP_CB, PP_DSK, PP_GSSD, PP_PSC, PP_DTB, PP_ALOG, PP_GSGU, PP_BSP = (
    0, 8, 16, 64, 184, 208, 224, 240, 248, 312, 376, 1400)
ARENA_KB = 207
CELL = 256
NSLOT = 5
SLOT_B = 4096
AHEAD = 2
NDSEM = 12
NFE = 24 * 512 + 8 * 2 * SEGP


def _esize(dt):
    return 2 if dt == BF16 else 4


class Op:
    __slots__ = ("stream", "dom", "idx", "fn", "waits", "signaled", "is_dma", "count")


class Sched:
    def __init__(self):
        self.streams = {k: [] for k in ("pe", "act", "dve", "pool", "sp")}
        self.domcnt = {}
        self.cells = {}
        self.waited = {k: {} for k in self.streams}
        self.dma_rr = {"sp": 0, "pool": 0}
        self.dma_last = {}
        self.nops = 0
        self.dry = False

    @staticmethod
    def region(ap):
        t = ap.tensor
        name = t.name
        es = _esize(ap.dtype)
        dims = [list(d) for d in ap.ap]
        cls = type(t).__name__
        if cls.startswith("DRam"):
            ext = sum((n - 1) * abs(s) for s, n in dims) + 1
            b0 = ap.offset * es
            return ("d:" + name, b0 // 65536, (b0 + ext * es - 1) // 65536 + 1, 0, 2)
        pstride, pn = dims[0]
        if pstride == 0:
            pstride = 1 << 40
        fd = dims[1:]
        ext = sum((n - 1) * abs(s) for s, n in fd) + 1
        p0 = ap.offset // pstride if pstride < (1 << 40) else 0
        c0 = ap.offset - p0 * pstride if pstride < (1 << 40) else ap.offset
        b0 = c0 * es
        b1 = (c0 + ext) * es
        p1 = p0 + pn
        h0 = 0 if p0 < 64 else 1
        h1 = 1 if p1 <= 64 else 2
        if cls.startswith("PS") or "psum" in name:
            return ("ps:" + name, b0 // 2048, (b1 - 1) // 2048 + 1, p0 // 32, (p1 - 1) // 32 + 1)
        return ("sb:" + name, b0 // CELL, (b1 - 1) // CELL + 1, h0, h1)

    def op(self, stream, fn, reads=(), writes=(), dma=False):
        self.nops += 1
        if self.dry:
            return None
        o = Op()
        o.stream = stream
        o.is_dma = dma
        o.fn = fn
        o.signaled = dma
        o.count = None
        if dma:
            k = self.dma_rr[stream] % NDSEM
            self.dma_rr[stream] += 1
            o.dom = (stream, k)
        else:
            o.dom = stream
        o.idx = self.domcnt.get(o.dom, 0)
        self.domcnt[o.dom] = o.idx + 1
        need = {}

        def want(w):
            if w is None:
                return
            if w.dom == "pe" and stream == "pe" and not dma:
                return
            cur = need.get(w.dom)
            if cur is None or cur.idx < w.idx:
                need[w.dom] = w
        if dma:
            want(self.dma_last.get(o.dom))
            self.dma_last[o.dom] = o
        cells = self.cells
        for ap in reads:
            if ap is None:
                continue
            sp, c0, c1, h0, h1 = self.region(ap)
            if sp.startswith("d:in_"):
                continue
            for c in range(c0, c1):
                for h in range(h0, h1):
                    rec = cells.get((sp, c, h))
                    if rec is None:
                        rec = [None, {}]
                        cells[(sp, c, h)] = rec
                    want(rec[0])
        for ap in writes:
            sp, c0, c1, h0, h1 = self.region(ap)
            for c in range(c0, c1):
                for h in range(h0, h1):
                    rec = cells.get((sp, c, h))
                    if rec is None:
                        rec = [None, {}]
                        cells[(sp, c, h)] = rec
                    want(rec[0])
                    for r in rec[1].values():
                        want(r)
        wl = []
        wd = self.waited[stream]
        for dom, w in need.items():
            if wd.get(dom, -1) >= w.idx:
                continue
            wd[dom] = w.idx
            w.signaled = True
            wl.append(w)
        o.waits = wl
        for ap in reads:
            if ap is None:
                continue
            sp, c0, c1, h0, h1 = self.region(ap)
            if sp.startswith("d:in_"):
                continue
            for c in range(c0, c1):
                for h in range(h0, h1):
                    rec = cells[(sp, c, h)]
                    cur = rec[1].get(o.dom)
                    if cur is None or cur.idx < o.idx:
                        rec[1][o.dom] = o
        for ap in writes:
            sp, c0, c1, h0, h1 = self.region(ap)
            for c in range(c0, c1):
                for h in range(h0, h1):
                    rec = cells[(sp, c, h)]
                    rec[0] = o
                    rec[1] = {}
        self.streams[stream].append(o)
        return o

    def emit(self, nc, final_waits):
        doms = set(self.domcnt.keys())
        for dom in doms:
            cnt = 0
            for st in self.streams.values():
                pass
        per_dom = {}
        for sname, ops in self.streams.items():
            for o in ops:
                per_dom.setdefault(o.dom, []).append(o)
        for dom, ops in per_dom.items():
            ops.sort(key=lambda x: x.idx)
            c = 0
            for o in ops:
                if o.signaled:
                    c += 1
                    o.count = c
        with ExitStack() as es:
            sems = {}
            for dom in per_dom:
                nm = dom if isinstance(dom, str) else "%s_d%d" % dom
                sems[dom] = es.enter_context(nc.semaphore("s_" + nm))
            block = es.enter_context(nc.Block())

            def run(sname, eng):
                for o in self.streams[sname]:
                    for w in o.waits:
                        inc = 16 if w.is_dma else 1
                        eng.wait_ge(sems[w.dom], w.count * inc)
                    ins = o.fn(eng)
                    if o.signaled:
                        ins.then_inc(sems[o.dom], 16 if o.is_dma else 1)
                if sname == "sp":
                    for o in final_waits:
                        eng.wait_ge(sems[o.dom], o.count * 16)

            @block.tensor
            def _(e):
                run("pe", e)

            @block.scalar
            def _(e):
                run("act", e)

            @block.vector
            def _(e):
                run("dve", e)

            @block.gpsimd
            def _(e):
                run("pool", e)

            @block.sync
            def _(e):
                run("sp", e)


class Builder:
    def __init__(self, nc, nl):
        self.nc = nc
        self.nl = nl
        self.S = Sched()
        self.top = 0
        self.arena = None
        self.psum = None
        self.wreq = []
        self.wi = 0
        self.wissued = 0
        self.out_dmas = []

    def alloc(self, nbytes):
        off = (self.top + CELL - 1) // CELL * CELL
        self.top = off + nbytes
        assert self.top <= ARENA_KB * 1024, ("SBUF arena overflow", self.top)
        return off

    def view(self, off, dt, shape):
        n = int(np.prod(shape))
        es = _esize(dt)
        a = self.arena[:, off // 4:(off + n * es + 3) // 4]
        if dt != F32:
            a = a.bitcast(dt)
        if len(shape) == 2:
            a = a.rearrange("p (a b) -> p a b", b=shape[1])
        elif len(shape) == 3:
            a = a.rearrange("p (a b c) -> p a b c", b=shape[1], c=shape[2])
        return a

    def new(self, dt, shape):
        return self.view(self.alloc(int(np.prod(shape)) * _esize(dt)), dt, shape)

    def bank(self, b, dt=F32):
        a = self.psum[:, b * 512:(b + 1) * 512]
        if dt != F32:
            a = a.bitcast(dt)
        return a

    def mm(self, out, lhsT, rhs, start=True, stop=True):
        rd = [lhsT, rhs] + ([] if start else [out])
        return self.S.op("pe", lambda e: e.matmul(out, lhsT=lhsT, rhs=rhs, start=start, stop=stop),
                         reads=rd, writes=[out])

    def tr(self, out, in_, ident):
        return self.S.op("pe", lambda e: e.transpose(out, in_, ident), reads=[in_, ident], writes=[out])

    def act(self, out, in_, func, bias=None, scale=1.0, accum_out=None):
        rd = [in_]
        kw = {}
        if bias is not None:
            kw["bias"] = bias
            if not isinstance(bias, (int, float)):
                rd.append(bias)
        if not isinstance(scale, (int, float)):
            rd.append(scale)
        wr = [out]
        if accum_out is not None:
            kw["accum_out"] = accum_out
            wr.append(accum_out)
        return self.S.op("act", lambda e: e.activation(out=out, in_=in_, func=func, scale=scale, **kw),
                         reads=rd, writes=wr)

    def tt(self, eng, out, in0, in1, op):
        return self.S.op(eng, lambda e: e.tensor_tensor(out=out, in0=in0, in1=in1, op=op),
                         reads=[in0, in1], writes=[out])

    def ts(self, eng, out, in0, s1, op0, s2=None, op1=None):
        rd = [in0] + [s for s in (s1, s2) if s is not None and not isinstance(s, (int, float))]
        if op1 is None:
            return self.S.op(eng, lambda e: e.tensor_scalar(out=out, in0=in0, scalar1=s1, scalar2=None, op0=op0),
                             reads=rd, writes=[out])
        return self.S.op(eng, lambda e: e.tensor_scalar(out=out, in0=in0, scalar1=s1, scalar2=s2, op0=op0, op1=op1),
                         reads=rd, writes=[out])

    def stt(self, out, in0, scalar, in1, op0, op1):
        rd = [in0, in1] + ([] if isinstance(scalar, (int, float)) else [scalar])
        return self.S.op("dve", lambda e: e.scalar_tensor_tensor(out=out, in0=in0, scalar=scalar, in1=in1,
                                                                   op0=op0, op1=op1), reads=rd, writes=[out])

    def copy(self, eng, out, in_):
        if eng == "act":
            return self.S.op("act", lambda e: e.copy(out=out, in_=in_), reads=[in_], writes=[out])
        return self.S.op(eng, lambda e: e.tensor_copy(out=out, in_=in_), reads=[in_], writes=[out])

    def memset(self, eng, ap, val):
        return self.S.op(eng, lambda e: e.memset(ap, val), reads=[], writes=[ap])

    def dma(self, stream, out, in_, slow=False):
        if slow:
            return self.S.op(stream, lambda e: e.dma_start(out=out, in_=in_, allow_slow_non_contiguous=True),
                             reads=[in_], writes=[out], dma=True)
        return self.S.op(stream, lambda e: e.dma_start(out=out, in_=in_), reads=[in_], writes=[out], dma=True)

    def wnext(self, src, K, ncols):
        assert K * ncols * 2 <= SLOT_B
        if self.S.dry:
            self.wreq.append((src, K, ncols))
            return self.view(self.wslots[0], BF16, [K, ncols])
        i = self.wi
        self.wi += 1
        while self.wissued < min(len(self.wreq), i + AHEAD + 1):
            s2, K2, n2 = self.wreq[self.wissued]
            dst = self.view(self.wslots[self.wissued % NSLOT], BF16, [K2, n2])
            self.dma("pool", dst, s2.rearrange("(k p) c -> p k c", p=128))
            self.wissued += 1
        return self.view(self.wslots[i % NSLOT], BF16, [K, ncols])


def build_program(nl=DEPTH):
    nc = bass.Bass("TRN2", target_bir_lowering=False)
    dt_in = {}

    def din(name, shape):
        dt_in[name] = nc.dram_tensor("in_" + name, list(shape), F32, kind="ExternalInput").ap()
        return dt_in[name]

    NLW = max(nl, 1)
    xin = din("xin", [T, D])
    s0f = din("s0f", [NLW, NH * 64, 128])
    s0b = din("s0b", [NLW, NH * 64, 128])
    cvec = din("cvec", [128, 8])
    flags = din("flags", [128, 16])
    consts = din("consts", [4, 128, 128])
    pp = din("pp", [NLW, 128, NPP])
    gfin = din("gfin", [128, D])
    w_ada = din("w_ada", [NLW, D, 6 * D])
    w_in = din("w_in", [NLW, D, INC])
    w_br_ssd = din("w_br_ssd", [NLW, 2048, D])
    w_pool = din("w_pool", [NLW, 4, 256, 256])
    w_br_pool = din("w_br_pool", [NLW, D, D])
    w_spatial = din("w_spatial", [NLW, 4, 128, 128])
    w_br_gmlp = din("w_br_gmlp", [NLW, D, D])
    w_out = din("w_out", [NLW, D, D])
    w_ffn_in = din("w_ffn_in", [NLW, D, 2 * DFF])
    w_ffn_out = din("w_ffn_out", [NLW, DFF, D])
    yout = nc.dram_tensor("yout", [T, D], F32, kind="ExternalOutput").ap()
    sfo = nc.dram_tensor("sfo", [4, NLW, NH * 64, 128], F32, kind="ExternalOutput").ap()
    sbo = nc.dram_tensor("sbo", [4, NLW, NH * 64, 128], F32, kind="ExternalOutput").ap()
    ybs = nc.dram_tensor("ybs", [16, 128, T], F32, kind="Internal").ap()
    fes = nc.dram_tensor("fes", [NTILE, 128, NFE], BF16, kind="Internal").ap()

    with ExitStack() as es:
        arena_t = es.enter_context(nc.sbuf_tensor("arena", [128, ARENA_KB * 256], F32))
        psum_t = es.enter_context(nc.psum_tensor("psum", [128, 8 * 512], F32))
        B = Builder(nc, nl)
        B.arena = arena_t[:]
        B.psum = psum_t[:]
        for dry in (True, False):
            B.S.dry = dry
            B.top = 0
            B.wi = 0
            B.wissued = 0
            _program(B, dt_in, yout, sfo, sbo, ybs, fes)
        B.S.emit(nc, B.out_dmas)
    return nc


def _program(B, I, yout, sfo, sbo, ybs, fes):
    nl = B.nl
    S = B.S
    xin, s0f, s0b, cvec, flags, consts, pp, gfin = (I[k] for k in
                                                      ("xin", "s0f", "s0b", "cvec", "flags", "consts", "pp", "gfin"))
    w_in = I["w_in"]
    B.wslots = [B.alloc(SLOT_B) for _ in range(NSLOT)]
    XR = B.new(F32, [8, XRW])
    CON = B.new(F32, [4, 128])
    IDF, TRI_F, MSK_F, MSK_B = CON[:, 0, :], CON[:, 1, :], CON[:, 1, :], CON[:, 3, :]
    CONB = B.new(BF16, [4, 128])
    IDB, TRIB_I, TRIB_E = CONB[:, 0, :], CONB[:, 1, :], CONB[:, 2, :]
    ONEB = B.new(BF16, [128])
    ONEF = B.new(F32, [128])
    ZERO = B.new(F32, [128])
    FLG = B.new(F32, [16])
    EPSC = B.new(F32, [2])
    MODV = B.new(F32, [DEPTH, 48])
    GS = B.new(F32, [DEPTH, 2, 8])
    PP = B.new(F32, [NPPS])
    NEGA = B.new(F32, [64])
    WST = B.new(BF16, [4, 128])
    SF = B.new(F32, [2048])
    SBF = B.new(BF16, [2048])
    HT = B.new(BF16, [8, 2, SEGP])
    HSAVE = B.new(BF16, [8, HALO])
    CV = B.new(F32, [8])
    CVB = B.new(BF16, [8])
    PPB = B.new(F32, [48])
    base_top = B.top

    def mcol(i):
        return FLG[:, 1 + i:2 + i]

    bankrr = [0]

    def mmbank():
        b = bankrr[0] % 4
        bankrr[0] += 1
        return b

    B.dma("sp", CON, consts.rearrange("c p f -> p c f"))
    B.dma("sp", FLG, flags)
    B.copy("dve", CONB, CON)
    B.memset("dve", ONEB, 1.0)
    B.memset("dve", ONEF, 1.0)
    B.memset("dve", ZERO, 0.0)
    B.memset("dve", EPSC[:, 0:1], EPS)
    B.memset("dve", EPSC[:, 1:2], 1.0)
    B.memset("dve", XR[:, :, 0:HALO], 0.0)
    B.memset("dve", XR[:, :, XRW - HALO:XRW], 0.0)
    EPS_AP = EPSC[:, 0:1]
    ONE_AP = EPSC[:, 1:2]

    tmp0 = B.top
    IOI = B.new(I32, [96])
    OMI = B.new(I32, [2])
    OM = B.new(F32, [2])
    POSV = B.new(F32, [96])
    ANG = B.new(F32, [4, 96])
    T1 = B.new(F32, [4, 96])
    KI = B.new(I32, [4 * 96])
    T2 = B.new(F32, [4, 96])
    S.op("pool", lambda e: e.iota(IOI[:, 0:32], pattern=[[1, 32]], base=0, channel_multiplier=0), writes=[IOI[:, 0:32]])
    S.op("pool", lambda e: e.iota(IOI[:, 32:96], pattern=[[1, 64]], base=0, channel_multiplier=0), writes=[IOI[:, 32:96]])
    S.op("pool", lambda e: e.iota(OMI, pattern=[[128, 2]], base=0, channel_multiplier=1), writes=[OMI])
    B.copy("dve", POSV, IOI)
    B.copy("dve", OM, OMI)
    B.act(OM, OM, AF.Exp, scale=-math.log(10000.0) / 256.0)
    TWO_PI = 2.0 * math.pi
    for b2 in range(2):
        B.ts("dve", ANG[:, b2, :], POSV, OM[:, b2:b2 + 1], ALU.mult)
        B.ts("dve", ANG[:, 2 + b2, :], POSV, OM[:, b2:b2 + 1], ALU.mult, math.pi / 2, ALU.add)
    A2 = ANG.rearrange("p a b -> p (a b)")
    T12 = T1.rearrange("p a b -> p (a b)")
    T22 = T2.rearrange("p a b -> p (a b)")
    B.ts("dve", T12, A2, 1.0 / TWO_PI, ALU.mult)
    B.copy("dve", KI, T12)
    B.copy("dve", T12, KI)
    B.stt(T22, T12, -TWO_PI, A2, ALU.mult, ALU.add)
    B.ts("dve", T12, T22, math.pi, ALU.is_gt)
    B.stt(T22, T12, -TWO_PI, T22, ALU.mult, ALU.add)
    B.ts("dve", T12, T22, -math.pi, ALU.is_lt)
    B.stt(T22, T12, TWO_PI, T22, ALU.mult, ALU.add)
    B.ts("dve", T22, T22, 3.1415925, ALU.min, -3.1415925, ALU.max)
    PTAB = B.new(F32, [4, 96])
    B.act(PTAB.rearrange("p a b -> p (a b)"), T22, AF.Sin)
    B.ts("dve", PTAB.rearrange("p a b -> p (a b)"), PTAB.rearrange("p a b -> p (a b)"), FLG[:, 0:1], ALU.mult)
    XS = [B.new(F32, [D]) for _ in range(2)]
    for tc in range(16):
        xs = XS[tc % 2]
        B.dma("sp", xs, xin[tc * 128:(tc + 1) * 128, :])
        for half in range(2):
            bk = 4 + (2 * tc + half) % 4
            pb = B.bank(bk)
            for j in range(4):
                blk = half * 4 + j
                B.tr(pb[:, j * 128:(j + 1) * 128], xs[:, blk * 128:(blk + 1) * 128], IDF)
            for j in range(4):
                blk = half * 4 + j
                src = pb[:, j * 128:(j + 1) * 128].rearrange("p (r c) -> p r c", c=64)
                dst = XR[:, blk, HALO + tc * 128:HALO + (tc + 1) * 128].rearrange("p (r c) -> p r c", c=64)
                if blk < 4:
                    pv = PTAB[:, blk, 2 * tc:2 * tc + 2].unsqueeze(2).broadcast_to([128, 2, 64])
                else:
                    pv = PTAB[:, blk - 4, 32:96].unsqueeze(1).broadcast_to([128, 2, 64])
                B.tt("dve", dst, src, pv, ALU.add)
    B.dma("sp", CV, cvec)
    B.act(CVB, CV, AF.Silu)

    def mod_piece(l, cb2):
        if cb2 == 0:
            B.dma("sp", PPB, pp[l, :, PP_BADA:PP_BADA + 48])
        pb = B.bank(6)[:, 510:512]
        wv = B.wnext(I["w_ada"][l, :, cb2 * 256:(cb2 + 1) * 256], 8, 256)
        for j in range(2):
            for k in range(8):
                B.mm(pb[:, j:j + 1], wv[:, k, j * 128:(j + 1) * 128], CVB[:, k:k + 1], start=(k == 0), stop=(k == 7))
        B.tt("dve", MODV[:, l, 2 * cb2:2 * cb2 + 2], pb, PPB[:, 2 * cb2:2 * cb2 + 2], ALU.add)

    for cb2 in range(24):
        mod_piece(0, cb2)
    B.top = base_top

    def modnorm(l, which, tau, mask_halo, fix_left=False):
        m0 = B.top
        SQ = B.new(BF16, [8, SEGP])
        LNV = B.new(F32, [SEGP])
        RSTD = B.new(F32, [SEGP])
        TMP = [B.new(F32, [SEGP]) for _ in range(2)]
        sh0 = 0 if which == 0 else 24
        for s in range(2):
            t0 = tau * TT + s * SEG
            xw = XR[:, :, t0:t0 + SEGP]
            B.act(SQ, xw, AF.Square)
            pb = B.bank(4 + s)
            for k in range(8):
                B.mm(pb[:, 0:SEGP], ONEB, SQ[:, k, :], start=(k == 0), stop=(k == 7))
            B.act(LNV, pb[:, 0:SEGP], AF.Ln, bias=EPS_AP, scale=1.0 / D)
            B.act(RSTD, LNV, AF.Exp, scale=-0.5)
            for k in range(8):
                tmp = TMP[k % 2]
                B.stt(tmp, XR[:, k, t0:t0 + SEGP], GS[:, l, which, k:k + 1], RSTD, ALU.mult, ALU.mult)
                B.act(HT[:, k, s, :], tmp, AF.Identity, bias=MODV[:, l, sh0 + k:sh0 + k + 1])
            if mask_halo:
                sg = 2 * tau + s
                B.ts("dve", HT[:, :, s, 0:HALO], HT[:, :, s, 0:HALO], mcol(sg), ALU.mult)
                B.ts("dve", HT[:, :, s, SEGP - HALO:SEGP], HT[:, :, s, SEGP - HALO:SEGP], mcol(sg + 1), ALU.mult)
        if fix_left:
            if tau >= 1:
                B.ts("dve", HT[:, :, 0, 0:HALO], HSAVE, mcol(2 * tau), ALU.mult)
            B.copy("dve", HSAVE, HT[:, :, 1, SEG:SEG + HALO])
        B.top = m0

    def layer_params(l):
        B.dma("sp", PP, pp[l, :, 0:NPPS])
        for which in range(2):
            sc0 = 8 if which == 0 else 32
            g0 = PP_G1 if which == 0 else PP_G2
            B.stt(GS[:, l, which, :], MODV[:, l, sc0:sc0 + 8], 1.0, PP[:, g0:g0 + 8], ALU.add, ALU.mult)
        B.act(NEGA, PP[:, PP_ALOG:PP_ALOG + 64], AF.Exp)
        B.ts("dve", NEGA, NEGA, -1.0, ALU.mult)
        m0 = B.top
        WSN = B.new(BF16, [4, 128])
        B.dma("pool", WSN, I["w_spatial"][l].rearrange("g i j -> i g j"))
        pb = B.bank(6, BF16)
        for g in range(4):
            B.tr(pb[:, g * 128:(g + 1) * 128], WSN[:, g, :], IDB)
        B.copy("dve", WST.rearrange("p a b -> p (a b)"), pb[:, 0:512])
        B.top = m0

    def load_state(l, src):
        m0 = B.top
        ST = [B.new(F32, [128]) for _ in range(2)]
        for blk in range(16):
            st = ST[blk % 2]
            B.dma("sp", st, src[l, blk * 128:(blk + 1) * 128, :])
            pb = B.bank(6 + blk % 2)
            B.tr(pb[:, 0:128], st, IDF)
            B.copy("act", SF[:, blk * 128:(blk + 1) * 128], pb[:, 0:128])
        B.copy("act", SBF, SF)
        B.top = m0

    v3 = lambda a: a.rearrange("p (c h) -> p c h", h=32)
    h64 = lambda a: a.rearrange("p (h x) -> p h x", x=64)

    def sweep(l, d):
        tiles = range(NTILE) if d == 0 else range(NTILE - 1, -1, -1)
        modq = [0]
        load_state(l, s0f if d == 0 else s0b)
        for tau in tiles:
            if d == 1:
                modnorm(l, 0, tau, True)
            tile_top = B.top
            if d == 0:
                YN_off = B.alloc(16 * TT * 2)
                YN = B.view(YN_off, BF16, [16, TT])
            pos_m = B.top
            XT = B.new(BF16, [16, TT])
            BT = B.new(BF16, [4, TT])
            CT = B.new(BF16, [4, TT])
            ACC = [B.new(F32, [SEG]) for _ in range(8)] if d == 1 else None
            CW0 = PP_CW
            ai = 0
            if d == 0:
                B.dma("sp", HT.rearrange("p a b c -> p (a b c)"), fes[tau, :, 24 * 512:NFE])
                B.dma("sp", XT.rearrange("p a b -> p (a b)"), fes[tau, :, 0:16 * 512])
                B.dma("sp", BT.rearrange("p a b -> p (a b)"), fes[tau, :, 16 * 512:20 * 512])
                B.dma("sp", CT.rearrange("p a b -> p (a b)"), fes[tau, :, 20 * 512:24 * 512])
            pend_silu = []

            def flush_silu():
                while pend_silu:
                    for (cb, s, pb, acc) in pend_silu.pop(0):
                        if cb < 16:
                            dst = XT[:, cb, s * SEG:(s + 1) * SEG]
                        elif cb < 20:
                            dst = BT[:, cb - 16, s * SEG:(s + 1) * SEG]
                        else:
                            dst = CT[:, cb - 20, s * SEG:(s + 1) * SEG]
                        B.act(dst, acc, AF.Silu)

            for cb2 in (range(12) if d == 1 else ()):
                wv = B.wnext(w_in[l, :, C_X + cb2 * 256:C_X + (cb2 + 1) * 256], 8, 256)
                o0 = HALO - 2
                ch = []
                for j in range(2):
                    cb = cb2 * 2 + j
                    for s in range(2):
                        pb = B.bank((cb2 % 2) * 4 + j * 2 + s)
                        for k in range(8):
                            B.mm(pb[:, 0:SEGP], wv[:, k, j * 128:(j + 1) * 128], HT[:, k, s, :],
                                 start=(k == 0), stop=(k == 7))
                        ch.append((cb, s, pb, ACC[(cb2 % 2) * 4 + j * 2 + s]))
                for (cb, s, pb, acc) in ch:
                    B.act(acc, pb[:, o0:o0 + SEG], AF.Identity, bias=PP[:, PP_CB + cb:PP_CB + cb + 1],
                          scale=PP[:, CW0 + cb:CW0 + cb + 1])
                flush_silu()
                for tap in range(1, 5):
                    for (cb, s, pb, acc) in ch:
                        B.stt(acc, pb[:, o0 + tap:o0 + tap + SEG],
                              PP[:, CW0 + tap * 24 + cb:CW0 + tap * 24 + cb + 1], acc, ALU.mult, ALU.add)
                pend_silu.append(ch)
                if cb2 == 11:
                    flush_silu()
            if d == 1:
                B.dma("sp", fes[tau, :, 24 * 512:NFE], HT.rearrange("p a b c -> p (a b c)"))
                B.dma("sp", fes[tau, :, 0:16 * 512], XT.rearrange("p a b -> p (a b)"))
                B.dma("sp", fes[tau, :, 16 * 512:20 * 512], BT.rearrange("p a b -> p (a b)"))
                B.dma("sp", fes[tau, :, 20 * 512:24 * 512], CT.rearrange("p a b -> p (a b)"))
            DTP = B.new(F32, [128])
            DT_ = B.new(F32, [128])
            DLA = B.new(F32, [128])
            ACU = B.new(F32, [128])
            TOT = B.new(F32, [128])
            WV = B.new(F32, [128])
            DEC = B.new(F32, [128])
            HI = B.new(BF16, [128])
            LO = B.new(BF16, [128])
            wdt = B.wnext(w_in[l, :, C_DT + 32 * d:C_DT + 32 * d + 32], 8, 32)
            pb = B.bank(6)
            for c in range(4):
                s, c0 = c // 2, HALO + 128 * (c % 2)
                for k in range(8):
                    B.mm(pb[:, c * 32:(c + 1) * 32], HT[:, k, s, c0:c0 + 128], wdt[:, k, :],
                         start=(k == 0), stop=(k == 7))
            bb = PP[:, PP_DTB + 32 * d:PP_DTB + 32 * d + 32].unsqueeze(1).broadcast_to([128, 4, 32])
            na = NEGA[:, 32 * d:32 * d + 32].unsqueeze(1).broadcast_to([128, 4, 32])
            B.tt("dve", v3(DTP), v3(pb[:, 0:128]), bb, ALU.add)
            B.act(DTP, DTP, AF.Exp)
            B.act(DT_, DTP, AF.Ln, bias=ONE_AP, scale=1.0)
            B.tt("dve", v3(DLA), v3(DT_), na, ALU.mult)
            pb = B.bank(7)
            B.mm(pb[:, 0:128], TRI_F, DLA)
            B.mm(pb[:, 128:256], ONEF, DLA)
            B.copy("act", ACU, pb[:, 0:128])
            B.copy("act", TOT, pb[:, 128:256])
            B.act(DEC, TOT, AF.Exp)
            if d == 0:
                tri_b, msk = TRIB_I, MSK_F
            else:
                B.tt("dve", ACU, DLA, ACU, ALU.subtract)
                B.tt("dve", ACU, ACU, TOT, ALU.add)
                tri_b, msk = CONB[:, 3, :], MSK_B
            B.tt("dve", WV, TOT, ACU, ALU.subtract)
            B.act(WV, WV, AF.Exp)
            B.tt("dve", WV, WV, DT_, ALU.mult)
            B.copy("dve", HI, DLA)
            B.tt("dve", LO, DLA, HI, ALU.subtract)
            CLB = ACU
            BK = B.new(BF16, [4, 512])
            for cp in range(2):
                pbb = B.bank(4 + cp, BF16)
                for c in (2 * cp, 2 * cp + 1):
                    for g in range(4):
                        o_ = ((c % 2) * 4 + g) * 128
                        B.tr(pbb[:, o_:o_ + 128], BT[:, g, c * 128:(c + 1) * 128], IDB)
                B.copy("act", BK[:, 2 * cp:2 * cp + 2, :].rearrange("p a b -> p (a b)"), pbb[:, 0:1024])
            xkc_off = B.alloc(2 * 4096)
            XKC = [B.view(xkc_off + i * 4096, BF16, [2048]) for i in range(2)]
            YG = B.view(xkc_off, F32, [4, TT]) if d == 0 else B.new(F32, [4, TT])
            CBM = [B.new(BF16, [128]) for _ in range(2)]
            SD4 = [B.new(F32, [512]) for _ in range(2)]
            SDB = [B.new(BF16, [512]) for _ in range(2)]
            ER4 = [B.new(BF16, [512]) for _ in range(2)]
            MR = [B.new(BF16, [4, 128]) for _ in range(4)]
            XDT = [B.new(BF16, [512]) for _ in range(2)]
            r3_off = B.alloc(8192)
            MR += [B.view(r3_off + i * 1024, BF16, [4, 128]) for i in range(2)]
            XDT.append(B.view(r3_off + 2048, BF16, [512]))
            CSR = [B.new(BF16, [4, 128]) for _ in range(4)]
            XW = [B.new(BF16, [512]) for _ in range(2)]
            CSR += [B.view(r3_off + 3072 + i * 1024, BF16, [4, 128]) for i in range(2)]
            XW.append(B.view(r3_off + 5120, BF16, [512]))
            STG4 = B.new(F32, [512])
            if d == 0:
                YBT = [B.new(F32, [TT]) for _ in range(2)]
                ZS2 = [B.view(r3_off + i * 2048, F32, [TT]) for i in range(2)]
                SQG2 = [B.view(r3_off + 4096 + i * 1024, BF16, [TT]) for i in range(4)]
                LNG = B.new(F32, [TT])
            corder = list(range(4)) if d == 0 else [3, 2, 1, 0]
            qd = [0]

            def stageA(g, c, it):
                cols = slice(c * 128, (c + 1) * 128)
                pcb = B.bank(6)
                cbo = (it % 2) * 128
                B.mm(pcb[:, cbo:cbo + 128], BT[:, g, cols], CT[:, g, cols])
                cbm = CBM[it % 2]
                B.tt("dve", cbm, pcb[:, cbo:cbo + 128], msk, ALU.mult)
                B.tt("pool", h64(XW[it % 3]), h64(XKC[c % 2][:, g * 512:(g + 1) * 512]),
                     WV[:, c * 32 + 8 * g:c * 32 + 8 * g + 8].unsqueeze(2).broadcast_to([128, 8, 64]), ALU.mult)
                B.tt("pool", h64(XDT[it % 3]), h64(XKC[c % 2][:, g * 512:(g + 1) * 512]),
                     DT_[:, c * 32 + 8 * g:c * 32 + 8 * g + 8].unsqueeze(2).broadcast_to([128, 8, 64]), ALU.mult)
                for quad in range(2):
                    pab = B.bank((it % 2) * 2 + quad)
                    sd = SD4[qd[0] % 2]
                    er = ER4[qd[0] % 2]
                    qd[0] += 1
                    cs = CSR[(it % 3) * 2 + quad]
                    for e4 in range(4):
                        q = c * 32 + 8 * g + quad * 4 + e4
                        ab = pab[:, e4 * 128:(e4 + 1) * 128]
                        B.mm(ab, HI[:, q:q + 1].broadcast_to([128, 128]), tri_b, start=True, stop=False)
                        B.mm(ab, LO[:, q:q + 1].broadcast_to([128, 128]), tri_b, start=False, stop=True)
                    for e4 in range(4):
                        q = c * 32 + 8 * g + quad * 4 + e4
                        ab = pab[:, e4 * 128:(e4 + 1) * 128]
                        B.act(sd[:, e4 * 128:(e4 + 1) * 128], ab, AF.Relu, bias=CLB[:, q:q + 1], scale=-1.0)
                    sdb = SDB[(qd[0] - 1) % 2]
                    B.act(sdb, sd, AF.Exp, scale=-1.0)
                    B.act(er, pab, AF.Exp)
                    B.tt("dve", MR[(it % 3) * 2 + quad], sdb.rearrange("p (a b) -> p a b", b=128),
                         cbm.unsqueeze(1).broadcast_to([128, 4, 128]), ALU.mult)
                    B.tt("dve", cs, CT[:, g, cols].unsqueeze(1).broadcast_to([128, 4, 128]),
                         er.rearrange("p (a b) -> p a b", b=128), ALU.mult)

            def stageB(g, c, it):
                cols = slice(c * 128, (c + 1) * 128)
                py4 = B.bank(4 + it % 2)
                for hp in range(4):
                    py = py4[:, hp * 128:(hp + 1) * 128]
                    for e in range(2):
                        e8 = 2 * hp + e
                        h = 8 * g + e8
                        B.mm(py[e * 64:(e + 1) * 64, :], XDT[it % 3][:, e8 * 64:(e8 + 1) * 64],
                             MR[(it % 3) * 2 + e8 // 4][:, e8 % 4, :], start=True, stop=False)
                        B.mm(py[e * 64:(e + 1) * 64, :], SBF[:, h * 64:(h + 1) * 64],
                             CSR[(it % 3) * 2 + e8 // 4][:, e8 % 4, :], start=False, stop=True)
                if d == 0:
                    for hp in range(4):
                        blk = 4 * g + hp
                        B.stt(YN[:, blk, cols], XT[:, blk, cols], PP[:, PP_DSK + blk:PP_DSK + blk + 1],
                              py4[:, hp * 128:(hp + 1) * 128], ALU.mult, ALU.add)
                if d == 1:
                    ygr = YG[:, :, (it % 4) * 128:(it % 4 + 1) * 128]
                    B.copy("dve", ygr, py4.rearrange("p (a b) -> p a b", b=128))
                    t_a = tau * TT + c * 128
                    B.dma("sp", ybs[4 * g:4 * g + 4, :, t_a:t_a + 128].rearrange("b p t -> p b t"), ygr)
                pl = B.bank(7)
                B.mm(pl, BK[:, c, g * 128:(g + 1) * 128], XW[it % 3])
                sg_ = SF[:, g * 512:(g + 1) * 512]
                B.tt("pool", h64(sg_), h64(sg_),
                     DEC[:, c * 32 + 8 * g:c * 32 + 8 * g + 8].unsqueeze(2).broadcast_to([128, 8, 64]), ALU.mult)
                B.tt("dve", sg_, sg_, pl, ALU.add)
                seg_end = (c % 2 == 1) if d == 0 else (c % 2 == 0)
                if seg_end:
                    sgi = 2 * tau + c // 2
                    if sgi < 4:
                        dst_t = sfo if d == 0 else sbo
                        ptb = B.bank(7)
                        for b4 in range(4):
                            B.tr(ptb[:, b4 * 128:(b4 + 1) * 128], sg_[:, b4 * 128:(b4 + 1) * 128], IDF)
                        B.copy("act", STG4, ptb)
                        r0 = g * 512
                        o = B.dma("sp", dst_t[sgi, l, r0:r0 + 512, :].rearrange("(b p) n -> p b n", p=128),
                                  STG4.rearrange("p (b n) -> p b n", n=128))
                        if o is not None:
                            B.out_dmas.append(o)
                    bnd = (sgi + 1) if d == 0 else sgi
                    B.ts("dve", sg_, sg_, mcol(bnd), ALU.mult)
                B.copy("act", SBF[:, g * 512:(g + 1) * 512], sg_)

            def xk_transposes(c):
                for half in range(2):
                    pbb = B.bank(4 + half, BF16)
                    for j in range(8):
                        B.tr(pbb[:, j * 128:(j + 1) * 128], XT[:, half * 8 + j, c * 128:(c + 1) * 128], IDB)
                    B.copy("act", XKC[c % 2][:, half * 1024:(half + 1) * 1024], pbb[:, 0:1024])

            its = [(g, c) for c in corder for g in range(4)]
            pend = []
            for n, (g, c) in enumerate(its):
                if g == 0:
                    xk_transposes(c)
                stageA(g, c, n)
                pend.append((g, c, n))
                if len(pend) > 2:
                    stageB(*pend.pop(0))
                if d == 1 and l + 1 < nl and modq[0] < 24:
                    mod_piece(l + 1, modq[0])
                    modq[0] += 1
            while pend:
                stageB(*pend.pop(0))
            if d == 0:
                gate_tail = []
                for g in range(4):
                    pss = B.bank(6)
                    pzs = []
                    for hp2 in range(2):
                        wv = B.wnext(w_in[l, :, C_Z + g * 512 + hp2 * 256:C_Z + g * 512 + (hp2 + 1) * 256], 8, 256)
                        for j in range(2):
                            hp = hp2 * 2 + j
                            pz = B.bank(mmbank())
                            for s in range(2):
                                for k in range(8):
                                    B.mm(pz[:, s * SEG:(s + 1) * SEG], wv[:, k, j * 128:(j + 1) * 128],
                                         HT[:, k, s, HALO:HALO + SEG], start=(k == 0), stop=(k == 7))
                            pzs.append(pz)
                    if gate_tail:
                        gate_tail.pop(0)()
                    for hp in range(4):
                        blk = 4 * g + hp
                        ybt = YBT[hp % 2]
                        B.dma("sp", ybt, ybs[blk, :, tau * TT:(tau + 1) * TT])
                        B.tt("dve", YG[:, hp, :], YN[:, blk, :], ybt, ALU.add)
                    B.act(ZS2[0], pzs[0], AF.Silu)
                    B.act(ZS2[1], pzs[1], AF.Silu)
                    for hp in range(4):
                        B.tt("dve", YG[:, hp, :], YG[:, hp, :], ZS2[hp % 2], ALU.mult)
                        if hp + 2 < 4:
                            B.act(ZS2[hp % 2], pzs[hp + 2], AF.Silu)
                        B.act(SQG2[hp], YG[:, hp, :], AF.Square)

                    def tail(g=g, pss=pss):
                        for hp in range(4):
                            B.mm(pss, ONEB, SQG2[hp], start=(hp == 0), stop=(hp == 3))
                        B.act(LNG, pss, AF.Ln, bias=EPS_AP, scale=1.0 / 512)
                        B.act(LNG, LNG, AF.Exp, scale=-0.5)
                        for hp in range(4):
                            blk = 4 * g + hp
                            B.stt(YN[:, blk, :], YG[:, hp, :], PP[:, PP_GSSD + blk:PP_GSSD + blk + 1], LNG,
                                  ALU.mult, ALU.mult)
                    gate_tail.append(tail)
                while gate_tail:
                    gate_tail.pop(0)()
            if d == 1:
                B.top = tile_top
                continue
            B.top = pos_m
            MRG = B.new(F32, [8, TT])
            MB = B.new(BF16, [8, TT])
            GA = [B.new(BF16, [TT]) for _ in range(2)]
            TMPF = [B.new(F32, [TT]) for _ in range(2)]
            br_top = B.top
            gate_w = {}

            def branch_out(wsrc, K, rhs_of, goff, mode, hook=None):
                for jo in range(8):
                    if hook is not None:
                        hook(jo)
                    if jo % 2 == 0:
                        gate_w[0] = B.wnext(w_in[l, :, goff + jo * 128:goff + (jo + 2) * 128], 8, 256)
                    wvg = gate_w[0]
                    pg = B.bank(mmbank())
                    for s in range(2):
                        for k in range(8):
                            B.mm(pg[:, s * SEG:(s + 1) * SEG], wvg[:, k, (jo % 2) * 128:(jo % 2) * 128 + 128],
                                 HT[:, k, s, HALO:HALO + SEG], start=(k == 0), stop=(k == 7))
                    ga = GA[jo % 2]
                    B.act(ga, pg, AF.Sigmoid)
                    pbr = B.bank(mmbank())
                    for kh in range((K + 15) // 16):
                        k0 = kh * 16
                        kk = min(16, K - k0)
                        wv = B.wnext(wsrc[k0 * 128:(k0 + kk) * 128, jo * 128:(jo + 1) * 128], kk, 128)
                        for k in range(kk):
                            B.mm(pbr, wv[:, k, :], rhs_of(k0 + k), start=(k0 + k == 0), stop=(k0 + k == K - 1))
                    if mode == 0:
                        B.tt("dve", MRG[:, jo, :], pbr, ga, ALU.mult)
                    else:
                        tf = TMPF[jo % 2]
                        B.tt("dve", tf, pbr, ga, ALU.mult)
                        B.tt("dve", MRG[:, jo, :] if mode == 1 else MB[:, jo, :], MRG[:, jo, :], tf, ALU.add)

            VI = B.new(F32, [2, SEGP])
            CN = [B.new(F32, [SEGP]) for _ in range(4)]
            RC = B.new(F32, [4, 2, SEG])
            for s in range(2):
                sg = 2 * tau + s
                B.memset("dve", VI[:, s, :], 1.0)
                B.ts("dve", VI[:, s, 0:HALO], VI[:, s, 0:HALO], mcol(sg), ALU.mult)
                B.ts("dve", VI[:, s, SEGP - HALO:SEGP], VI[:, s, SEGP - HALO:SEGP], mcol(sg + 1), ALU.mult)
                B.tt("dve", CN[0][:, 1:SEGP], VI[:, s, 0:SEGP - 1], VI[:, s, 1:SEGP], ALU.add)
                B.tt("dve", CN[1][:, 2:SEGP - 1], CN[0][:, 1:SEGP - 2], CN[0][:, 3:SEGP], ALU.add)
                B.tt("dve", CN[2][:, 4:SEGP - 3], CN[1][:, 2:SEGP - 5], CN[1][:, 6:SEGP - 1], ALU.add)
                B.tt("dve", CN[3][:, 8:SEGP - 7], CN[2][:, 4:SEGP - 11], CN[2][:, 12:SEGP - 3], ALU.add)
                for wi in range(4):
                    S.op("dve", (lambda o_, i_: (lambda e: e.reciprocal(out=o_, in_=i_)))(
                        RC[:, wi, s, :], CN[wi][:, HALO:HALO + SEG]),
                        reads=[CN[wi][:, HALO:HALO + SEG]], writes=[RC[:, wi, s, :]])
            PL = B.new(BF16, [8, TT])
            P0 = [B.new(F32, [SEGP]) for _ in range(2)]
            SM = [B.new(F32, [SEGP]) for _ in range(4)]
            pst = {"pi": 0, "wv": None}

            def pool_chain(cb):
                if cb % 2 == 0:
                    pst["wv"] = B.wnext(w_in[l, :, C_P + (cb // 2) * 256:C_P + (cb // 2 + 1) * 256], 8, 256)
                wv = pst["wv"]
                j = cb % 2
                wi = cb // 2
                for s in range(2):
                    pb = B.bank(mmbank())
                    for k in range(8):
                        B.mm(pb[:, 0:SEGP], wv[:, k, j * 128:(j + 1) * 128], HT[:, k, s, :],
                             start=(k == 0), stop=(k == 7))
                    p0 = P0[pst["pi"] % 2]
                    pst["pi"] += 1
                    B.copy("act", p0, pb[:, 0:SEGP])
                    B.tt("dve", SM[0][:, 1:SEGP], p0[:, 0:SEGP - 1], p0[:, 1:SEGP], ALU.add)
                    if wi >= 1:
                        B.tt("dve", SM[1][:, 2:SEGP - 1], SM[0][:, 1:SEGP - 2], SM[0][:, 3:SEGP], ALU.add)
                    if wi >= 2:
                        B.tt("dve", SM[2][:, 4:SEGP - 3], SM[1][:, 2:SEGP - 5], SM[1][:, 6:SEGP - 1], ALU.add)
                    if wi >= 3:
                        B.tt("dve", SM[3][:, 8:SEGP - 7], SM[2][:, 4:SEGP - 11], SM[2][:, 12:SEGP - 3], ALU.add)
                    sm = SM[wi]
                    B.tt("dve", sm[:, HALO:HALO + SEG], sm[:, HALO:HALO + SEG], RC[:, wi, s, :], ALU.mult)
                    B.tt("dve", PL[:, cb, s * SEG:(s + 1) * SEG], sm[:, HALO:HALO + SEG],
                         p0[:, HALO:HALO + SEG], ALU.subtract)

            branch_out(I["w_br_ssd"][l], 16, lambda k: YN[:, k, :], C_G, 0, hook=pool_chain)
            VW = B.view(YN_off, BF16, [8, 1024])
            for q4 in range(4):
                B.dma("pool", VW[:, :, q4 * 256:(q4 + 1) * 256],
                      w_in[l, :, C_V + q4 * 256:C_V + (q4 + 1) * 256].rearrange("(k p) c -> p k c", p=128))
            YP = B.new(BF16, [8, TT])
            wpv = B.wnext(I["w_pool"][l].rearrange("g r c -> (g r) c"), 8, 256)
            for g4 in range(4):
                for ob in range(2):
                    pb = B.bank(mmbank())
                    for kb in range(2):
                        B.mm(pb, wpv[:, g4 * 2 + kb, ob * 128:(ob + 1) * 128], PL[:, 2 * g4 + kb, :],
                             start=(kb == 0), stop=(kb == 1))
                    blk = 2 * g4 + ob
                    B.act(YP[:, blk, :], pb, AF.Identity, scale=PP[:, PP_PSC + blk:PP_PSC + blk + 1])
            branch_out(I["w_br_pool"][l], 8, lambda k: YP[:, k, :], C_G + 1024, 1)
            B.top = br_top
            GSGU = B.new(F32, [1024])
            BSP = B.new(F32, [512])
            B.dma("sp", GSGU, pp[l, :, PP_GSGU:PP_GSGU + 1024])
            B.dma("sp", BSP, pp[l, :, PP_BSP:PP_BSP + 512])
            VN = B.new(BF16, [4, 1024])
            UT = B.new(BF16, [8, TT])
            YM = B.new(BF16, [8, TT])
            SSV = B.new(F32, [8])
            JNK = B.new(BF16, [1024])
            for c in range(4):
                s, c0 = c // 2, HALO + 128 * (c % 2)
                b0 = (c % 2) * 2
                for q4 in range(4):
                    pb = B.bank(b0 + q4 // 2)
                    for k in range(8):
                        B.mm(pb[:, (q4 % 2) * 256:(q4 % 2) * 256 + 256], HT[:, k, s, c0:c0 + 128],
                             VW[:, k, q4 * 256:(q4 + 1) * 256], start=(k == 0), stop=(k == 7))
                pv = B.psum[:, b0 * 512:(b0 + 2) * 512]
                B.act(JNK, pv, AF.Square, accum_out=SSV[:, c:c + 1])
                B.act(SSV[:, 4 + c:5 + c], SSV[:, c:c + 1], AF.Ln, bias=EPS_AP, scale=1.0 / D)
                B.act(SSV[:, 4 + c:5 + c], SSV[:, 4 + c:5 + c], AF.Exp, scale=-0.5)
                B.stt(VN[:, c, :], pv, SSV[:, 4 + c:5 + c], GSGU, ALU.mult, ALU.mult)
            for cb2 in range(4):
                wv = B.wnext(w_in[l, :, C_U + cb2 * 256:C_U + (cb2 + 1) * 256], 8, 256)
                for j in range(2):
                    cb = cb2 * 2 + j
                    pu = B.bank(mmbank())
                    for s in range(2):
                        for k in range(8):
                            B.mm(pu[:, s * SEG:(s + 1) * SEG], wv[:, k, j * 128:(j + 1) * 128],
                                 HT[:, k, s, HALO:HALO + SEG], start=(k == 0), stop=(k == 7))
                    B.copy("act", UT[:, cb, :], pu)
                    g4 = cb // 2
                    psv = B.bank(mmbank())
                    for c in range(4):
                        B.mm(psv[:, c * 128:(c + 1) * 128], VN[:, c, cb * 128:(cb + 1) * 128], WST[:, g4, :])
                    tf = TMPF[cb % 2]
                    B.tt("dve", tf.rearrange("p (c i) -> p c i", i=128), psv.rearrange("p (c i) -> p c i", i=128),
                         BSP[:, g4 * 128:(g4 + 1) * 128].unsqueeze(1).broadcast_to([128, 4, 128]), ALU.add)
                    B.tt("dve", YM[:, cb, :], tf, UT[:, cb, :], ALU.mult)
            branch_out(I["w_br_gmlp"][l], 8, lambda k: YM[:, k, :], C_G + 2048, 2)
            tcol = slice(HALO + tau * TT, HALO + (tau + 1) * TT)
            for jo in range(8):
                wv = B.wnext(I["w_out"][l][:, jo * 128:(jo + 1) * 128], 8, 128)
                po = B.bank(mmbank())
                for k in range(8):
                    B.mm(po, wv[:, k, :], MB[:, k, :], start=(k == 0), stop=(k == 7))
                B.stt(XR[:, jo, tcol], po, MODV[:, l, 16 + jo:17 + jo], XR[:, jo, tcol], ALU.mult, ALU.add)
            B.top = tile_top
            modnorm(l, 1, tau, False)
            HID = B.new(BF16, [22, TT])
            SGT = [B.new(F32, [TT]) for _ in range(2)]
            for j2 in range(11):
                wg = B.wnext(I["w_ffn_in"][l][:, j2 * 256:(j2 + 1) * 256], 8, 256)
                wu = B.wnext(I["w_ffn_in"][l][:, DFF + j2 * 256:DFF + (j2 + 1) * 256], 8, 256)
                for j in range(2):
                    jj = j2 * 2 + j
                    pg = B.bank(mmbank())
                    pu = B.bank(mmbank())
                    for (pp_, wv) in ((pg, wg), (pu, wu)):
                        for s in range(2):
                            for k in range(8):
                                B.mm(pp_[:, s * SEG:(s + 1) * SEG], wv[:, k, j * 128:(j + 1) * 128],
                                     HT[:, k, s, HALO:HALO + SEG], start=(k == 0), stop=(k == 7))
                    sg_ = SGT[jj % 2]
                    B.act(sg_, pg, AF.Silu)
                    B.tt("dve", HID[:, jj, :], pu, sg_, ALU.mult)
            for jo in range(8):
                po = B.bank(mmbank())
                for kh in range(2):
                    wv = B.wnext(I["w_ffn_out"][l][kh * 11 * 128:(kh + 1) * 11 * 128, jo * 128:(jo + 1) * 128], 11, 128)
                    for k in range(11):
                        B.mm(po, wv[:, k, :], HID[:, kh * 11 + k, :], start=(kh == 0 and k == 0),
                             stop=(kh == 1 and k == 10))
                B.stt(XR[:, jo, tcol], po, MODV[:, l, 40 + jo:41 + jo], XR[:, jo, tcol], ALU.mult, ALU.add)
            B.top = tile_top

    for l in range(nl):
        layer_params(l)
        sweep(l, 1)
        sweep(l, 0)

    GF = B.new(F32, [D])
    B.dma("sp", GF, gfin)
    OS = [B.new(F32, [D]) for _ in range(2)]
    SSO = B.new(F32, [32])
    JN2 = B.new(BF16, [D])
    for tc in range(16):
        b0 = (tc % 2) * 2
        for blk in range(8):
            pb = B.bank(b0 + blk // 4)
            B.tr(pb[:, (blk % 4) * 128:(blk % 4) * 128 + 128], XR[:, blk, HALO + tc * 128:HALO + (tc + 1) * 128], IDF)
        pv = B.psum[:, b0 * 512:(b0 + 2) * 512]
        B.act(JN2, pv, AF.Square, accum_out=SSO[:, tc:tc + 1])
        B.act(SSO[:, 16 + tc:17 + tc], SSO[:, tc:tc + 1], AF.Ln, bias=EPS_AP, scale=1.0 / D)
        B.act(SSO[:, 16 + tc:17 + tc], SSO[:, 16 + tc:17 + tc], AF.Exp, scale=-0.5)
        osb = OS[tc % 2]
        B.stt(osb, pv, SSO[:, 16 + tc:17 + tc], GF, ALU.mult, ALU.mult)
        o = B.dma("sp", yout[tc * 128:(tc + 1) * 128, :], osb)
        if o is not None:
            B.out_dmas.append(o)


_CACHE = {}


def _consts():
    j = np.arange(128)[:, None]
    i = np.arange(128)[None, :]
    return np.stack([np.eye(128), (j <= i), (j < i), (j >= i)]).astype(np.float32)


def _pack_pp(inp, l):
    f = np.float32
    pm = lambda v: np.ascontiguousarray(np.asarray(v, f).reshape(-1, 128).T)
    cols = [pm(inp["g_norm1"][l]), pm(inp["g_norm2"][l]), pm(inp["b_ada"][l])]
    cw = np.asarray(inp["conv_w"][l], f)[:, 2048 - 2048:]
    cols.append(np.concatenate([pm(cw[t]) for t in range(5)], axis=1))
    cols.append(pm(inp["conv_b"][l]))
    cols.append(pm(np.repeat(np.asarray(inp["d_skip"][l], f), 64)))
    cols.append(pm(inp["g_ssd"][l]))
    cols.append(pm(inp["pool_scale"][l]))
    cols.append(np.broadcast_to(np.asarray(inp["dt_bias"][l], f).reshape(1, 64), (128, 64)))
    cols.append(np.broadcast_to(np.asarray(inp["a_log"][l], f).reshape(1, 64), (128, 64)))
    cols.append(np.broadcast_to(np.asarray(inp["g_sgu"][l], f).reshape(1, 1024), (128, 1024)))
    cols.append(np.broadcast_to(np.asarray(inp["b_spatial"][l], f).reshape(1, 512), (128, 512)))
    out = np.concatenate(cols, axis=1).astype(f)
    assert out.shape == (128, NPP), out.shape
    return out


def kernel(nl=DEPTH, **inp):
    f = np.float32
    if nl not in _CACHE:
        _CACHE[nl] = build_program(nl)
    nc = _CACHE[nl]
    xp = np.asarray(inp["x_prompt"], f)
    xs = np.asarray(inp["x_sample"], f)
    NLW = max(nl, 1)
    pp = np.stack([_pack_pp(inp, l) for l in range(NLW)])
    shared = {
        "in_consts": _consts(),
        "in_pp": pp,
        "in_gfin": np.ascontiguousarray(np.broadcast_to(np.asarray(inp["g_final"], f)[None, :], (128, D))),
    }
    for k in ("w_ada", "w_in", "w_br_ssd", "w_pool", "w_br_pool", "w_spatial", "w_br_gmlp", "w_out",
              "w_ffn_in", "w_ffn_out"):
        shared["in_" + k] = np.ascontiguousarray(np.asarray(inp[k], f)[:NLW])
    in_maps = []
    zst = np.zeros((NLW, NH * 64, 128), f)
    for core in range(8):
        m = dict(shared)
        fl = np.zeros((128, 16), f)
        if core < 4:
            m["in_xin"] = np.ascontiguousarray(xs[core])
            m["in_s0f"] = np.ascontiguousarray(np.asarray(inp["state_ssd_fwd"], f)[core, :NLW].reshape(NLW, NH * 64, 128))
            m["in_s0b"] = np.ascontiguousarray(np.asarray(inp["state_ssd_bwd"], f)[core, :NLW].reshape(NLW, NH * 64, 128))
            cv = np.asarray(inp["c"], f)[core]
            fl[:, 0] = 1.0
            fl[:, 2:9] = 1.0
        else:
            q = core - 4
            xx = np.zeros((T, D), f)
            xx[:1024] = xp[4 * q:4 * q + 4].reshape(1024, D)
            m["in_xin"] = xx
            m["in_s0f"] = zst
            m["in_s0b"] = zst
            cv = np.asarray(inp["c_ctx"], f)
        m["in_cvec"] = np.ascontiguousarray(cv.reshape(8, 128).T)
        m["in_flags"] = fl
        in_maps.append(m)
    res = run_bass_kernel_spmd(nc, in_maps, core_ids=list(range(8)))
    R = res.results
    y_sample = np.stack([R[c]["yout"] for c in range(4)]).astype(f)
    y_prompt = np.concatenate([R[4 + q]["yout"][:1024].reshape(4, 256, D) for q in range(4)]).astype(f)
    nsf = np.concatenate([R[4 + q]["sfo"].reshape(4, NLW, NH, 64, 128) for q in range(4)]).astype(f)
    nsb = np.concatenate([R[4 + q]["sbo"].reshape(4, NLW, NH, 64, 128) for q in range(4)]).astype(f)
    return (y_prompt, y_sample, nsf, nsb)
```

```python
import math
from contextlib import ExitStack
import numpy as np
import concourse.bass as bass
import concourse.mybir as mybir
from concourse.bass_utils import run_bass_kernel_spmd

F32 = mybir.dt.float32
BF16 = mybir.dt.bfloat16
I32 = mybir.dt.int32
AF = mybir.ActivationFunctionType
ALU = mybir.AluOpType

D = 1024
DEPTH = 4
T = 2048
NTILE = 4
TT = 512
SEG = 256
HALO = 8
SEGP = SEG + 2 * HALO
XRW = T + 2 * HALO
NH = 32
DFF = 2816
INC = 11328
EPS = 1e-6
C_Z, C_X, C_B, C_C, C_DT, C_P, C_U, C_V, C_G = 0, 2048, 4096, 4608, 5120, 5184, 6208, 7232, 8256
NPP = 376 + 1024 + 512
NPPS = 376
PP_G1, PP_G2, PP_BADA, PP_CW, PP_CB, PP_DSK, PP_GSSD, PP_PSC, PP_DTB, PP_ALOG, PP_GSGU, PP_BSP = (
    0, 8, 16, 64, 184, 208, 224, 240, 248, 312, 376, 1400)
ARENA_KB = 207
CELL = 256
NSLOT = 5
SLOT_B = 4096
AHEAD = 2
NDSEM = 12
NFE = 24 * 512 + 8 * 2 * SEGP


def _esize(dt):
    return 2 if dt == BF16 else 4


class Op:
    __slots__ = ("stream", "dom", "idx", "fn", "waits", "signaled", "is_dma", "count")


class Sched:
    def __init__(self):
        self.streams = {k: [] for k in ("pe", "act", "dve", "pool", "sp")}
        self.domcnt = {}
        self.cells = {}
        self.waited = {k: {} for k in self.streams}
        self.dma_rr = {"sp": 0, "pool": 0}
        self.dma_last = {}
        self.nops = 0
        self.dry = False

    @staticmethod
    def region(ap):
        t = ap.tensor
        name = t.name
        es = _esize(ap.dtype)
        dims = [list(d) for d in ap.ap]
        cls = type(t).__name__
        if cls.startswith("DRam"):
            ext = sum((n - 1) * abs(s) for s, n in dims) + 1
            b0 = ap.offset * es
            return ("d:" + name, b0 // 65536, (b0 + ext * es - 1) // 65536 + 1, 0, 2)
        pstride, pn = dims[0]
        if pstride == 0:
            pstride = 1 << 40
        fd = dims[1:]
        ext = sum((n - 1) * abs(s) for s, n in fd) + 1
        p0 = ap.offset // pstride if pstride < (1 << 40) else 0
        c0 = ap.offset - p0 * pstride if pstride < (1 << 40) else ap.offset
        b0 = c0 * es
        b1 = (c0 + ext) * es
        p1 = p0 + pn
        h0 = 0 if p0 < 64 else 1
        h1 = 1 if p1 <= 64 else 2
        if cls.startswith("PS") or "psum" in name:
            return ("ps:" + name, b0 // 2048, (b1 - 1) // 2048 + 1, p0 // 32, (p1 - 1) // 32 + 1)
        return ("sb:" + name, b0 // CELL, (b1 - 1) // CELL + 1, h0, h1)

    def op(self, stream, fn, reads=(), writes=(), dma=False):
        self.nops += 1
        if self.dry:
            return None
        o = Op()
        o.stream = stream
        o.is_dma = dma
        o.fn = fn
        o.signaled = dma
        o.count = None
        if dma:
            k = self.dma_rr[stream] % NDSEM
            self.dma_rr[stream] += 1
            o.dom = (stream, k)
        else:
            o.dom = stream
        o.idx = self.domcnt.get(o.dom, 0)
        self.domcnt[o.dom] = o.idx + 1
        need = {}

        def want(w):
            if w is None:
                return
            if w.dom == "pe" and stream == "pe" and not dma:
                return
            cur = need.get(w.dom)
            if cur is None or cur.idx < w.idx:
                need[w.dom] = w
        if dma:
            want(self.dma_last.get(o.dom))
            self.dma_last[o.dom] = o
        cells = self.cells
        for ap in reads:
            if ap is None:
                continue
            sp, c0, c1, h0, h1 = self.region(ap)
            if sp.startswith("d:in_"):
                continue
            for c in range(c0, c1):
                for h in range(h0, h1):
                    rec = cells.get((sp, c, h))
                    if rec is None:
                        rec = [None, {}]
                        cells[(sp, c, h)] = rec
                    want(rec[0])
        for ap in writes:
            sp, c0, c1, h0, h1 = self.region(ap)
            for c in range(c0, c1):
                for h in range(h0, h1):
                    rec = cells.get((sp, c, h))
                    if rec is None:
                        rec = [None, {}]
                        cells[(sp, c, h)] = rec
                    want(rec[0])
                    for r in rec[1].values():
                        want(r)
        wl = []
        wd = self.waited[stream]
        for dom, w in need.items():
            if wd.get(dom, -1) >= w.idx:
                continue
            wd[dom] = w.idx
            w.signaled = True
            wl.append(w)
        o.waits = wl
        for ap in reads:
            if ap is None:
                continue
            sp, c0, c1, h0, h1 = self.region(ap)
            if sp.startswith("d:in_"):
                continue
            for c in range(c0, c1):
                for h in range(h0, h1):
                    rec = cells[(sp, c, h)]
                    cur = rec[1].get(o.dom)
                    if cur is None or cur.idx < o.idx:
                        rec[1][o.dom] = o
        for ap in writes:
            sp, c0, c1, h0, h1 = self.region(ap)
            for c in range(c0, c1):
                for h in range(h0, h1):
                    rec = cells[(sp, c, h)]
                    rec[0] = o
                    rec[1] = {}
        self.streams[stream].append(o)
        return o

    def emit(self, nc, final_waits):
        doms = set(self.domcnt.keys())
        for dom in doms:
            cnt = 0
            for st in self.streams.values():
                pass
        per_dom = {}
        for sname, ops in self.streams.items():
            for o in ops:
                per_dom.setdefault(o.dom, []).append(o)
        for dom, ops in per_dom.items():
            ops.sort(key=lambda x: x.idx)
            c = 0
            for o in ops:
                if o.signaled:
                    c += 1
                    o.count = c
        with ExitStack() as es:
            sems = {}
            for dom in per_dom:
                nm = dom if isinstance(dom, str) else "%s_d%d" % dom
                sems[dom] = es.enter_context(nc.semaphore("s_" + nm))
            block = es.enter_context(nc.Block())

            def run(sname, eng):
                for o in self.streams[sname]:
                    for w in o.waits:
                        inc = 16 if w.is_dma else 1
                        eng.wait_ge(sems[w.dom], w.count * inc)
                    ins = o.fn(eng)
                    if o.signaled:
                        ins.then_inc(sems[o.dom], 16 if o.is_dma else 1)
                if sname == "sp":
                    for o in final_waits:
                        eng.wait_ge(sems[o.dom], o.count * 16)

            @block.tensor
            def _(e):
                run("pe", e)

            @block.scalar
            def _(e):
                run("act", e)

            @block.vector
            def _(e):
                run("dve", e)

            @block.gpsimd
            def _(e):
                run("pool", e)

            @block.sync
            def _(e):
                run("sp", e)


class Builder:
    def __init__(self, nc, nl):
        self.nc = nc
        self.nl = nl
        self.S = Sched()
        self.top = 0
        self.arena = None
        self.psum = None
        self.wreq = []
        self.wi = 0
        self.wissued = 0
        self.out_dmas = []

    def alloc(self, nbytes):
        off = (self.top + CELL - 1) // CELL * CELL
        self.top = off + nbytes
        assert self.top <= ARENA_KB * 1024, ("SBUF arena overflow", self.top)
        return off

    def view(self, off, dt, shape):
        n = int(np.prod(shape))
        es = _esize(dt)
        a = self.arena[:, off // 4:(off + n * es + 3) // 4]
        if dt != F32:
            a = a.bitcast(dt)
        if len(shape) == 2:
            a = a.rearrange("p (a b) -> p a b", b=shape[1])
        elif len(shape) == 3:
            a = a.rearrange("p (a b c) -> p a b c", b=shape[1], c=shape[2])
        return a

    def new(self, dt, shape):
        return self.view(self.alloc(int(np.prod(shape)) * _esize(dt)), dt, shape)

    def bank(self, b, dt=F32):
        a = self.psum[:, b * 512:(b + 1) * 512]
        if dt != F32:
            a = a.bitcast(dt)
        return a

    def mm(self, out, lhsT, rhs, start=True, stop=True):
        rd = [lhsT, rhs] + ([] if start else [out])
        return self.S.op("pe", lambda e: e.matmul(out, lhsT=lhsT, rhs=rhs, start=start, stop=stop),
                         reads=rd, writes=[out])

    def tr(self, out, in_, ident):
        return self.S.op("pe", lambda e: e.transpose(out, in_, ident), reads=[in_, ident], writes=[out])

    def act(self, out, in_, func, bias=None, scale=1.0, accum_out=None):
        rd = [in_]
        kw = {}
        if bias is not None:
            kw["bias"] = bias
            if not isinstance(bias, (int, float)):
                rd.append(bias)
        if not isinstance(scale, (int, float)):
            rd.append(scale)
        wr = [out]
        if accum_out is not None:
            kw["accum_out"] = accum_out
            wr.append(accum_out)
        return self.S.op("act", lambda e: e.activation(out=out, in_=in_, func=func, scale=scale, **kw),
                         reads=rd, writes=wr)

    def tt(self, eng, out, in0, in1, op):
        return self.S.op(eng, lambda e: e.tensor_tensor(out=out, in0=in0, in1=in1, op=op),
                         reads=[in0, in1], writes=[out])

    def ts(self, eng, out, in0, s1, op0, s2=None, op1=None):
        rd = [in0] + [s for s in (s1, s2) if s is not None and not isinstance(s, (int, float))]
        if op1 is None:
            return self.S.op(eng, lambda e: e.tensor_scalar(out=out, in0=in0, scalar1=s1, scalar2=None, op0=op0),
                             reads=rd, writes=[out])
        return self.S.op(eng, lambda e: e.tensor_scalar(out=out, in0=in0, scalar1=s1, scalar2=s2, op0=op0, op1=op1),
                         reads=rd, writes=[out])

    def stt(self, out, in0, scalar, in1, op0, op1):
        rd = [in0, in1] + ([] if isinstance(scalar, (int, float)) else [scalar])
        return self.S.op("dve", lambda e: e.scalar_tensor_tensor(out=out, in0=in0, scalar=scalar, in1=in1,
                                                                   op0=op0, op1=op1), reads=rd, writes=[out])

    def copy(self, eng, out, in_):
        if eng == "act":
            return self.S.op("act", lambda e: e.copy(out=out, in_=in_), reads=[in_], writes=[out])
        return self.S.op(eng, lambda e: e.tensor_copy(out=out, in_=in_), reads=[in_], writes=[out])

    def memset(self, eng, ap, val):
        return self.S.op(eng, lambda e: e.memset(ap, val), reads=[], writes=[ap])

    def dma(self, stream, out, in_, slow=False):
        if slow:
            return self.S.op(stream, lambda e: e.dma_start(out=out, in_=in_, allow_slow_non_contiguous=True),
                             reads=[in_], writes=[out], dma=True)
        return self.S.op(stream, lambda e: e.dma_start(out=out, in_=in_), reads=[in_], writes=[out], dma=True)

    def wnext(self, src, K, ncols):
        assert K * ncols * 2 <= SLOT_B
        if self.S.dry:
            self.wreq.append((src, K, ncols))
            return self.view(self.wslots[0], BF16, [K, ncols])
        i = self.wi
        self.wi += 1
        while self.wissued < min(len(self.wreq), i + AHEAD + 1):
            s2, K2, n2 = self.wreq[self.wissued]
            dst = self.view(self.wslots[self.wissued % NSLOT], BF16, [K2, n2])
            self.dma("pool", dst, s2.rearrange("(k p) c -> p k c", p=128))
            self.wissued += 1
        return self.view(self.wslots[i % NSLOT], BF16, [K, ncols])


def build_program(nl=DEPTH):
    nc = bass.Bass("TRN2", target_bir_lowering=False)
    dt_in = {}

    def din(name, shape):
        dt_in[name] = nc.dram_tensor("in_" + name, list(shape), F32, kind="ExternalInput").ap()
        return dt_in[name]

    NLW = max(nl, 1)
    xin = din("xin", [T, D])
    s0f = din("s0f", [NLW, NH * 64, 128])
    s0b = din("s0b", [NLW, NH * 64, 128])
    cvec = din("cvec", [128, 8])
    flags = din("flags", [128, 16])
    consts = din("consts", [4, 128, 128])
    pp = din("pp", [NLW, 128, NPP])
    gfin = din("gfin", [128, D])
    w_ada = din("w_ada", [NLW, D, 6 * D])
    w_in = din("w_in", [NLW, D, INC])
    w_br_ssd = din("w_br_ssd", [NLW, 2048, D])
    w_pool = din("w_pool", [NLW, 4, 256, 256])
    w_br_pool = din("w_br_pool", [NLW, D, D])
    w_spatial = din("w_spatial", [NLW, 4, 128, 128])
    w_br_gmlp = din("w_br_gmlp", [NLW, D, D])
    w_out = din("w_out", [NLW, D, D])
    w_ffn_in = din("w_ffn_in", [NLW, D, 2 * DFF])
    w_ffn_out = din("w_ffn_out", [NLW, DFF, D])
    yout = nc.dram_tensor("yout", [T, D], F32, kind="ExternalOutput").ap()
    sfo = nc.dram_tensor("sfo", [4, NLW, NH * 64, 128], F32, kind="ExternalOutput").ap()
    sbo = nc.dram_tensor("sbo", [4, NLW, NH * 64, 128], F32, kind="ExternalOutput").ap()
    ybs = nc.dram_tensor("ybs", [16, 128, T], F32, kind="Internal").ap()
    fes = nc.dram_tensor("fes", [NTILE, 128, NFE], BF16, kind="Internal").ap()

    with ExitStack() as es:
        arena_t = es.enter_context(nc.sbuf_tensor("arena", [128, ARENA_KB * 256], F32))
        psum_t = es.enter_context(nc.psum_tensor("psum", [128, 8 * 512], F32))
        B = Builder(nc, nl)
        B.arena = arena_t[:]
        B.psum = psum_t[:]
        for dry in (True, False):
            B.S.dry = dry
            B.top = 0
            B.wi = 0
            B.wissued = 0
            _program(B, dt_in, yout, sfo, sbo, ybs, fes)
        B.S.emit(nc, B.out_dmas)
    return nc


def _program(B, I, yout, sfo, sbo, ybs, fes):
    nl = B.nl
    S = B.S
    xin, s0f, s0b, cvec, flags, consts, pp, gfin = (I[k] for k in
                                                      ("xin", "s0f", "s0b", "cvec", "flags", "consts", "pp", "gfin"))
    w_in = I["w_in"]
    B.wslots = [B.alloc(SLOT_B) for _ in range(NSLOT)]
    XR = B.new(F32, [8, XRW])
    CON = B.new(F32, [4, 128])
    IDF, TRI_F, MSK_F, MSK_B = CON[:, 0, :], CON[:, 1, :], CON[:, 1, :], CON[:, 3, :]
    CONB = B.new(BF16, [4, 128])
    IDB, TRIB_I, TRIB_E = CONB[:, 0, :], CONB[:, 1, :], CONB[:, 2, :]
    ONEB = B.new(BF16, [128])
    ONEF = B.new(F32, [128])
    ZERO = B.new(F32, [128])
    FLG = B.new(F32, [16])
    EPSC = B.new(F32, [2])
    MODV = B.new(F32, [DEPTH, 48])
    GS = B.new(F32, [DEPTH, 2, 8])
    PP = B.new(F32, [NPPS])
    NEGA = B.new(F32, [64])
    WST = B.new(BF16, [4, 128])
    SF = B.new(F32, [2048])
    SBF = B.new(BF16, [2048])
    HT = B.new(BF16, [8, 2, SEGP])
    HSAVE = B.new(BF16, [8, HALO])
    CV = B.new(F32, [8])
    CVB = B.new(BF16, [8])
    PPB = B.new(F32, [48])
    base_top = B.top

    def mcol(i):
        return FLG[:, 1 + i:2 + i]

    bankrr = [0]

    def mmbank():
        b = bankrr[0] % 4
        bankrr[0] += 1
        return b

    B.dma("sp", CON, consts.rearrange("c p f -> p c f"))
    B.dma("sp", FLG, flags)
    B.copy("dve", CONB, CON)
    B.memset("dve", ONEB, 1.0)
    B.memset("dve", ONEF, 1.0)
    B.memset("dve", ZERO, 0.0)
    B.memset("dve", EPSC[:, 0:1], EPS)
    B.memset("dve", EPSC[:, 1:2], 1.0)
    B.memset("dve", XR[:, :, 0:HALO], 0.0)
    B.memset("dve", XR[:, :, XRW - HALO:XRW], 0.0)
    EPS_AP = EPSC[:, 0:1]
    ONE_AP = EPSC[:, 1:2]

    tmp0 = B.top
    IOI = B.new(I32, [96])
    OMI = B.new(I32, [2])
    OM = B.new(F32, [2])
    POSV = B.new(F32, [96])
    ANG = B.new(F32, [4, 96])
    T1 = B.new(F32, [4, 96])
    KI = B.new(I32, [4 * 96])
    T2 = B.new(F32, [4, 96])
    S.op("pool", lambda e: e.iota(IOI[:, 0:32], pattern=[[1, 32]], base=0, channel_multiplier=0), writes=[IOI[:, 0:32]])
    S.op("pool", lambda e: e.iota(IOI[:, 32:96], pattern=[[1, 64]], base=0, channel_multiplier=0), writes=[IOI[:, 32:96]])
    S.op("pool", lambda e: e.iota(OMI, pattern=[[128, 2]], base=0, channel_multiplier=1), writes=[OMI])
    B.copy("dve", POSV, IOI)
    B.copy("dve", OM, OMI)
    B.act(OM, OM, AF.Exp, scale=-math.log(10000.0) / 256.0)
    TWO_PI = 2.0 * math.pi
    for b2 in range(2):
        B.ts("dve", ANG[:, b2, :], POSV, OM[:, b2:b2 + 1], ALU.mult)
        B.ts("dve", ANG[:, 2 + b2, :], POSV, OM[:, b2:b2 + 1], ALU.mult, math.pi / 2, ALU.add)
    A2 = ANG.rearrange("p a b -> p (a b)")
    T12 = T1.rearrange("p a b -> p (a b)")
    T22 = T2.rearrange("p a b -> p (a b)")
    B.ts("dve", T12, A2, 1.0 / TWO_PI, ALU.mult)
    B.copy("dve", KI, T12)
    B.copy("dve", T12, KI)
    B.stt(T22, T12, -TWO_PI, A2, ALU.mult, ALU.add)
    B.ts("dve", T12, T22, math.pi, ALU.is_gt)
    B.stt(T22, T12, -TWO_PI, T22, ALU.mult, ALU.add)
    B.ts("dve", T12, T22, -math.pi, ALU.is_lt)
    B.stt(T22, T12, TWO_PI, T22, ALU.mult, ALU.add)
    B.ts("dve", T22, T22, 3.1415925, ALU.min, -3.1415925, ALU.max)
    PTAB = B.new(F32, [4, 96])
    B.act(PTAB.rearrange("p a b -> p (a b)"), T22, AF.Sin)
    B.ts("dve", PTAB.rearrange("p a b -> p (a b)"), PTAB.rearrange("p a b -> p (a b)"), FLG[:, 0:1], ALU.mult)
    XS = [B.new(F32, [D]) for _ in range(2)]
    for tc in range(16):
        xs = XS[tc % 2]
        B.dma("sp", xs, xin[tc * 128:(tc + 1) * 128, :])
        for half in range(2):
            bk = 4 + (2 * tc + half) % 4
            pb = B.bank(bk)
            for j in range(4):
                blk = half * 4 + j
                B.tr(pb[:, j * 128:(j + 1) * 128], xs[:, blk * 128:(blk + 1) * 128], IDF)
            for j in range(4):
                blk = half * 4 + j
                src = pb[:, j * 128:(j + 1) * 128].rearrange("p (r c) -> p r c", c=64)
                dst = XR[:, blk, HALO + tc * 128:HALO + (tc + 1) * 128].rearrange("p (r c) -> p r c", c=64)
                if blk < 4:
                    pv = PTAB[:, blk, 2 * tc:2 * tc + 2].unsqueeze(2).broadcast_to([128, 2, 64])
                else:
                    pv = PTAB[:, blk - 4, 32:96].unsqueeze(1).broadcast_to([128, 2, 64])
                B.tt("dve", dst, src, pv, ALU.add)
    B.dma("sp", CV, cvec)
    B.act(CVB, CV, AF.Silu)

    def mod_piece(l, cb2):
        if cb2 == 0:
            B.dma("sp", PPB, pp[l, :, PP_BADA:PP_BADA + 48])
        pb = B.bank(6)[:, 510:512]
        wv = B.wnext(I["w_ada"][l, :, cb2 * 256:(cb2 + 1) * 256], 8, 256)
        for j in range(2):
            for k in range(8):
                B.mm(pb[:, j:j + 1], wv[:, k, j * 128:(j + 1) * 128], CVB[:, k:k + 1], start=(k == 0), stop=(k == 7))
        B.tt("dve", MODV[:, l, 2 * cb2:2 * cb2 + 2], pb, PPB[:, 2 * cb2:2 * cb2 + 2], ALU.add)

    for cb2 in range(24):
        mod_piece(0, cb2)
    B.top = base_top

    def modnorm(l, which, tau, mask_halo, fix_left=False):
        m0 = B.top
        SQ = B.new(BF16, [8, SEGP])
        LNV = B.new(F32, [SEGP])
        RSTD = B.new(F32, [SEGP])
        TMP = [B.new(F32, [SEGP]) for _ in range(2)]
        sh0 = 0 if which == 0 else 24
        for s in range(2):
            t0 = tau * TT + s * SEG
            xw = XR[:, :, t0:t0 + SEGP]
            B.act(SQ, xw, AF.Square)
            pb = B.bank(4 + s)
            for k in range(8):
                B.mm(pb[:, 0:SEGP], ONEB, SQ[:, k, :], start=(k == 0), stop=(k == 7))
            B.act(LNV, pb[:, 0:SEGP], AF.Ln, bias=EPS_AP, scale=1.0 / D)
            B.act(RSTD, LNV, AF.Exp, scale=-0.5)
            for k in range(8):
                tmp = TMP[k % 2]
                B.stt(tmp, XR[:, k, t0:t0 + SEGP], GS[:, l, which, k:k + 1], RSTD, ALU.mult, ALU.mult)
                B.act(HT[:, k, s, :], tmp, AF.Identity, bias=MODV[:, l, sh0 + k:sh0 + k + 1])
            if mask_halo:
                sg = 2 * tau + s
                B.ts("dve", HT[:, :, s, 0:HALO], HT[:, :, s, 0:HALO], mcol(sg), ALU.mult)
                B.ts("dve", HT[:, :, s, SEGP - HALO:SEGP], HT[:, :, s, SEGP - HALO:SEGP], mcol(sg + 1), ALU.mult)
        if fix_left:
            if tau >= 1:
                B.ts("dve", HT[:, :, 0, 0:HALO], HSAVE, mcol(2 * tau), ALU.mult)
            B.copy("dve", HSAVE, HT[:, :, 1, SEG:SEG + HALO])
        B.top = m0

    def layer_params(l):
        B.dma("sp", PP, pp[l, :, 0:NPPS])
        for which in range(2):
            sc0 = 8 if which == 0 else 32
            g0 = PP_G1 if which == 0 else PP_G2
            B.stt(GS[:, l, which, :], MODV[:, l, sc0:sc0 + 8], 1.0, PP[:, g0:g0 + 8], ALU.add, ALU.mult)
        B.act(NEGA, PP[:, PP_ALOG:PP_ALOG + 64], AF.Exp)
        B.ts("dve", NEGA, NEGA, -1.0, ALU.mult)
        m0 = B.top
        WSN = B.new(BF16, [4, 128])
        B.dma("pool", WSN, I["w_spatial"][l].rearrange("g i j -> i g j"))
        pb = B.bank(6, BF16)
        for g in range(4):
            B.tr(pb[:, g * 128:(g + 1) * 128], WSN[:, g, :], IDB)
        B.copy("dve", WST.rearrange("p a b -> p (a b)"), pb[:, 0:512])
        B.top = m0

    def load_state(l, src):
        m0 = B.top
        ST = [B.new(F32, [128]) for _ in range(2)]
        for blk in range(16):
            st = ST[blk % 2]
            B.dma("sp", st, src[l, blk * 128:(blk + 1) * 128, :])
            pb = B.bank(6 + blk % 2)
            B.tr(pb[:, 0:128], st, IDF)
            B.copy("act", SF[:, blk * 128:(blk + 1) * 128], pb[:, 0:128])
        B.copy("act", SBF, SF)
        B.top = m0

    v3 = lambda a: a.rearrange("p (c h) -> p c h", h=32)
    h64 = lambda a: a.rearrange("p (h x) -> p h x", x=64)

    def sweep(l, d):
        tiles = range(NTILE) if d == 0 else range(NTILE - 1, -1, -1)
        modq = [0]
        load_state(l, s0f if d == 0 else s0b)
        for tau in tiles:
            if d == 1:
                modnorm(l, 0, tau, True)
            tile_top = B.top
            if d == 0:
                YN_off = B.alloc(16 * TT * 2)
                YN = B.view(YN_off, BF16, [16, TT])
            pos_m = B.top
            XT = B.new(BF16, [16, TT])
            BT = B.new(BF16, [4, TT])
            CT = B.new(BF16, [4, TT])
            ACC = [B.new(F32, [SEG]) for _ in range(8)] if d == 1 else None
            CW0 = PP_CW
            ai = 0
            if d == 0:
                B.dma("sp", HT.rearrange("p a b c -> p (a b c)"), fes[tau, :, 24 * 512:NFE])
                B.dma("sp", XT.rearrange("p a b -> p (a b)"), fes[tau, :, 0:16 * 512])
                B.dma("sp", BT.rearrange("p a b -> p (a b)"), fes[tau, :, 16 * 512:20 * 512])
                B.dma("sp", CT.rearrange("p a b -> p (a b)"), fes[tau, :, 20 * 512:24 * 512])
            pend_silu = []

            def flush_silu():
                while pend_silu:
                    for (cb, s, pb, acc) in pend_silu.pop(0):
                        if cb < 16:
                            dst = XT[:, cb, s * SEG:(s + 1) * SEG]
                        elif cb < 20:
                            dst = BT[:, cb - 16, s * SEG:(s + 1) * SEG]
                        else:
                            dst = CT[:, cb - 20, s * SEG:(s + 1) * SEG]
                        B.act(dst, acc, AF.Silu)

            for cb2 in (range(12) if d == 1 else ()):
                wv = B.wnext(w_in[l, :, C_X + cb2 * 256:C_X + (cb2 + 1) * 256], 8, 256)
                o0 = HALO - 2
                ch = []
                for j in range(2):
                    cb = cb2 * 2 + j
                    for s in range(2):
                        pb = B.bank((cb2 % 2) * 4 + j * 2 + s)
                        for k in range(8):
                            B.mm(pb[:, 0:SEGP], wv[:, k, j * 128:(j + 1) * 128], HT[:, k, s, :],
                                 start=(k == 0), stop=(k == 7))
                        ch.append((cb, s, pb, ACC[(cb2 % 2) * 4 + j * 2 + s]))
                for (cb, s, pb, acc) in ch:
                    B.act(acc, pb[:, o0:o0 + SEG], AF.Identity, bias=PP[:, PP_CB + cb:PP_CB + cb + 1],
                          scale=PP[:, CW0 + cb:CW0 + cb + 1])
                flush_silu()
                for tap in range(1, 5):
                    for (cb, s, pb, acc) in ch:
                        B.stt(acc, pb[:, o0 + tap:o0 + tap + SEG],
                              PP[:, CW0 + tap * 24 + cb:CW0 + tap * 24 + cb + 1], acc, ALU.mult, ALU.add)
                pend_silu.append(ch)
                if cb2 == 11:
                    flush_silu()
            if d == 1:
                B.dma("sp", fes[tau, :, 24 * 512:NFE], HT.rearrange("p a b c -> p (a b c)"))
                B.dma("sp", fes[tau, :, 0:16 * 512], XT.rearrange("p a b -> p (a b)"))
                B.dma("sp", fes[tau, :, 16 * 512:20 * 512], BT.rearrange("p a b -> p (a b)"))
                B.dma("sp", fes[tau, :, 20 * 512:24 * 512], CT.rearrange("p a b -> p (a b)"))
            DTP = B.new(F32, [128])
            DT_ = B.new(F32, [128])
            DLA = B.new(F32, [128])
            ACU = B.new(F32, [128])
            TOT = B.new(F32, [128])
            WV = B.new(F32, [128])
            DEC = B.new(F32, [128])
            HI = B.new(BF16, [128])
            LO = B.new(BF16, [128])
            wdt = B.wnext(w_in[l, :, C_DT + 32 * d:C_DT + 32 * d + 32], 8, 32)
            pb = B.bank(6)
            for c in range(4):
                s, c0 = c // 2, HALO + 128 * (c % 2)
                for k in range(8):
                    B.mm(pb[:, c * 32:(c + 1) * 32], HT[:, k, s, c0:c0 + 128], wdt[:, k, :],
                         start=(k == 0), stop=(k == 7))
            bb = PP[:, PP_DTB + 32 * d:PP_DTB + 32 * d + 32].unsqueeze(1).broadcast_to([128, 4, 32])
            na = NEGA[:, 32 * d:32 * d + 32].unsqueeze(1).broadcast_to([128, 4, 32])
            B.tt("dve", v3(DTP), v3(pb[:, 0:128]), bb, ALU.add)
            B.act(DTP, DTP, AF.Exp)
            B.act(DT_, DTP, AF.Ln, bias=ONE_AP, scale=1.0)
            B.tt("dve", v3(DLA), v3(DT_), na, ALU.mult)
            pb = B.bank(7)
            B.mm(pb[:, 0:128], TRI_F, DLA)
            B.mm(pb[:, 128:256], ONEF, DLA)
            B.copy("act", ACU, pb[:, 0:128])
            B.copy("act", TOT, pb[:, 128:256])
            B.act(DEC, TOT, AF.Exp)
            if d == 0:
                tri_b, msk = TRIB_I, MSK_F
            else:
                B.tt("dve", ACU, DLA, ACU, ALU.subtract)
                B.tt("dve", ACU, ACU, TOT, ALU.add)
                tri_b, msk = CONB[:, 3, :], MSK_B
            B.tt("dve", WV, TOT, ACU, ALU.subtract)
            B.act(WV, WV, AF.Exp)
            B.tt("dve", WV, WV, DT_, ALU.mult)
            B.copy("dve", HI, DLA)
            B.tt("dve", LO, DLA, HI, ALU.subtract)
            CLB = ACU
            BK = B.new(BF16, [4, 512])
            for cp in range(2):
                pbb = B.bank(4 + cp, BF16)
                for c in (2 * cp, 2 * cp + 1):
                    for g in range(4):
                        o_ = ((c % 2) * 4 + g) * 128
                        B.tr(pbb[:, o_:o_ + 128], BT[:, g, c * 128:(c + 1) * 128], IDB)
                B.copy("act", BK[:, 2 * cp:2 * cp + 2, :].rearrange("p a b -> p (a b)"), pbb[:, 0:1024])
            xkc_off = B.alloc(2 * 4096)
            XKC = [B.view(xkc_off + i * 4096, BF16, [2048]) for i in range(2)]
            YG = B.view(xkc_off, F32, [4, TT]) if d == 0 else B.new(F32, [4, TT])
            CBM = [B.new(BF16, [128]) for _ in range(2)]
            SD4 = [B.new(F32, [512]) for _ in range(2)]
            SDB = [B.new(BF16, [512]) for _ in range(2)]
            ER4 = [B.new(BF16, [512]) for _ in range(2)]
            MR = [B.new(BF16, [4, 128]) for _ in range(4)]
            XDT = [B.new(BF16, [512]) for _ in range(2)]
            r3_off = B.alloc(8192)
            MR += [B.view(r3_off + i * 1024, BF16, [4, 128]) for i in range(2)]
            XDT.append(B.view(r3_off + 2048, BF16, [512]))
            CSR = [B.new(BF16, [4, 128]) for _ in range(4)]
            XW = [B.new(BF16, [512]) for _ in range(2)]
            CSR += [B.view(r3_off + 3072 + i * 1024, BF16, [4, 128]) for i in range(2)]
            XW.append(B.view(r3_off + 5120, BF16, [512]))
            STG4 = B.new(F32, [512])
            if d == 0:
                YBT = [B.new(F32, [TT]) for _ in range(2)]
                ZS2 = [B.view(r3_off + i * 2048, F32, [TT]) for i in range(2)]
                SQG2 = [B.view(r3_off + 4096 + i * 1024, BF16, [TT]) for i in range(4)]
                LNG = B.new(F32, [TT])
            corder = list(range(4)) if d == 0 else [3, 2, 1, 0]
            qd = [0]

            def stageA(g, c, it):
                cols = slice(c * 128, (c + 1) * 128)
                pcb = B.bank(6)
                cbo = (it % 2) * 128
                B.mm(pcb[:, cbo:cbo + 128], BT[:, g, cols], CT[:, g, cols])
                cbm = CBM[it % 2]
                B.tt("dve", cbm, pcb[:, cbo:cbo + 128], msk, ALU.mult)
                B.tt("pool", h64(XW[it % 3]), h64(XKC[c % 2][:, g * 512:(g + 1) * 512]),
                     WV[:, c * 32 + 8 * g:c * 32 + 8 * g + 8].unsqueeze(2).broadcast_to([128, 8, 64]), ALU.mult)
                B.tt("pool", h64(XDT[it % 3]), h64(XKC[c % 2][:, g * 512:(g + 1) * 512]),
                     DT_[:, c * 32 + 8 * g:c * 32 + 8 * g + 8].unsqueeze(2).broadcast_to([128, 8, 64]), ALU.mult)
                for quad in range(2):
                    pab = B.bank((it % 2) * 2 + quad)
                    sd = SD4[qd[0] % 2]
                    er = ER4[qd[0] % 2]
                    qd[0] += 1
                    cs = CSR[(it % 3) * 2 + quad]
                    for e4 in range(4):
                        q = c * 32 + 8 * g + quad * 4 + e4
                        ab = pab[:, e4 * 128:(e4 + 1) * 128]
                        B.mm(ab, HI[:, q:q + 1].broadcast_to([128, 128]), tri_b, start=True, stop=False)
                        B.mm(ab, LO[:, q:q + 1].broadcast_to([128, 128]), tri_b, start=False, stop=True)
                    for e4 in range(4):
                        q = c * 32 + 8 * g + quad * 4 + e4
                        ab = pab[:, e4 * 128:(e4 + 1) * 128]
                        B.act(sd[:, e4 * 128:(e4 + 1) * 128], ab, AF.Relu, bias=CLB[:, q:q + 1], scale=-1.0)
                    sdb = SDB[(qd[0] - 1) % 2]
                    B.act(sdb, sd, AF.Exp, scale=-1.0)
                    B.act(er, pab, AF.Exp)
                    B.tt("dve", MR[(it % 3) * 2 + quad], sdb.rearrange("p (a b) -> p a b", b=128),
                         cbm.unsqueeze(1).broadcast_to([128, 4, 128]), ALU.mult)
                    B.tt("dve", cs, CT[:, g, cols].unsqueeze(1).broadcast_to([128, 4, 128]),
                         er.rearrange("p (a b) -> p a b", b=128), ALU.mult)

            def stageB(g, c, it):
                cols = slice(c * 128, (c + 1) * 128)
                py4 = B.bank(4 + it % 2)
                for hp in range(4):
                    py = py4[:, hp * 128:(hp + 1) * 128]
                    for e in range(2):
                        e8 = 2 * hp + e
                        h = 8 * g + e8
                        B.mm(py[e * 64:(e + 1) * 64, :], XDT[it % 3][:, e8 * 64:(e8 + 1) * 64],
                             MR[(it % 3) * 2 + e8 // 4][:, e8 % 4, :], start=True, stop=False)
                        B.mm(py[e * 64:(e + 1) * 64, :], SBF[:, h * 64:(h + 1) * 64],
                             CSR[(it % 3) * 2 + e8 // 4][:, e8 % 4, :], start=False, stop=True)
                if d == 0:
                    for hp in range(4):
                        blk = 4 * g + hp
                        B.stt(YN[:, blk, cols], XT[:, blk, cols], PP[:, PP_DSK + blk:PP_DSK + blk + 1],
                              py4[:, hp * 128:(hp + 1) * 128], ALU.mult, ALU.add)
                if d == 1:
                    ygr = YG[:, :, (it % 4) * 128:(it % 4 + 1) * 128]
                    B.copy("dve", ygr, py4.rearrange("p (a b) -> p a b", b=128))
                    t_a = tau * TT + c * 128
                    B.dma("sp", ybs[4 * g:4 * g + 4, :, t_a:t_a + 128].rearrange("b p t -> p b t"), ygr)
                pl = B.bank(7)
                B.mm(pl, BK[:, c, g * 128:(g + 1) * 128], XW[it % 3])
                sg_ = SF[:, g * 512:(g + 1) * 512]
                B.tt("pool", h64(sg_), h64(sg_),
                     DEC[:, c * 32 + 8 * g:c * 32 + 8 * g + 8].unsqueeze(2).broadcast_to([128, 8, 64]), ALU.mult)
                B.tt("dve", sg_, sg_, pl, ALU.add)
                seg_end = (c % 2 == 1) if d == 0 else (c % 2 == 0)
                if seg_end:
                    sgi = 2 * tau + c // 2
                    if sgi < 4:
                        dst_t = sfo if d == 0 else sbo
                        ptb = B.bank(7)
                        for b4 in range(4):
                            B.tr(ptb[:, b4 * 128:(b4 + 1) * 128], sg_[:, b4 * 128:(b4 + 1) * 128], IDF)
                        B.copy("act", STG4, ptb)
                        r0 = g * 512
                        o = B.dma("sp", dst_t[sgi, l, r0:r0 + 512, :].rearrange("(b p) n -> p b n", p=128),
                                  STG4.rearrange("p (b n) -> p b n", n=128))
                        if o is not None:
                            B.out_dmas.append(o)
                    bnd = (sgi + 1) if d == 0 else sgi
                    B.ts("dve", sg_, sg_, mcol(bnd), ALU.mult)
                B.copy("act", SBF[:, g * 512:(g + 1) * 512], sg_)

            def xk_transposes(c):
                for half in range(2):
                    pbb = B.bank(4 + half, BF16)
                    for j in range(8):
                        B.tr(pbb[:, j * 128:(j + 1) * 128], XT[:, half * 8 + j, c * 128:(c + 1) * 128], IDB)
                    B.copy("dve" if d == 0 else "act", XKC[c % 2][:, half * 1024:(half + 1) * 1024], pbb[:, 0:1024])

            its = [(g, c) for c in corder for g in range(4)]
            pend = []
            for n, (g, c) in enumerate(its):
                if g == 0:
                    xk_transposes(c)
                stageA(g, c, n)
                pend.append((g, c, n))
                if len(pend) > 2:
                    stageB(*pend.pop(0))
                if d == 1 and l + 1 < nl and modq[0] < 24:
                    mod_piece(l + 1, modq[0])
                    modq[0] += 1
            while pend:
                stageB(*pend.pop(0))
            if d == 0:
                gate_tail = []
                for g in range(4):
                    pss = B.bank(6)
                    pzs = []
                    for hp2 in range(2):
                        wv = B.wnext(w_in[l, :, C_Z + g * 512 + hp2 * 256:C_Z + g * 512 + (hp2 + 1) * 256], 8, 256)
                        for j in range(2):
                            hp = hp2 * 2 + j
                            pz = B.bank(mmbank())
                            for s in range(2):
                                for k in range(8):
                                    B.mm(pz[:, s * SEG:(s + 1) * SEG], wv[:, k, j * 128:(j + 1) * 128],
                                         HT[:, k, s, HALO:HALO + SEG], start=(k == 0), stop=(k == 7))
                            pzs.append(pz)
                    if gate_tail:
                        gate_tail.pop(0)()
                    for hp in range(4):
                        blk = 4 * g + hp
                        ybt = YBT[hp % 2]
                        B.dma("sp", ybt, ybs[blk, :, tau * TT:(tau + 1) * TT])
                        B.tt("dve", YG[:, hp, :], YN[:, blk, :], ybt, ALU.add)
                    B.act(ZS2[0], pzs[0], AF.Silu)
                    B.act(ZS2[1], pzs[1], AF.Silu)
                    for hp in range(4):
                        B.tt("dve", YG[:, hp, :], YG[:, hp, :], ZS2[hp % 2], ALU.mult)
                        if hp + 2 < 4:
                            B.act(ZS2[hp % 2], pzs[hp + 2], AF.Silu)
                        B.act(SQG2[hp], YG[:, hp, :], AF.Square)

                    def tail(g=g, pss=pss):
                        for hp in range(4):
                            B.mm(pss, ONEB, SQG2[hp], start=(hp == 0), stop=(hp == 3))
                        B.act(LNG, pss, AF.Ln, bias=EPS_AP, scale=1.0 / 512)
                        B.act(LNG, LNG, AF.Exp, scale=-0.5)
                        for hp in range(4):
                            blk = 4 * g + hp
                            B.stt(YN[:, blk, :], YG[:, hp, :], PP[:, PP_GSSD + blk:PP_GSSD + blk + 1], LNG,
                                  ALU.mult, ALU.mult)
                    gate_tail.append(tail)
                while gate_tail:
                    gate_tail.pop(0)()
            if d == 1:
                B.top = tile_top
                continue
            B.top = pos_m
            MRG = B.new(F32, [8, TT])
            MB = B.new(BF16, [8, TT])
            GA = [B.new(BF16, [TT]) for _ in range(2)]
            TMPF = [B.new(F32, [TT]) for _ in range(2)]
            br_top = B.top
            gate_w = {}

            def branch_out(wsrc, K, rhs_of, goff, mode, hook=None):
                for jo in range(8):
                    if hook is not None:
                        hook(jo)
                    if jo % 2 == 0:
                        gate_w[0] = B.wnext(w_in[l, :, goff + jo * 128:goff + (jo + 2) * 128], 8, 256)
                    wvg = gate_w[0]
                    pg = B.bank(mmbank())
                    for s in range(2):
                        for k in range(8):
                            B.mm(pg[:, s * SEG:(s + 1) * SEG], wvg[:, k, (jo % 2) * 128:(jo % 2) * 128 + 128],
                                 HT[:, k, s, HALO:HALO + SEG], start=(k == 0), stop=(k == 7))
                    ga = GA[jo % 2]
                    B.act(ga, pg, AF.Sigmoid)
                    pbr = B.bank(mmbank())
                    for kh in range((K + 15) // 16):
                        k0 = kh * 16
                        kk = min(16, K - k0)
                        wv = B.wnext(wsrc[k0 * 128:(k0 + kk) * 128, jo * 128:(jo + 1) * 128], kk, 128)
                        for k in range(kk):
                            B.mm(pbr, wv[:, k, :], rhs_of(k0 + k), start=(k0 + k == 0), stop=(k0 + k == K - 1))
                    if mode == 0:
                        B.tt("dve", MRG[:, jo, :], pbr, ga, ALU.mult)
                    else:
                        tf = TMPF[jo % 2]
                        B.tt("dve", tf, pbr, ga, ALU.mult)
                        B.tt("dve", MRG[:, jo, :] if mode == 1 else MB[:, jo, :], MRG[:, jo, :], tf, ALU.add)

            VI = B.new(F32, [2, SEGP])
            CN = [B.new(F32, [SEGP]) for _ in range(4)]
            RC = B.new(F32, [4, 2, SEG])
            for s in range(2):
                sg = 2 * tau + s
                B.memset("dve", VI[:, s, :], 1.0)
                B.ts("dve", VI[:, s, 0:HALO], VI[:, s, 0:HALO], mcol(sg), ALU.mult)
                B.ts("dve", VI[:, s, SEGP - HALO:SEGP], VI[:, s, SEGP - HALO:SEGP], mcol(sg + 1), ALU.mult)
                B.tt("dve", CN[0][:, 1:SEGP], VI[:, s, 0:SEGP - 1], VI[:, s, 1:SEGP], ALU.add)
                B.tt("dve", CN[1][:, 2:SEGP - 1], CN[0][:, 1:SEGP - 2], CN[0][:, 3:SEGP], ALU.add)
                B.tt("dve", CN[2][:, 4:SEGP - 3], CN[1][:, 2:SEGP - 5], CN[1][:, 6:SEGP - 1], ALU.add)
                B.tt("dve", CN[3][:, 8:SEGP - 7], CN[2][:, 4:SEGP - 11], CN[2][:, 12:SEGP - 3], ALU.add)
                for wi in range(4):
                    S.op("dve", (lambda o_, i_: (lambda e: e.reciprocal(out=o_, in_=i_)))(
                        RC[:, wi, s, :], CN[wi][:, HALO:HALO + SEG]),
                        reads=[CN[wi][:, HALO:HALO + SEG]], writes=[RC[:, wi, s, :]])
            PL = B.new(BF16, [8, TT])
            P0 = [B.new(F32, [SEGP]) for _ in range(2)]
            SM = [B.new(F32, [SEGP]) for _ in range(4)]
            pst = {"pi": 0, "wv": None}

            def pool_chain(cb):
                if cb % 2 == 0:
                    pst["wv"] = B.wnext(w_in[l, :, C_P + (cb // 2) * 256:C_P + (cb // 2 + 1) * 256], 8, 256)
                wv = pst["wv"]
                j = cb % 2
                wi = cb // 2
                for s in range(2):
                    pb = B.bank(mmbank())
                    for k in range(8):
                        B.mm(pb[:, 0:SEGP], wv[:, k, j * 128:(j + 1) * 128], HT[:, k, s, :],
                             start=(k == 0), stop=(k == 7))
                    p0 = P0[pst["pi"] % 2]
                    pst["pi"] += 1
                    B.copy("act", p0, pb[:, 0:SEGP])
                    B.tt("dve", SM[0][:, 1:SEGP], p0[:, 0:SEGP - 1], p0[:, 1:SEGP], ALU.add)
                    if wi >= 1:
                        B.tt("dve", SM[1][:, 2:SEGP - 1], SM[0][:, 1:SEGP - 2], SM[0][:, 3:SEGP], ALU.add)
                    if wi >= 2:
                        B.tt("dve", SM[2][:, 4:SEGP - 3], SM[1][:, 2:SEGP - 5], SM[1][:, 6:SEGP - 1], ALU.add)
                    if wi >= 3:
                        B.tt("dve", SM[3][:, 8:SEGP - 7], SM[2][:, 4:SEGP - 11], SM[2][:, 12:SEGP - 3], ALU.add)
                    sm = SM[wi]
                    B.tt("dve", sm[:, HALO:HALO + SEG], sm[:, HALO:HALO + SEG], RC[:, wi, s, :], ALU.mult)
                    B.tt("dve", PL[:, cb, s * SEG:(s + 1) * SEG], sm[:, HALO:HALO + SEG],
                         p0[:, HALO:HALO + SEG], ALU.subtract)

            branch_out(I["w_br_ssd"][l], 16, lambda k: YN[:, k, :], C_G, 0, hook=pool_chain)
            VW = B.view(YN_off, BF16, [8, 1024])
            for q4 in range(4):
                B.dma("pool", VW[:, :, q4 * 256:(q4 + 1) * 256],
                      w_in[l, :, C_V + q4 * 256:C_V + (q4 + 1) * 256].rearrange("(k p) c -> p k c", p=128))
            YP = B.new(BF16, [8, TT])
            wpv = B.wnext(I["w_pool"][l].rearrange("g r c -> (g r) c"), 8, 256)
            for g4 in range(4):
                for ob in range(2):
                    pb = B.bank(mmbank())
                    for kb in range(2):
                        B.mm(pb, wpv[:, g4 * 2 + kb, ob * 128:(ob + 1) * 128], PL[:, 2 * g4 + kb, :],
                             start=(kb == 0), stop=(kb == 1))
                    blk = 2 * g4 + ob
                    B.act(YP[:, blk, :], pb, AF.Identity, scale=PP[:, PP_PSC + blk:PP_PSC + blk + 1])
            branch_out(I["w_br_pool"][l], 8, lambda k: YP[:, k, :], C_G + 1024, 1)
            B.top = br_top
            GSGU = B.new(F32, [1024])
            BSP = B.new(F32, [512])
            B.dma("sp", GSGU, pp[l, :, PP_GSGU:PP_GSGU + 1024])
            B.dma("sp", BSP, pp[l, :, PP_BSP:PP_BSP + 512])
            VN = B.new(BF16, [4, 1024])
            UT = B.new(BF16, [8, TT])
            YM = B.new(BF16, [8, TT])
            SSV = B.new(F32, [8])
            JNK = B.new(BF16, [1024])
            for c in range(4):
                s, c0 = c // 2, HALO + 128 * (c % 2)
                b0 = (c % 2) * 2
                for q4 in range(4):
                    pb = B.bank(b0 + q4 // 2)
                    for k in range(8):
                        B.mm(pb[:, (q4 % 2) * 256:(q4 % 2) * 256 + 256], HT[:, k, s, c0:c0 + 128],
                             VW[:, k, q4 * 256:(q4 + 1) * 256], start=(k == 0), stop=(k == 7))
                pv = B.psum[:, b0 * 512:(b0 + 2) * 512]
                B.act(JNK, pv, AF.Square, accum_out=SSV[:, c:c + 1])
                B.act(SSV[:, 4 + c:5 + c], SSV[:, c:c + 1], AF.Ln, bias=EPS_AP, scale=1.0 / D)
                B.act(SSV[:, 4 + c:5 + c], SSV[:, 4 + c:5 + c], AF.Exp, scale=-0.5)
                B.stt(VN[:, c, :], pv, SSV[:, 4 + c:5 + c], GSGU, ALU.mult, ALU.mult)
            for cb2 in range(4):
                wv = B.wnext(w_in[l, :, C_U + cb2 * 256:C_U + (cb2 + 1) * 256], 8, 256)
                for j in range(2):
                    cb = cb2 * 2 + j
                    pu = B.bank(mmbank())
                    for s in range(2):
                        for k in range(8):
                            B.mm(pu[:, s * SEG:(s + 1) * SEG], wv[:, k, j * 128:(j + 1) * 128],
                                 HT[:, k, s, HALO:HALO + SEG], start=(k == 0), stop=(k == 7))
                    B.copy("act", UT[:, cb, :], pu)
                    g4 = cb // 2
                    psv = B.bank(mmbank())
                    for c in range(4):
                        B.mm(psv[:, c * 128:(c + 1) * 128], VN[:, c, cb * 128:(cb + 1) * 128], WST[:, g4, :])
                    tf = TMPF[cb % 2]
                    B.tt("dve", tf.rearrange("p (c i) -> p c i", i=128), psv.rearrange("p (c i) -> p c i", i=128),
                         BSP[:, g4 * 128:(g4 + 1) * 128].unsqueeze(1).broadcast_to([128, 4, 128]), ALU.add)
                    B.tt("dve", YM[:, cb, :], tf, UT[:, cb, :], ALU.mult)
            branch_out(I["w_br_gmlp"][l], 8, lambda k: YM[:, k, :], C_G + 2048, 2)
            tcol = slice(HALO + tau * TT, HALO + (tau + 1) * TT)
            for jo in range(8):
                wv = B.wnext(I["w_out"][l][:, jo * 128:(jo + 1) * 128], 8, 128)
                po = B.bank(mmbank())
                for k in range(8):
                    B.mm(po, wv[:, k, :], MB[:, k, :], start=(k == 0), stop=(k == 7))
                B.stt(XR[:, jo, tcol], po, MODV[:, l, 16 + jo:17 + jo], XR[:, jo, tcol], ALU.mult, ALU.add)
            B.top = tile_top
            modnorm(l, 1, tau, False)
            HID = B.new(BF16, [22, TT])
            SGT = [B.new(F32, [TT]) for _ in range(2)]
            for j2 in range(11):
                wg = B.wnext(I["w_ffn_in"][l][:, j2 * 256:(j2 + 1) * 256], 8, 256)
                wu = B.wnext(I["w_ffn_in"][l][:, DFF + j2 * 256:DFF + (j2 + 1) * 256], 8, 256)
                for j in range(2):
                    jj = j2 * 2 + j
                    pg = B.bank(mmbank())
                    pu = B.bank(mmbank())
                    for (pp_, wv) in ((pg, wg), (pu, wu)):
                        for s in range(2):
                            for k in range(8):
                                B.mm(pp_[:, s * SEG:(s + 1) * SEG], wv[:, k, j * 128:(j + 1) * 128],
                                     HT[:, k, s, HALO:HALO + SEG], start=(k == 0), stop=(k == 7))
                    sg_ = SGT[jj % 2]
                    B.act(sg_, pg, AF.Silu)
                    B.tt("dve", HID[:, jj, :], pu, sg_, ALU.mult)
            for jo in range(8):
                po = B.bank(mmbank())
                for kh in range(2):
                    wv = B.wnext(I["w_ffn_out"][l][kh * 11 * 128:(kh + 1) * 11 * 128, jo * 128:(jo + 1) * 128], 11, 128)
                    for k in range(11):
                        B.mm(po, wv[:, k, :], HID[:, kh * 11 + k, :], start=(kh == 0 and k == 0),
                             stop=(kh == 1 and k == 10))
                B.stt(XR[:, jo, tcol], po, MODV[:, l, 40 + jo:41 + jo], XR[:, jo, tcol], ALU.mult, ALU.add)
            B.top = tile_top

    for l in range(nl):
        layer_params(l)
        sweep(l, 1)
        sweep(l, 0)

    GF = B.new(F32, [D])
    B.dma("sp", GF, gfin)
    OS = [B.new(F32, [D]) for _ in range(2)]
    SSO = B.new(F32, [32])
    JN2 = B.new(BF16, [D])
    for tc in range(16):
        b0 = (tc % 2) * 2
        for blk in range(8):
            pb = B.bank(b0 + blk // 4)
            B.tr(pb[:, (blk % 4) * 128:(blk % 4) * 128 + 128], XR[:, blk, HALO + tc * 128:HALO + (tc + 1) * 128], IDF)
        pv = B.psum[:, b0 * 512:(b0 + 2) * 512]
        B.act(JN2, pv, AF.Square, accum_out=SSO[:, tc:tc + 1])
        B.act(SSO[:, 16 + tc:17 + tc], SSO[:, tc:tc + 1], AF.Ln, bias=EPS_AP, scale=1.0 / D)
        B.act(SSO[:, 16 + tc:17 + tc], SSO[:, 16 + tc:17 + tc], AF.Exp, scale=-0.5)
        osb = OS[tc % 2]
        B.stt(osb, pv, SSO[:, 16 + tc:17 + tc], GF, ALU.mult, ALU.mult)
        o = B.dma("sp", yout[tc * 128:(tc + 1) * 128, :], osb)
        if o is not None:
            B.out_dmas.append(o)


_CACHE = {}


def _consts():
    j = np.arange(128)[:, None]
    i = np.arange(128)[None, :]
    return np.stack([np.eye(128), (j <= i), (j < i), (j >= i)]).astype(np.float32)


def _pack_pp(inp, l):
    f = np.float32
    pm = lambda v: np.ascontiguousarray(np.asarray(v, f).reshape(-1, 128).T)
    cols = [pm(inp["g_norm1"][l]), pm(inp["g_norm2"][l]), pm(inp["b_ada"][l])]
    cw = np.asarray(inp["conv_w"][l], f)[:, 2048 - 2048:]
    cols.append(np.concatenate([pm(cw[t]) for t in range(5)], axis=1))
    cols.append(pm(inp["conv_b"][l]))
    cols.append(pm(np.repeat(np.asarray(inp["d_skip"][l], f), 64)))
    cols.append(pm(inp["g_ssd"][l]))
    cols.append(pm(inp["pool_scale"][l]))
    cols.append(np.broadcast_to(np.asarray(inp["dt_bias"][l], f).reshape(1, 64), (128, 64)))
    cols.append(np.broadcast_to(np.asarray(inp["a_log"][l], f).reshape(1, 64), (128, 64)))
    cols.append(np.broadcast_to(np.asarray(inp["g_sgu"][l], f).reshape(1, 1024), (128, 1024)))
    cols.append(np.broadcast_to(np.asarray(inp["b_spatial"][l], f).reshape(1, 512), (128, 512)))
    out = np.concatenate(cols, axis=1).astype(f)
    assert out.shape == (128, NPP), out.shape
    return out


def kernel(nl=DEPTH, **inp):
    f = np.float32
    if nl not in _CACHE:
        _CACHE[nl] = build_program(nl)
    nc = _CACHE[nl]
    xp = np.asarray(inp["x_prompt"], f)
    xs = np.asarray(inp["x_sample"], f)
    NLW = max(nl, 1)
    pp = np.stack([_pack_pp(inp, l) for l in range(NLW)])
    shared = {
        "in_consts": _consts(),
        "in_pp": pp,
        "in_gfin": np.ascontiguousarray(np.broadcast_to(np.asarray(inp["g_final"], f)[None, :], (128, D))),
    }
    for k in ("w_ada", "w_in", "w_br_ssd", "w_pool", "w_br_pool", "w_spatial", "w_br_gmlp", "w_out",
              "w_ffn_in", "w_ffn_out"):
        shared["in_" + k] = np.ascontiguousarray(np.asarray(inp[k], f)[:NLW])
    in_maps = []
    zst = np.zeros((NLW, NH * 64, 128), f)
    for core in range(8):
        m = dict(shared)
        fl = np.zeros((128, 16), f)
        if core < 4:
            m["in_xin"] = np.ascontiguousarray(xs[core])
            m["in_s0f"] = np.ascontiguousarray(np.asarray(inp["state_ssd_fwd"], f)[core, :NLW].reshape(NLW, NH * 64, 128))
            m["in_s0b"] = np.ascontiguousarray(np.asarray(inp["state_ssd_bwd"], f)[core, :NLW].reshape(NLW, NH * 64, 128))
            cv = np.asarray(inp["c"], f)[core]
            fl[:, 0] = 1.0
            fl[:, 2:9] = 1.0
        else:
            q = core - 4
            xx = np.zeros((T, D), f)
            xx[:1024] = xp[4 * q:4 * q + 4].reshape(1024, D)
            m["in_xin"] = xx
            m["in_s0f"] = zst
            m["in_s0b"] = zst
            cv = np.asarray(inp["c_ctx"], f)
        m["in_cvec"] = np.ascontiguousarray(cv.reshape(8, 128).T)
        m["in_flags"] = fl
        in_maps.append(m)
    res = run_bass_kernel_spmd(nc, in_maps, core_ids=list(range(8)))
    R = res.results
    y_sample = np.stack([R[c]["yout"] for c in range(4)]).astype(f)
    y_prompt = np.concatenate([R[4 + q]["yout"][:1024].reshape(4, 256, D) for q in range(4)]).astype(f)
    nsf = np.concatenate([R[4 + q]["sfo"].reshape(4, NLW, NH, 64, 128) for q in range(4)]).astype(f)
    nsb = np.concatenate([R[4 + q]["sbo"].reshape(4, NLW, NH, 64, 128) for q in range(4)]).astype(f)
    return (y_prompt, y_sample, nsf, nsb)
```
